# Optimizing a Trainium2 kernel written in Bass

```python
import jax, jax.numpy as jnp
from jax import lax
import numpy as np

D_MODEL = 1024
BATCH = 2
SEQ = 8192
DEPTH = 4

N_A = DEPTH // 2
N_B = DEPTH - N_A
N_VRES = max(N_A - 1, 0)
HEAD_DIM = 64
N_HEADS = D_MODEL // HEAD_DIM
D_FF = 4 * D_MODEL
DECAY_LORA = 64
AAA_LORA = 64
MV_LORA = 32
GATE_LORA = 160
Q_BLOCK = 128
N_MOD = 6
NORM_EPS = 1e-6
GN_EPS = 64e-5
L2_EPS = 1e-12

kernel_name = 'yoco_rwkv7_fox_hybrid'


def rms_norm(x, g):
    xf = x.astype(jnp.float32)
    y = xf * lax.rsqrt(jnp.mean(xf * xf, axis=-1, keepdims=True) + NORM_EPS)
    return (y * g.astype(jnp.float32)).astype(x.dtype)


def modulate(h, shift, scale):
    return h * (1.0 + scale) + shift


def token_shift(x):
    return jnp.pad(x, ((0, 0), (1, 0), (0, 0)))[:, :-1, :]


def split_heads(t):
    return t.reshape(t.shape[0], t.shape[1], N_HEADS, HEAD_DIM)


def wkv7_scan(r, decay, k, v, a_vec, b_vec):
    bsz = r.shape[0]
    xs = tuple(jnp.moveaxis(t.astype(jnp.float32), 1, 0) for t in (r, decay, k, v, a_vec, b_vec))

    def step(S, inp):
        r_t, w_t, k_t, v_t, a_t, b_t = inp
        sa = jnp.einsum('bhvk,bhk->bhv', S, a_t)
        S = S * w_t[:, :, None, :] + sa[..., :, None] * b_t[..., None, :] + v_t[..., :, None] * k_t[..., None, :]
        y = jnp.einsum('bhvk,bhk->bhv', S, r_t)
        return S, y

    S0 = jnp.zeros((bsz, N_HEADS, HEAD_DIM, HEAD_DIM), jnp.float32)
    _, ys = lax.scan(step, S0, xs)
    return jnp.moveaxis(ys, 0, 1)


def rwkv7_time_mix(h, v_first, vres, mu, wr, wk, wv, wo, w0, w1, w2, a0, a1, a2,
                   g1, g2, k_k, k_a, r_k, ln_w, ln_b):
    bsz, T, _ = h.shape
    xx = token_shift(h) - h
    xr, xw, xk, xv, xa, xg = (h + xx * mu[j] for j in range(6))
    r = xr @ wr
    w_log = -jax.nn.softplus(-(w0 + jnp.tanh(xw @ w1) @ w2)) - 0.5
    k = xk @ wk
    v = xv @ wv
    if vres is None:
        v_first = v
    else:
        v0, v1, v2 = vres
        v = v + (v_first - v) * jax.nn.sigmoid(v0 + (xv @ v1) @ v2)
    a = jax.nn.sigmoid(a0 + (xa @ a1) @ a2)
    g = jax.nn.sigmoid(xg @ g1) @ g2
    kk = split_heads(k * k_k).astype(jnp.float32)
    kk = kk / jnp.maximum(jnp.sqrt(jnp.sum(kk * kk, axis=-1, keepdims=True)), L2_EPS)
    k = k * (1.0 + (a - 1.0) * k_a)
    decay = jnp.exp(-jnp.exp(w_log.astype(jnp.float32)))
    rh, kh, vh = split_heads(r), split_heads(k), split_heads(v)
    ah = split_heads(a).astype(jnp.float32)
    y = wkv7_scan(rh, split_heads(decay), kh, vh, -kk, kk * ah)
    mean = jnp.mean(y, axis=-1, keepdims=True)
    var = jnp.mean(jnp.square(y - mean), axis=-1, keepdims=True)
    y = ((y - mean) * lax.rsqrt(var + GN_EPS)).reshape(bsz, T, D_MODEL)
    y = y * ln_w.astype(jnp.float32) + ln_b.astype(jnp.float32)
    bonus = jnp.sum((rh * kh * r_k).astype(jnp.float32), axis=-1, keepdims=True) * vh.astype(jnp.float32)
    y = (y + bonus.reshape(bsz, T, D_MODEL)).astype(h.dtype)
    return (y * g) @ wo, v_first


def shared_kv(x, shift, scale, norm_g, w_kv, f_bias, k_gain):
    h = modulate(rms_norm(x, norm_g), shift, scale)
    kvf = h @ w_kv
    k = kvf[..., :D_MODEL]
    v = kvf[..., D_MODEL:2 * D_MODEL]
    f_logit = kvf[..., 2 * D_MODEL:]
    k = rms_norm(split_heads(k), k_gain)
    log_f = jax.nn.log_sigmoid(f_logit.astype(jnp.float32) + f_bias.astype(jnp.float32))
    F = jnp.cumsum(log_f, axis=1)
    return (k.transpose(0, 2, 1, 3), split_heads(v).transpose(0, 2, 1, 3).astype(jnp.float32),
            F.transpose(0, 2, 1))


def forgetting_attention(h, k, v, F, w_qg, q_gain, w_o):
    bsz, T, _ = h.shape
    qg = h @ w_qg
    q, gate = qg[..., :D_MODEL], qg[..., D_MODEL:]
    q = rms_norm(split_heads(q), q_gain).transpose(0, 2, 1, 3)
    sm_scale = HEAD_DIM ** -0.5
    kpos = jnp.arange(T)

    def block(start):
        qb = lax.dynamic_slice_in_dim(q, start, Q_BLOCK, axis=2)
        Fq = lax.dynamic_slice_in_dim(F, start, Q_BLOCK, axis=2)
        s = jnp.einsum('bhqd,bhkd->bhqk', qb, k).astype(jnp.float32) * sm_scale
        s = s + Fq[..., :, None] - F[..., None, :]
        qpos = start + jnp.arange(Q_BLOCK)
        s = jnp.where(kpos[None, :] <= qpos[:, None], s, -jnp.inf)
        p = jax.nn.softmax(s, axis=-1)
        return jnp.einsum('bhqk,bhkd->bhqd', p, v)

    starts = jnp.arange(T // Q_BLOCK) * Q_BLOCK
    o = lax.map(block, starts)
    o = o.transpose(1, 0, 3, 2, 4).reshape(bsz, T, D_MODEL).astype(h.dtype)
    return (o * jax.nn.sigmoid(gate)) @ w_o


def sq_relu_mlp(h, w_up, w_down):
    return jnp.square(jax.nn.relu(h @ w_up)) @ w_down


def setup_inputs(seed: int = 0) -> dict:
    key = jax.random.key(seed)
    keys = iter(jax.random.split(key, 48))
    f32 = jnp.float32

    def nrm(shape, scale):
        return jax.random.normal(next(keys), shape, f32) * scale

    def unif(shape, lo, hi):
        return jax.random.uniform(next(keys), shape, f32, lo, hi)

    D = D_MODEL
    return {
        'x': nrm((BATCH, SEQ, D), 1.0),
        'c': nrm((BATCH, D), 1.0),
        'mod_w': nrm((DEPTH, D, N_MOD * D), 0.2 * D ** -0.5),
        'mod_b': nrm((DEPTH, N_MOD * D), 0.01),
        'norm_mix_g': 1.0 + nrm((DEPTH, D), 0.02),
        'norm_mlp_g': 1.0 + nrm((DEPTH, D), 0.02),
        'mlp_up': nrm((DEPTH, D, D_FF), D ** -0.5),
        'mlp_down': nrm((DEPTH, D_FF, D), D_FF ** -0.5),
        'rw_mu': unif((N_A, 6, D), 0.0, 1.0),
        'rw_wr': nrm((N_A, D, D), D ** -0.5),
        'rw_wk': nrm((N_A, D, D), D ** -0.5),
        'rw_wv': nrm((N_A, D, D), D ** -0.5),
        'rw_wo': nrm((N_A, D, D), D ** -0.5),
        'rw_w0': unif((N_A, D), -6.0, 0.0),
        'rw_w1': nrm((N_A, D, DECAY_LORA), D ** -0.5),
        'rw_w2': nrm((N_A, DECAY_LORA, D), 0.1 * DECAY_LORA ** -0.5),
        'rw_a0': nrm((N_A, D), 0.1),
        'rw_a1': nrm((N_A, D, AAA_LORA), D ** -0.5),
        'rw_a2': nrm((N_A, AAA_LORA, D), 0.1 * AAA_LORA ** -0.5),
        'rw_g1': nrm((N_A, D, GATE_LORA), D ** -0.5),
        'rw_g2': nrm((N_A, GATE_LORA, D), GATE_LORA ** -0.5),
        'rw_kk': 0.85 + nrm((N_A, D), 0.02),
        'rw_ka': 1.0 + nrm((N_A, D), 0.02),
        'rw_rk': nrm((N_A, N_HEADS, HEAD_DIM), 0.1),
        'rw_lnw': 1.0 + nrm((N_A, D), 0.02),
        'rw_lnb': nrm((N_A, D), 0.01),
        'rw_v0': 1.0 + nrm((N_VRES, D), 0.02),
        'rw_v1': nrm((N_VRES, D, MV_LORA), D ** -0.5),
        'rw_v2': nrm((N_VRES, MV_LORA, D), 0.1 * MV_LORA ** -0.5),
        'kv_norm_g': 1.0 + nrm((D,), 0.02),
        'kv_mod_w': nrm((D, 2 * D), 0.2 * D ** -0.5),
        'kv_mod_b': nrm((2 * D,), 0.01),
        'kv_w': nrm((D, 2 * D + N_HEADS), D ** -0.5),
        'kv_fb': unif((N_HEADS,), 1.0, 5.0),
        'kv_kg': 1.0 + nrm((HEAD_DIM,), 0.02),
        'fx_wqg': nrm((N_B, D, 2 * D), D ** -0.5),
        'fx_qg': 1.0 + nrm((N_B, HEAD_DIM), 0.02),
        'fx_wo': nrm((N_B, D, D), D ** -0.5),
        'final_g': 1.0 + nrm((D,), 0.02),
    }


def reference(x, c, mod_w, mod_b, norm_mix_g, norm_mlp_g, mlp_up, mlp_down,
              rw_mu, rw_wr, rw_wk, rw_wv, rw_wo, rw_w0, rw_w1, rw_w2, rw_a0, rw_a1, rw_a2,
              rw_g1, rw_g2, rw_kk, rw_ka, rw_rk, rw_lnw, rw_lnb, rw_v0, rw_v1, rw_v2,
              kv_norm_g, kv_mod_w, kv_mod_b, kv_w, kv_fb, kv_kg,
              fx_wqg, fx_qg, fx_wo, final_g):
    c_act = jax.nn.silu(c)
    v_first = None
    k_sh = v_sh = F_sh = None
    for i in range(DEPTH):
        mod = (c_act @ mod_w[i] + mod_b[i])[:, None, :]
        sh1, sc1, gt1, sh2, sc2, gt2 = jnp.split(mod, N_MOD, axis=-1)
        if i == N_A:
            kvm = (c_act @ kv_mod_w + kv_mod_b)[:, None, :]
            kv_shift, kv_scale = jnp.split(kvm, 2, axis=-1)
            k_sh, v_sh, F_sh = shared_kv(x, kv_shift, kv_scale, kv_norm_g, kv_w, kv_fb, kv_kg)
        h = modulate(rms_norm(x, norm_mix_g[i]), sh1, sc1)
        if i < N_A:
            vres = None if i == 0 else (rw_v0[i - 1], rw_v1[i - 1], rw_v2[i - 1])
            y, v_first = rwkv7_time_mix(h, v_first, vres, rw_mu[i], rw_wr[i], rw_wk[i], rw_wv[i], rw_wo[i],
                                        rw_w0[i], rw_w1[i], rw_w2[i], rw_a0[i], rw_a1[i], rw_a2[i],
                                        rw_g1[i], rw_g2[i], rw_kk[i], rw_ka[i], rw_rk[i],
                                        rw_lnw[i], rw_lnb[i])
        else:
            j = i - N_A
            y = forgetting_attention(h, k_sh, v_sh, F_sh, fx_wqg[j], fx_qg[j], fx_wo[j])
        x = x + (1.0 + gt1) * y
        h = modulate(rms_norm(x, norm_mlp_g[i]), sh2, sc2)
        x = x + (1.0 + gt2) * sq_relu_mlp(h, mlp_up[i], mlp_down[i])
    return rms_norm(x, final_g)
```

```python
import numpy as np
from contextlib import ExitStack
import concourse.bass as bass
import concourse.mybir as mybir
from concourse.bass_utils import run_bass_kernel_spmd

F32 = mybir.dt.float32
BF16 = mybir.dt.bfloat16
ALU = mybir.AluOpType
AF = mybir.ActivationFunctionType
AX = mybir.AxisListType

SAME_ENGINE_SYNC = True


import types as _types


def _snapshot(fn):
    cl = fn.__closure__
    if not cl:
        return fn
    cells = []
    for c in cl:
        try:
            cells.append(_types.CellType(c.cell_contents))
        except ValueError:
            cells.append(c)
    g = _types.FunctionType(fn.__code__, fn.__globals__, fn.__name__, fn.__defaults__, tuple(cells))
    g.__kwdefaults__ = fn.__kwdefaults__
    return g


class Res:
    __slots__ = ("name", "w", "r", "excl")

    def __init__(self, name):
        self.name = name
        self.w = None
        self.r = []
        self.excl = False


class Op:
    __slots__ = ("eng", "fn", "deps", "is_dma", "sem", "count", "signal", "idx", "line", "alldeps")


class Prog:
    ENGS = ("pe", "act", "dve", "pool", "sp")

    def __init__(self, nc, es):
        self.nc = nc
        self.es = es
        self.ops = []
        self.dma_keys = {}
        self.nres = 0
        self._uid = 0
        self.fence = {}
        self.scopes = []
        self.barriers = []

    def sb(self, name, shape, dt=F32):
        es = self.scopes[-1] if self.scopes else self.es
        self._uid += 1
        return es.enter_context(self.nc.sbuf_tensor("%s_u%d" % (name, self._uid), list(shape), dt))

    def scope(self):
        import contextlib

        @contextlib.contextmanager
        def cm():
            with ExitStack() as es2:
                self.scopes.append(es2)
                try:
                    yield es2
                finally:
                    self.scopes.pop()
        return cm()

    def ps(self, name, shape, dt=F32):
        return self.es.enter_context(self.nc.psum_tensor(name, list(shape), dt))

    def res(self, name=None):
        self.nres += 1
        return Res(name or ("r%d" % self.nres))

    def resl(self, n, name="r"):
        return [self.res("%s%d" % (name, i)) for i in range(n)]

    def op(self, eng, fn, r=(), w=(), dma_key=None):
        o = Op()
        o.eng = eng
        o.fn = _snapshot(fn)
        o.is_dma = dma_key is not None
        o.signal = False
        o.sem = dma_key
        o.count = 0
        o.idx = len(self.ops)
        import sys as _sys
        o.line = _sys._getframe(1).f_lineno
        ex = [x for x in r if x.excl]
        if ex:
            w = list(w) + [x for x in ex if x not in w]
            r = [x for x in r if not x.excl]
        deps = {}
        for x in r:
            if x.w is not None:
                deps[x.w.idx] = (x.w, "raw")
        for x in w:
            if x.w is not None:
                deps[x.w.idx] = (x.w, "waw")
            for rd in x.r:
                if rd.idx not in deps:
                    deps[rd.idx] = (rd, "war")
        fd = []
        if self.fence.get(eng):
            for d in self.fence[eng]:
                if d.is_dma or d.eng != eng or eng == "pool":
                    fd.append(d)
            self.fence[eng] = None
        for d, kind in deps.values():
            if d is o:
                continue
            if d.is_dma:
                fd.append(d)
            elif d.eng != eng:
                fd.append(d)
            else:
                if o.is_dma or eng == "pool" or (SAME_ENGINE_SYNC and kind != "war" and eng != "pe"):
                    fd.append(d)
        o.deps = fd
        o.alldeps = [(d.idx, d.eng, d.line, k) for d, k in deps.values()]
        for d in fd:
            d.signal = True
        for x in r:
            x.r.append(o)
        for x in w:
            x.w = o
            x.r = []
        self.ops.append(o)
        return o

    def barrier(self):
        self.barriers.append(len(self.ops))
        last = {}
        for o in self.ops:
            if o.is_dma:
                last[("dma", o.sem)] = o
            else:
                last[o.eng] = o
        ops = list(last.values())
        for e in self.ENGS:
            self.fence[e] = list(ops)

    def dma(self, out, in_, r=(), w=(), key=None, q="sp"):
        if key is None:
            key = w[0].name
        return self.op(q, lambda e, out=out, in_=in_: e.dma_start(out=out, in_=in_), r=r, w=w, dma_key=key)

    def emit(self):
        nc = self.nc
        es = self.es
        ROT = 60000
        eng_sems = {e: [] for e in self.ENGS}
        dma_sem = {}
        dma_cnt = {}
        dma_uses = {}
        cnt = {e: 0 for e in self.ENGS}
        nsem = 0
        all_dma_sems = []
        final_cnt = {}
        free_sems = []
        bset = set(self.barriers)
        for o in self.ops:
            if o.idx in bset:
                for k in list(dma_sem.keys()):
                    free_sems.append((dma_sem.pop(k), dma_cnt.pop(k)))
            if o.is_dma:
                k = o.sem
                if k in dma_sem and dma_cnt[k] > 60000:
                    dma_sem.pop(k)
                    dma_cnt.pop(k)
                if k not in dma_sem:
                    while free_sems and free_sems[0][1] > 50000:
                        free_sems.pop(0)
                    if free_sems:
                        dma_sem[k], dma_cnt[k] = free_sems.pop(0)
                    else:
                        dma_sem[k] = es.enter_context(nc.semaphore("dsem_%d" % nsem))
                        dma_cnt[k] = 0
                        nsem += 1
                        all_dma_sems.append(dma_sem[k])
                dma_cnt[k] += 16
                final_cnt[dma_sem[k].num] = (dma_sem[k], dma_cnt[k])
                o.sem = dma_sem[k]
                o.count = dma_cnt[k]
            elif o.signal:
                n = cnt[o.eng]
                cnt[o.eng] += 1
                si = n // ROT
                if si >= len(eng_sems[o.eng]):
                    eng_sems[o.eng].append(es.enter_context(nc.semaphore("sem_%s%d" % (o.eng, si))))
                    nsem += 1
                o.sem = eng_sems[o.eng][si]
                o.count = n % ROT + 1
        self.stats = dict(cnt)
        self.stats["n_ops"] = len(self.ops)
        self.stats["n_sems"] = nsem
        per = {e: [o for o in self.ops if o.eng == e] for e in self.ENGS}
        final_dma = list(final_cnt.values())

        def run(e, ename):
            seen = {}
            nw = 0
            for o in per[ename]:
                need = {}
                for d in o.deps:
                    s = d.sem
                    if need.get(s.num, (None, 0))[1] < d.count:
                        need[s.num] = (s, d.count)
                for s, c in need.values():
                    if seen.get(s.num, 0) < c:
                        e.wait_ge(s, c)
                        seen[s.num] = c
                        nw += 1
                ins = o.fn(e)
                if o.is_dma:
                    ins.then_inc(o.sem, 16)
                elif o.signal:
                    ins.then_inc(o.sem, 1)
            if ename == "sp":
                for s, c in final_dma:
                    e.wait_ge(s, c)
            self.stats["waits_" + ename] = nw

        with nc.Block() as block:
            block.sync(lambda e: run(e, "sp"))
            block.tensor(lambda e: run(e, "pe"))
            block.scalar(lambda e: run(e, "act"))
            block.vector(lambda e: run(e, "dve"))
            block.gpsimd(lambda e: run(e, "pool"))


D = 1024
KC = 8
NT = 2048
TG = 512
NTG = NT // TG
DFF = 4096
FC = DFF // 128
NORM_EPS = 1e-6
GN_EPS = 64e-5


class Ctx:
    def __init__(self, p):
        self.p = p
        nc = p.nc
        self.ps = [p.ps("psb%d" % i, [128, 512], F32) for i in range(8)]
        self.psr = [p.res("psb%d" % i) for i in range(8)]
        for r_ in self.psr:
            r_.excl = True
        self.psi = 0
        self.rot = list(range(8))
        self.ones_bf = p.sb("ones_bf", [128, 128], BF16)
        self.r_const = p.res("consts")
        self.eps_t = p.sb("eps_t", [128, 4], F32)
        self.ones_f32 = p.sb("ones_f32c", [128, 128], F32)
        p.op("pool", lambda e: e.memset(self.ones_f32[:], 1.0), w=[self.r_const])
        p.op("pool", lambda e: e.memset(self.ones_bf[:], 1.0), w=[self.r_const])
        p.op("pool", lambda e: e.memset(self.eps_t[:, 0:1], NORM_EPS), w=[self.r_const])
        p.op("pool", lambda e: e.memset(self.eps_t[:, 1:2], GN_EPS), w=[self.r_const])
        p.op("pool", lambda e: e.memset(self.eps_t[:, 2:3], 1.0), w=[self.r_const])
        p.op("pool", lambda e: e.memset(self.eps_t[:, 3:4], 0.0), w=[self.r_const])

    def bank(self):
        self.psi = (self.psi + 1) % len(self.rot)
        i = self.rot[self.psi]
        return self.ps[i], self.psr[i]


class WStream:
    def __init__(self, p, name, shape, nbuf=2, cast_engs=("pool",), direct=False):
        self.p = p
        self.shape = shape
        self.nbuf = nbuf
        if not direct:
            self.stg = [p.sb("%s_stg%d" % (name, i), shape, F32) for i in range(nbuf)]
            self.stg_r = [p.res("%s_stg%d" % (name, i)) for i in range(nbuf)]
        self.bf = [p.sb("%s_bf%d" % (name, i), shape, BF16) for i in range(nbuf)]
        self.bf_r = [p.res("%s_bf%d" % (name, i)) for i in range(nbuf)]
        self.i = 0
        self.cast_engs = cast_engs
        self.ci = 0

    def load(self, src_ap, sl=None):
        p = self.p
        i = self.i
        self.i = (i + 1) % self.nbuf
        bf, br = self.bf[i], self.bf_r[i]
        idx = sl if sl is not None else tuple(slice(None) for _ in self.shape)
        if src_ap.dtype == BF16:
            p.dma(bf[idx], src_ap, w=[br])
            return bf, br
        stg, sr = self.stg[i], self.stg_r[i]
        p.dma(stg[idx], src_ap, w=[sr])
        ce = self.cast_engs[self.ci % len(self.cast_engs)]
        self.ci += 1
        copy_op(p, ce, bf[idx], stg[idx], r=[sr], w=[br])
        return bf, br


def copy_op(p, eng, out, in_, r, w):
    if eng == "act":
        return p.op("act", lambda e: e.copy(out=out, in_=in_), r=r, w=w)
    return p.op(eng, lambda e: e.tensor_copy(out=out, in_=in_), r=r, w=w)


def emit_norm(p, cx, x_sb, x_r, gmul, shift, vec_r, h_out, h_r, scr):
    ts = slice(0, TG)
    sq, sq_r = scr["sq"]
    rstd, rstd_r = scr["rstd"]
    tmp, tmp_r = scr["tmp"]
    p.op("act", lambda e: e.activation(out=sq[:], in_=x_sb[:, :, ts], func=AF.Square), r=[x_r], w=[sq_r])
    ps, pr = cx.bank()
    for kc in range(KC):
        p.op("pe", lambda e, kc=kc: e.matmul(ps[:, 0:TG], cx.ones_bf[:], sq[:, kc, :], start=(kc == 0), stop=(kc == KC - 1)),
             r=[sq_r, cx.r_const], w=[pr])
    p.op("act", lambda e: e.activation(out=rstd[:], in_=ps[:, 0:TG], func=AF.Sqrt, bias=cx.eps_t[:, 0:1], scale=1.0 / D),
         r=[pr, cx.r_const], w=[rstd_r])
    p.op("dve", lambda e: e.reciprocal(out=rstd[:], in_=rstd[:]), r=[rstd_r], w=[rstd_r])
    for kc in range(KC):
        if shift is None:
            p.op("dve", lambda e, kc=kc: e.scalar_tensor_tensor(out=h_out[:, kc, :], in0=x_sb[:, kc, ts], scalar=gmul[:, kc:kc + 1],
                                                               in1=rstd[:], op0=ALU.mult, op1=ALU.mult),
                 r=[x_r, rstd_r, vec_r], w=[h_r])
        else:
            t2, t2r = tmp[kc % 2], tmp_r[kc % 2]
            p.op("dve", lambda e, kc=kc, t2=t2: e.scalar_tensor_tensor(out=t2[:], in0=x_sb[:, kc, ts], scalar=gmul[:, kc:kc + 1],
                                                                      in1=rstd[:], op0=ALU.mult, op1=ALU.mult),
                 r=[x_r, rstd_r, vec_r], w=[t2r])
            p.op("act", lambda e, kc=kc, t2=t2: e.activation(out=h_out[:, kc, :], in_=t2[:], func=AF.Identity,
                                                            bias=shift[:, kc:kc + 1], scale=1.0),
                 r=[t2r, vec_r], w=[h_r])


def norm_scratch(p, name):
    return {
        "sq": (p.sb(name + "_sq", [128, KC, TG], BF16), p.res(name + "_sq")),
        "rstd": (p.sb(name + "_rstd", [128, TG], F32), p.res(name + "_rstd")),
        "tmp": ([p.sb(name + "_tmp%d" % i, [128, TG], F32) for i in range(2)], [p.res(name + "_tmp%d" % i) for i in range(2)]),
    }


def emit_mlp(p, cx, xt, x_r, gm, gm_r, shift, vec_r, up_d, down_d, st):
    h, h_r, a, a_r, relu, relu_r, wu, wd, scr = st["h"], st["h_r"], st["a"], st["a_r"], st["relu"], st["relu_r"], st["wu"], st["wd"], st["scr"]
    emit_norm(p, cx, xt, x_r, gm[:, 0:KC], shift, vec_r, h, h_r, scr)
    for f2 in range(FC // 2):
        wb, wr = wu.load(up_d[f2])
        for fi in range(2):
            fc = f2 * 2 + fi
            ps, pr = cx.bank()
            for kc in range(KC):
                p.op("pe", lambda e, kc=kc, ps=ps, wb=wb, fi=fi: e.matmul(ps[:, 0:TG], wb[:, kc, fi * 128:(fi + 1) * 128], h[:, kc, :],
                                                                      start=(kc == 0), stop=(kc == KC - 1)),
                     r=[wr, h_r], w=[pr])
            ri = st["ri"]
            st["ri"] += 1
            rl, rr = relu[ri % 2], relu_r[ri % 2]
            p.op("act", lambda e, ps=ps, rl=rl: e.activation(out=rl[:], in_=ps[:, 0:TG], func=AF.Relu), r=[pr], w=[rr])
            p.op("dve", lambda e, ps=ps, rl=rl, fc=fc: e.scalar_tensor_tensor(
                out=a[:, fc, :], in0=ps[:, 0:TG], scalar=0.0, in1=rl[:], op0=ALU.max, op1=ALU.mult),
                r=[pr, rr], w=[a_r])
    for oc in range(KC):
        ps, pr = cx.bank()
        wb, wr = wd.load(down_d[oc])
        for fc in range(FC):
            p.op("pe", lambda e, fc=fc, ps=ps, wb=wb: e.matmul(ps[:, 0:TG], wb[:, fc, :], a[:, fc, :],
                                                             start=(fc == 0), stop=(fc == FC - 1)),
                 r=[wr, a_r], w=[pr])
        xs = xt[:, oc, :]
        p.op("dve", lambda e, ps=ps, xs=xs, oc=oc: e.scalar_tensor_tensor(out=xs, in0=ps[:, 0:TG], scalar=gm[:, KC + oc:KC + oc + 1],
                                                                      in1=xs, op0=ALU.mult, op1=ALU.add),
             r=[pr, gm_r, x_r], w=[x_r])


def mlp_state(p):
    return {
        "h": p.sb("mlp_h", [128, KC, TG], BF16), "h_r": p.res("mlp_h"),
        "a": p.sb("mlp_a", [128, FC, TG], BF16), "a_r": p.res("mlp_a"),
        "relu": [p.sb("mlp_relu%d" % i, [128, TG], F32) for i in range(2)], "relu_r": p.resl(2, "mlp_relu"),
        "wu": WStream(p, "wup", [128, KC, 256], nbuf=3, direct=True),
        "wd": WStream(p, "wdn", [128, FC, 128], nbuf=2, direct=True),
        "scr": norm_scratch(p, "mlpn"), "ri": 0,
    }


def mod_gm(p, name, vec, vec_r, col_g, col_sc, col_gt):
    gm = p.sb(name, [128, 2 * KC], F32)
    gm_r = p.res(name)
    p.op("dve", lambda e: e.scalar_tensor_tensor(out=gm[:, 0:KC], in0=vec[:, col_sc:col_sc + KC], scalar=1.0,
                                                in1=vec[:, col_g:col_g + KC], op0=ALU.add, op1=ALU.mult), r=[vec_r], w=[gm_r])
    if col_gt is not None:
        p.op("dve", lambda e: e.tensor_scalar_add(out=gm[:, KC:2 * KC], in0=vec[:, col_gt:col_gt + KC], scalar1=1.0), r=[vec_r], w=[gm_r])
    return gm, gm_r


import os
PRE_STOP = int(os.environ.get('PRE_STOP', '99'))
H = 16
HD = 64
NMODC = 208
LW_SCALE = -0.6065306597126334


def vec_cols(names):
    return {n: i * KC for i, n in enumerate(names)}


RW_VECS = ["norm_mix_g", "norm_mlp_g", "mu0", "mu1", "mu2", "mu3", "mu4", "mu5", "w0", "a0", "v0", "kk", "ka", "rk", "lnw", "lnb"]
FX_VECS = ["norm_mix_g", "norm_mlp_g"]


class Consts:
    def __init__(self, p, cx):
        self.r = p.res("consts2")
        self.bo1 = p.sb("bo1", [128, 128], F32)
        self.bo64 = p.sb("bo64", [128, 128], F32)
        self.ident = p.sb("ident", [128, 128], F32)
        self.ones_f = p.sb("ones_f", [128, 128], F32)
        for t, v in ((self.bo1, 1.0), (self.bo64, 1.0 / 64)):
            p.op("pool", lambda e, t=t: e.memset(t[:], 0.0), w=[self.r])
            p.op("pool", lambda e, t=t, v=v: e.memset(t[0:64, 0:64], v), w=[self.r])
            p.op("pool", lambda e, t=t, v=v: e.memset(t[64:128, 64:128], v), w=[self.r])
        p.op("pool", lambda e: e.memset(self.ones_f[:], 1.0), w=[self.r])
        p.op("pool", lambda e: e.affine_select(out=self.ident[:], in_=self.ones_f[:], pattern=[[1, 128]], compare_op=ALU.is_equal,
                                               fill=0.0, base=0, channel_multiplier=-1), r=[self.r], w=[self.r])


def stage_mod(p, cx, cT_d, modw_d, kvmodw_d, modb_d, mod_sb, mod_r):
    with p.scope() as es2:
        nc = p.nc
        sb = p.sb
        cT = sb("mod_cT", [128, KC])
        ca = sb("mod_ca", [128, KC, 128])
        bias = sb("mod_bias", [128, NMODC])
        r_c = p.res("mod_c")
        r_b = p.res("mod_b")
        p.dma(cT[:], cT_d, w=[r_c])
        p.dma(bias[:], modb_d, w=[r_b])
        sig = sb("mod_sig", [128, KC])
        p.op("act", lambda e: e.activation(out=sig[:], in_=cT[:], func=AF.Sigmoid), r=[r_c], w=[r_c])
        p.op("dve", lambda e: e.tensor_tensor(out=sig[:], in0=cT[:], in1=sig[:], op=ALU.mult), r=[r_c], w=[r_c])
        for kc in range(KC):
            p.op("dve", lambda e, kc=kc: e.tensor_scalar_mul(out=ca[:, kc, :], in0=cx_ones(cx), scalar1=sig[:, kc:kc + 1]), r=[r_c, cx.r_const], w=[r_c])
        wst = [sb("mod_w%d" % i, [128, KC, 512]) for i in range(2)]
        wr = p.resl(2, "mod_w")
        blocks = []
        for i in range(4):
            v = modw_d[i].rearrange("(kc p) m -> p kc m", p=128)
            for j in range(12):
                blocks.append((v[:, :, j * 512:(j + 1) * 512], i * 48 + j * 4))
        v = kvmodw_d.rearrange("(kc p) m -> p kc m", p=128)
        for j in range(4):
            blocks.append((v[:, :, j * 512:(j + 1) * 512], 192 + j * 4))
        for bi, (src, c0) in enumerate(blocks):
            w, r = wst[bi % 2], wr[bi % 2]
            p.dma(w[:], src, w=[r])
            ps, pr = cx.bank()
            for j in range(4):
                for kc in range(KC):
                    p.op("pe", lambda e, w=w, j=j, kc=kc, ps=ps: e.matmul(ps[:, j * 128:(j + 1) * 128], w[:, kc, j * 128:(j + 1) * 128], ca[:, kc, :],
                                                                       start=(kc == 0), stop=(kc == KC - 1)),
                         r=[r, r_c], w=[pr])
            psv = ps[:, 0:512].rearrange("p (j c) -> p j c", j=4)
            p.op("dve", lambda e, psv=psv, c0=c0: e.tensor_tensor(out=mod_sb[:, c0:c0 + 4], in0=psv[:, :, 0], in1=bias[:, c0:c0 + 4], op=ALU.add), r=[pr, r_b], w=[mod_r])


def cx_ones(cx):
    return cx.ones_f32[:]


class ProjW:
    def __init__(self, p, name):
        self.ws = WStream(p, name, [128, KC, 256], nbuf=3, direct=True)
        self.p = p

    def load(self, W_d, c0, n=256):
        return self.ws.load(W_d[c0 // 256])


def proj_mm(p, cx, wb, wr, off, h, h_r, ncols=128):
    ps, pr = cx.bank()
    for kc in range(KC):
        p.op("pe", lambda e, kc=kc: e.matmul(ps[0:ncols, 0:TG], wb[:, kc, off:off + ncols], h[:, kc, :], start=(kc == 0), stop=(kc == KC - 1)),
             r=[wr, h_r], w=[pr])
    return ps, pr


def head_sum(p, cx, cs, mat, src, src_r):
    ps, pr = cx.bank()
    p.op("pe", lambda e: e.matmul(ps[:, 0:TG], mat[:], src[:], start=True, stop=True), r=[src_r, cs.r], w=[pr])
    return ps, pr


class Rot:
    def __init__(self, p, name, shape, dt, n):
        self.t = [p.sb("%s%d" % (name, i), shape, dt) for i in range(n)]
        self.r = [p.res("%s%d" % (name, i)) for i in range(n)]
        self.i = 0

    def get(self):
        i = self.i
        self.i = (i + 1) % len(self.t)
        return self.t[i], self.r[i]


OUT_Q = os.environ.get("OUT_Q", "sp")


def out_dma(p, dst, src, src_r, key, dst_r=None):
    p.dma(dst, src, r=[src_r], w=[dst_r if dst_r is not None else p.res("o_" + key)], key="o_" + src_r.name, q=OUT_Q)


def stage_pre_rwkv(p, cx, cs, T, layer, x_d, vec, vec_r, vc, mod_sb, mod_r, W, dd):
    nc = p.nc
    NTL = T // TG
    mc = layer * 48
    with p.scope() as es2:
        sb = p.sb
        gm = sb("pr_gm", [128, 2 * KC])
        gm_r = p.res("pr_gm")
        g0 = vc["norm_mix_g"]
        p.op("dve", lambda e: e.scalar_tensor_tensor(out=gm[:, 0:KC], in0=mod_sb[:, mc + 8:mc + 16], scalar=1.0, in1=vec[:, g0:g0 + KC],
                                                    op0=ALU.add, op1=ALU.mult), r=[vec_r, mod_r], w=[gm_r])
        ka0 = vc["ka"]
        p.op("dve", lambda e: e.tensor_scalar(out=gm[:, KC:2 * KC], in0=vec[:, ka0:ka0 + KC], scalar1=-1.0, scalar2=1.0, op0=ALU.mult, op1=ALU.add),
             r=[vec_r], w=[gm_r])
        shift = mod_sb[:, mc:mc + 8]
        lw_r = p.res("pr_lora")
        specs = [("a1", W["a1"].rearrange("(kc p) m -> p kc m", p=128), [128, KC, 64]),
                 ("w1", W["w1"].rearrange("(kc p) m -> p kc m", p=128), [128, KC, 64]),
                 ("g1", W["g1"].rearrange("(kc p) m -> p kc m", p=128), [128, KC, 160]),
                 ("a2", W["a2"], [64, D]), ("w2", W["w2"], [64, D]),
                 ("g2a", W["g2"][0:128, :], [128, D]), ("g2b", W["g2"][128:160, :], [32, D])]
        if layer > 0:
            specs += [("v1", W["v1"].rearrange("(kc p) m -> p kc m", p=128), [128, KC, 32]), ("v2", W["v2"], [32, D])]
        lt_ = {n: sb("prb_" + n, shp, BF16) for n, _, shp in specs}
        with p.scope() as stg_es:
            for n, src, shp in specs:
                st = p.sb("prs_" + n, shp)
                rr = p.res("prs_" + n)
                p.dma(st[:], src, w=[rr])
                p.op("pool", lambda e, n=n, st=st: e.tensor_copy(out=lt_[n][:], in_=st[:]), r=[rr], w=[lw_r])
        a1, w1, g1, a2, w2, g2a, g2b = (lt_[n] for n in ("a1", "w1", "g1", "a2", "w2", "g2a", "g2b"))
        if layer > 0:
            v1, v2 = lt_["v1"], lt_["v2"]
        p.barrier()
        xt = Rot(p, "pr_x", [128, KC, TG], F32, 1)
        hh = sb("pr_h", [128, KC, TG + 1])
        hh_r = p.res("pr_h")
        xx = sb("pr_xx", [128, KC, TG], BF16)
        xx_r = p.res("pr_xx")
        xj = Rot(p, "pr_xj", [128, KC, TG], BF16, 1)
        a_t = sb("pr_a", [128, KC, TG])
        a_r = p.res("pr_a")
        k_t = sb("pr_k", [128, KC, TG])
        k_r = p.res("pr_k")
        v_t = sb("pr_v", [128, KC, TG])
        v_r = p.res("pr_v")
        scr = norm_scratch(p, "prn")
        pw = ProjW(p, "pr_w")
        f1 = Rot(p, "pr_f1", [128, TG], F32, 2)
        f2 = Rot(p, "pr_f2", [128, TG], F32, 2)
        ob = Rot(p, "pr_ob", [128, TG], F32, 3)
        obn = Rot(p, "pr_obn", [128, TG], F32, int(os.environ.get("OBN", "2")))
        if os.environ.get("PRE_INIT"):
            for t_, r_ in zip(obn.t, obn.r):
                p.op("pool", lambda e, t_=t_: e.memset(t_[:], 0.0), w=[r_])
        lt = Rot(p, "pr_lt", [128, TG], BF16, 2)
        xv = x_d.rearrange("(c p) t -> p c t", p=128)
        p.op("pool", lambda e: e.memset(hh[:, :, 0:1], 0.0), w=[hh_r])
        if os.environ.get("PRE_PAD"):
            dr = p.res("padr")
            for i_ in range(int(os.environ["PRE_PAD"])):
                p.op("dve", lambda e: e.tensor_scalar_mul(out=xx[:, 0, 0:8], in0=xx[:, 0, 0:8], scalar1=1.0), r=[dr], w=[dr])

        def mix_dve(j):
            t, r = xj.get()
            m0 = vc["mu%d" % j]
            for kc in range(KC):
                p.op("dve", lambda e, kc=kc: e.scalar_tensor_tensor(
                    out=t[:, kc, :], in0=xx[:, kc, :], scalar=vec[:, m0 + kc:m0 + kc + 1], in1=hh[:, kc, 1:TG + 1], op0=ALU.mult, op1=ALU.add),
                    r=[xx_r, hh_r, vec_r], w=[r])
            return t, r

        def lora1(xm, xm_r, w1t, R, func):
            ps, pr = cx.bank()
            for kc in range(KC):
                p.op("pe", lambda e, kc=kc: e.matmul(ps[0:R, 0:TG], w1t[:, kc, 0:R], xm[:, kc, :], start=(kc == 0), stop=(kc == KC - 1)),
                     r=[lw_r, xm_r], w=[pr])
            t, r = lt.get()
            p.op("act", lambda e: e.activation(out=t[0:R, :], in_=ps[0:R, 0:TG], func=func), r=[pr], w=[r])
            return t, r

        for ti in range(NTL):
            cols = slice(ti * TG, (ti + 1) * TG)
            x_t, x_r = xt.get()
            p.dma(x_t[:], xv[:, :, cols], w=[x_r])
            if ti > 0:
                p.op("pool", lambda e: e.tensor_copy(out=hh[:, :, 0:1], in_=hh[:, :, TG:TG + 1]), r=[hh_r], w=[hh_r])
            emit_norm(p, cx, x_t, x_r, gm[:, 0:KC], shift, gm_r, hh[:, :, 1:TG + 1], hh_r, scr)
            p.op("dve", lambda e: e.tensor_tensor(out=xx[:], in0=hh[:, :, 0:TG], in1=hh[:, :, 1:TG + 1], op=ALU.subtract), r=[hh_r], w=[xx_r])
            if PRE_STOP <= 0:
                continue
            xm, xm_r = mix_dve(4)
            ta, ta_r = lora1(xm, xm_r, a1, 64, AF.Identity)
            for oc in range(KC):
                ps, pr = cx.bank()
                p.op("pe", lambda e, oc=oc, ps=ps: e.matmul(ps[:, 0:TG], a2[:, oc * 128:(oc + 1) * 128], ta[0:64, :], start=True, stop=True),
                     r=[lw_r, ta_r], w=[pr])
                c0 = vc["a0"] + oc
                p.op("act", lambda e, oc=oc, ps=ps, c0=c0: e.activation(out=a_t[:, oc, :], in_=ps[:, 0:TG], func=AF.Sigmoid, bias=vec[:, c0:c0 + 1], scale=1.0),
                     r=[pr, vec_r], w=[a_r])
            if PRE_STOP <= 1:
                continue
            xm, xm_r = mix_dve(2)
            for o2 in range(KC // 2):
                wb, wr = pw.load(W["wk"], o2 * 256)
                for oi in range(2):
                    oc = o2 * 2 + oi
                    ps, pr = proj_mm(p, cx, wb, wr, oi * 128, xm, xm_r)
                    kkr, kkr_r = f1.get()
                    ck = vc["kk"] + oc
                    p.op("dve", lambda e, ps=ps, kkr=kkr, ck=ck: e.tensor_scalar_mul(out=kkr[:], in0=ps[:, 0:TG], scalar1=vec[:, ck:ck + 1]), r=[pr, vec_r], w=[kkr_r])
                    sq, sq_r = f2.get()
                    p.op("act", lambda e, kkr=kkr, sq=sq: e.activation(out=sq[:], in_=kkr[:], func=AF.Square), r=[kkr_r], w=[sq_r])
                    ps2, pr2 = head_sum(p, cx, cs, cs.bo1, sq, sq_r)
                    rn, rn_r = f2.get()
                    p.op("act", lambda e, ps2=ps2, rn=rn: e.activation(out=rn[:], in_=ps2[:, 0:TG], func=AF.Sqrt), r=[pr2], w=[rn_r])
                    p.op("dve", lambda e, rn=rn: e.tensor_scalar_max(out=rn[:], in0=rn[:], scalar1=1e-12), r=[rn_r], w=[rn_r])
                    p.op("dve", lambda e, rn=rn: e.reciprocal(out=rn[:], in_=rn[:]), r=[rn_r], w=[rn_r])
                    nk, nk_r = ob.get()
                    p.op("dve", lambda e, nk=nk, kkr=kkr, rn=rn: e.scalar_tensor_tensor(out=nk[:], in0=kkr[:], scalar=-1.0, in1=rn[:], op0=ALU.mult, op1=ALU.mult),
                         r=[kkr_r, rn_r], w=[nk_r])
                    out_dma(p, dd["a"][oc * 128:(oc + 1) * 128, cols], nk[:], nk_r, "pr_oa")
                    bt, bt_r = ob.get()
                    p.op("dve", lambda e, bt=bt, nk=nk, oc=oc: e.scalar_tensor_tensor(out=bt[:], in0=nk[:], scalar=-1.0, in1=a_t[:, oc, :], op0=ALU.mult, op1=ALU.mult),
                         r=[nk_r, a_r], w=[bt_r])
                    out_dma(p, dd["b"][oc * 128:(oc + 1) * 128, cols], bt[:], bt_r, "pr_ob")
                    tm, tm_r = f1.get()
                    cka = vc["ka"] + oc
                    p.op("dve", lambda e, tm=tm, oc=oc, cka=cka: e.tensor_scalar(out=tm[:], in0=a_t[:, oc, :], scalar1=vec[:, cka:cka + 1],
                                                                             scalar2=gm[:, KC + oc:KC + oc + 1], op0=ALU.mult, op1=ALU.add),
                         r=[a_r, vec_r, gm_r], w=[tm_r])
                    p.op("dve", lambda e, tm=tm, ps=ps, oc=oc: e.tensor_tensor(out=k_t[:, oc, :], in0=ps[:, 0:TG], in1=tm[:], op=ALU.mult),
                         r=[pr, tm_r], w=[k_r])
            out_dma(p, dd["k"].rearrange("(c p) t -> p c t", p=128)[:, :, cols], k_t[:], k_r, "pr_ok")
            if PRE_STOP <= 2:
                continue
            xm, xm_r = mix_dve(3)
            if layer > 0:
                tv, tv_r = lora1(xm, xm_r, v1, 32, AF.Identity)
            for o2 in range(KC // 2):
                wb, wr = pw.load(W["wv"], o2 * 256)
                for oi in range(2):
                    oc = o2 * 2 + oi
                    ps, pr = proj_mm(p, cx, wb, wr, oi * 128, xm, xm_r)
                    if layer == 0:
                        p.op("act", lambda e, ps=ps, oc=oc: e.copy(out=v_t[:, oc, :], in_=ps[:, 0:TG]), r=[pr], w=[v_r])
                    else:
                        ps2, pr2 = cx.bank()
                        p.op("pe", lambda e, oc=oc, ps2=ps2: e.matmul(ps2[:, 0:TG], v2[:, oc * 128:(oc + 1) * 128], tv[0:32, :], start=True, stop=True),
                             r=[lw_r, tv_r], w=[pr2])
                        gt, gt_r = f1.get()
                        c0 = vc["v0"] + oc
                        p.op("act", lambda e, gt=gt, ps2=ps2, c0=c0: e.activation(out=gt[:], in_=ps2[:, 0:TG], func=AF.Sigmoid, bias=vec[:, c0:c0 + 1], scale=1.0),
                             r=[pr2, vec_r], w=[gt_r])
                        vf, vf_r = f2.get()
                        p.dma(vf[:], dd["vfirst"][oc * 128:(oc + 1) * 128, cols], w=[vf_r])
                        p.op("dve", lambda e, vf=vf, ps=ps: e.tensor_tensor(out=vf[:], in0=vf[:], in1=ps[:, 0:TG], op=ALU.subtract), r=[vf_r, pr], w=[vf_r])
                        p.op("dve", lambda e, vf=vf, gt=gt: e.tensor_tensor(out=vf[:], in0=vf[:], in1=gt[:], op=ALU.mult), r=[vf_r, gt_r], w=[vf_r])
                        p.op("dve", lambda e, vf=vf, ps=ps, oc=oc: e.tensor_tensor(out=v_t[:, oc, :], in0=vf[:], in1=ps[:, 0:TG], op=ALU.add), r=[vf_r, pr], w=[v_r])
            out_dma(p, dd["v"].rearrange("(c p) t -> p c t", p=128)[:, :, cols], v_t[:], v_r, "pr_ov")
            if layer == 0:
                out_dma(p, dd["vfirst"].rearrange("(c p) t -> p c t", p=128)[:, :, cols], v_t[:], v_r, "pr_ovf")
            if PRE_STOP <= 3:
                continue
            xm, xm_r = mix_dve(0)
            for o2 in range(KC // 2):
                wb, wr = pw.load(W["wr"], o2 * 256)
                for oi in range(2):
                    oc = o2 * 2 + oi
                    ps, pr = proj_mm(p, cx, wb, wr, oi * 128, xm, xm_r)
                    rt, rt_r = ob.get()
                    p.op("act", lambda e, rt=rt, ps=ps: e.copy(out=rt[:], in_=ps[:, 0:TG]), r=[pr], w=[rt_r])
                    out_dma(p, dd["r"][oc * 128:(oc + 1) * 128, cols], rt[:], rt_r, "pr_or")
                    if os.environ.get("PRE_NOBONUS"):
                        continue
                    rk, rk_r = f1.get()
                    crk = vc["rk"] + oc
                    p.op("dve", lambda e, rk=rk, ps=ps, oc=oc, crk=crk: e.scalar_tensor_tensor(out=rk[:], in0=ps[:, 0:TG], scalar=vec[:, crk:crk + 1],
                                                                                          in1=k_t[:, oc, :], op0=ALU.mult, op1=ALU.mult),
                         r=[pr, vec_r, k_r], w=[rk_r])
                    if os.environ.get("PRE_NOBONUS") == "2":
                        continue
                    ps2, pr2 = head_sum(p, cx, cs, cs.bo1, rk, rk_r)
                    if os.environ.get("PRE_NOBONUS") == "3":
                        continue
                    bn, bn_r = obn.get()
                    p.op("dve", lambda e, bn=bn, ps2=ps2, oc=oc: e.tensor_tensor(out=bn[:], in0=ps2[:, 0:TG], in1=v_t[:, oc, :], op=ALU.mult),
                         r=[pr2, v_r], w=[bn_r])
                    if os.environ.get("PRE_NOBONUS") == "5":
                        p.op("pool", lambda e, bn=bn: e.tensor_scalar_mul(out=bn[:], in0=bn[:], scalar1=1.0), r=[bn_r], w=[bn_r])
                    elif os.environ.get("PRE_NOBONUS") == "8":
                        b2, b2_r = ob.get()
                        p.op("pool", lambda e, bn=bn, b2=b2: e.tensor_copy(out=b2[:], in_=bn[:]), r=[bn_r], w=[b2_r])
                        out_dma(p, dd["bonus"][oc * 128:(oc + 1) * 128, cols], b2[:], b2_r, "pr_obn")
                    elif os.environ.get("PRE_NOBONUS") == "6":
                        out_dma(p, dd["bonus"][oc * 128:(oc + 1) * 128, cols], rt[:], rt_r, "pr_obn")
                    elif os.environ.get("PRE_NOBONUS") != "4":
                        out_dma(p, dd["bonus"][oc * 128:(oc + 1) * 128, cols], bn[:], bn_r, "pr_obn")
            if PRE_STOP <= 4:
                continue
            xm, xm_r = mix_dve(1)
            tw, tw_r = lora1(xm, xm_r, w1, 64, AF.Tanh)
            for oc in range(KC):
                ps, pr = cx.bank()
                p.op("pe", lambda e, oc=oc, ps=ps: e.matmul(ps[:, 0:TG], w2[:, oc * 128:(oc + 1) * 128], tw[0:64, :], start=True, stop=True),
                     r=[lw_r, tw_r], w=[pr])
                wt, wt_r = ob.get()
                c0 = vc["w0"] + oc
                p.op("act", lambda e, wt=wt, ps=ps, c0=c0: e.activation(out=wt[:], in_=ps[:, 0:TG], func=AF.Sigmoid, bias=vec[:, c0:c0 + 1], scale=1.0),
                     r=[pr, vec_r], w=[wt_r])
                p.op("pool", lambda e, wt=wt: e.tensor_scalar_mul(out=wt[:], in0=wt[:], scalar1=LW_SCALE), r=[wt_r], w=[wt_r])
                out_dma(p, dd["lw"][oc * 128:(oc + 1) * 128, cols], wt[:], wt_r, "pr_ow")
            if PRE_STOP <= 5:
                continue
            xm, xm_r = mix_dve(5)
            tg0, tg0_r = lora1(xm, xm_r, g1, 128, AF.Sigmoid)
            ps, pr = cx.bank()
            for kc in range(KC):
                p.op("pe", lambda e, kc=kc, ps=ps: e.matmul(ps[0:32, 0:TG], g1[:, kc, 128:160], xm[:, kc, :], start=(kc == 0), stop=(kc == KC - 1)),
                     r=[lw_r, xm_r], w=[pr])
            tg1, tg1_r = lt.get()
            p.op("act", lambda e, ps=ps: e.activation(out=tg1[0:32, :], in_=ps[0:32, 0:TG], func=AF.Sigmoid), r=[pr], w=[tg1_r])
            for oc in range(KC):
                ps, pr = cx.bank()
                p.op("pe", lambda e, oc=oc, ps=ps: e.matmul(ps[:, 0:TG], g2a[:, oc * 128:(oc + 1) * 128], tg0[:, :], start=True, stop=False),
                     r=[lw_r, tg0_r], w=[pr])
                p.op("pe", lambda e, oc=oc, ps=ps: e.matmul(ps[:, 0:TG], g2b[:, oc * 128:(oc + 1) * 128], tg1[0:32, :], start=False, stop=True),
                     r=[lw_r, tg1_r], w=[pr])
                gt, gt_r = ob.get()
                p.op("act", lambda e, gt=gt, ps=ps: e.copy(out=gt[:], in_=ps[:, 0:TG]), r=[pr], w=[gt_r])
                out_dma(p, dd["g"][oc * 128:(oc + 1) * 128, cols], gt[:], gt_r, "pr_og")


def stage_post(p, cx, cs, T, layer, variant, x_in_d, x_out_d, vec, vec_r, vc, mod_sb, mod_r, wo_d, up_d, down_d, dd, final_g=None):
    nc = p.nc
    NTL = T // TG
    mc = layer * 48
    with p.scope() as es2:
        sb = p.sb
        gm1 = sb("po_gm1", [128, KC])
        gm1_r = p.res("po_gm1")
        p.op("dve", lambda e: e.tensor_scalar_add(out=gm1[:], in0=mod_sb[:, mc + 16:mc + 24], scalar1=1.0), r=[mod_r], w=[gm1_r])
        gm2 = sb("po_gm2", [128, 2 * KC])
        gm2_r = p.res("po_gm2")
        g0 = vc["norm_mlp_g"]
        p.op("dve", lambda e: e.scalar_tensor_tensor(out=gm2[:, 0:KC], in0=mod_sb[:, mc + 32:mc + 40], scalar=1.0, in1=vec[:, g0:g0 + KC],
                                                    op0=ALU.add, op1=ALU.mult), r=[vec_r, mod_r], w=[gm2_r])
        p.op("dve", lambda e: e.tensor_scalar_add(out=gm2[:, KC:2 * KC], in0=mod_sb[:, mc + 40:mc + 48], scalar1=1.0), r=[mod_r], w=[gm2_r])
        shift2 = mod_sb[:, mc + 24:mc + 32]
        st = mlp_state(p)
        xt = Rot(p, "po_x", [128, KC, TG], F32, 1)
        z = sb("po_z", [128, KC, TG], BF16)
        z_r = p.res("po_z")
        i1 = Rot(p, "po_i1", [128, TG], F32, 2)
        i3 = Rot(p, "po_i3", [128, TG], F32, 2)
        if variant == "rwkv":
            i2 = Rot(p, "po_i2", [128, TG], F32, 2)
            f1 = Rot(p, "po_f1", [128, TG], F32, 2)
        pw = ProjW(p, "po_w")
        xv = x_in_d.rearrange("(c p) t -> p c t", p=128)
        ov = x_out_d.rearrange("(c p) t -> p c t", p=128)
        for ti in range(NTL):
            cols = slice(ti * TG, (ti + 1) * TG)
            x_t, x_r = xt.get()
            p.dma(x_t[:], xv[:, :, cols], w=[x_r])
            for oc in range(KC):
                rows = slice(oc * 128, (oc + 1) * 128)
                if variant == "rwkv":
                    y, y_r = i1.get()
                    p.dma(y[:], dd["y"][rows, cols], w=[y_r])
                    bn, bn_r = i2.get()
                    p.dma(bn[:], dd["bonus"][rows, cols], w=[bn_r])
                    g, g_r = i3.get()
                    p.dma(g[:], dd["g"][rows, cols], w=[g_r])
                    ps, pr = head_sum(p, cx, cs, cs.bo64, y, y_r)
                    p.op("dve", lambda e, y=y, ps=ps: e.tensor_tensor(out=y[:], in0=y[:], in1=ps[:, 0:TG], op=ALU.subtract), r=[y_r, pr], w=[y_r])
                    sq, sq_r = f1.get()
                    p.op("act", lambda e, y=y, sq=sq: e.activation(out=sq[:], in_=y[:], func=AF.Square), r=[y_r], w=[sq_r])
                    ps2, pr2 = head_sum(p, cx, cs, cs.bo64, sq, sq_r)
                    p.op("act", lambda e, sq=sq, ps2=ps2: e.activation(out=sq[:], in_=ps2[:, 0:TG], func=AF.Sqrt, bias=cx.eps_t[:, 1:2], scale=1.0),
                         r=[pr2, cx.r_const], w=[sq_r])
                    p.op("dve", lambda e, sq=sq: e.reciprocal(out=sq[:], in_=sq[:]), r=[sq_r], w=[sq_r])
                    p.op("dve", lambda e, y=y, sq=sq: e.tensor_tensor(out=y[:], in0=y[:], in1=sq[:], op=ALU.mult), r=[y_r, sq_r], w=[y_r])
                    cw, cb = vc["lnw"] + oc, vc["lnb"] + oc
                    p.op("dve", lambda e, y=y, cw=cw, cb=cb: e.tensor_scalar(out=y[:], in0=y[:], scalar1=vec[:, cw:cw + 1], scalar2=vec[:, cb:cb + 1],
                                                                         op0=ALU.mult, op1=ALU.add), r=[y_r, vec_r], w=[y_r])
                    p.op("pool", lambda e, y=y, bn=bn: e.tensor_tensor(out=y[:], in0=y[:], in1=bn[:], op=ALU.add), r=[y_r, bn_r], w=[y_r])
                    p.op("pool", lambda e, y=y, g=g, oc=oc: e.tensor_tensor(out=z[:, oc, :], in0=y[:], in1=g[:], op=ALU.mult), r=[y_r, g_r], w=[z_r])
                else:
                    o, o_r = i1.get()
                    p.dma(o[:], dd["o"][rows, cols], w=[o_r])
                    g, g_r = i3.get()
                    p.dma(g[:], dd["sig"][rows, cols], w=[g_r])
                    p.op("pool", lambda e, o=o, g=g, oc=oc: e.tensor_tensor(out=z[:, oc, :], in0=o[:], in1=g[:], op=ALU.mult), r=[o_r, g_r], w=[z_r])
            for o2 in range(KC // 2):
                wb, wr = pw.load(wo_d, o2 * 256)
                for oi in range(2):
                    oc = o2 * 2 + oi
                    ps, pr = proj_mm(p, cx, wb, wr, oi * 128, z, z_r)
                    xs = x_t[:, oc, :]
                    p.op("dve", lambda e, ps=ps, xs=xs, oc=oc: e.scalar_tensor_tensor(out=xs, in0=ps[:, 0:TG], scalar=gm1[:, oc:oc + 1], in1=xs,
                                                                                  op0=ALU.mult, op1=ALU.add), r=[pr, gm1_r, x_r], w=[x_r])
            emit_mlp(p, cx, x_t, x_r, gm2, gm2_r, shift2, mod_r, up_d, down_d, st)
            if final_g is not None:
                emit_norm(p, cx, x_t, x_r, final_g, None, vec_r, x_t, x_r, st["scr"])
                out_dma(p, ov[:, :, cols], x_t[:], x_r, "po_out")
            else:
                out_dma(p, ov[:, :, cols], x_t[:], x_r, "po_out")


def stage_scan(p, cx, cs, T, dd):
    nc = p.nc
    NP4 = T // 512
    with p.scope() as es2:
        sb = p.sb
        mr = p.res("sc_masks")
        mU2 = sb("sc_mU2", [128, 256])
        mL = sb("sc_mL", [128, 128])
        p.op("pool", lambda e: e.affine_select(out=mU2[:, 0:128], in_=cs.ones_f[:], pattern=[[1, 128]], compare_op=ALU.is_ge, fill=0.0, base=-1, channel_multiplier=-1), r=[cs.r], w=[mr])
        p.op("pool", lambda e: e.affine_select(out=mU2[:, 128:256], in_=cs.ones_f[:], pattern=[[1, 128]], compare_op=ALU.is_ge, fill=0.0, base=0, channel_multiplier=-1), r=[cs.r], w=[mr])
        p.op("pool", lambda e: e.affine_select(out=mL[:], in_=cs.ones_f[:], pattern=[[-1, 128]], compare_op=ALU.is_ge, fill=0.0, base=-1, channel_multiplier=1), r=[cs.r], w=[mr])
        p.op("pool", lambda e: e.memset(mU2[0:64, 64:128], 0.0), w=[mr])
        p.op("pool", lambda e: e.memset(mU2[0:64, 192:256], 0.0), w=[mr])
        p.op("pool", lambda e: e.memset(mL[64:128, 0:64], 0.0), w=[mr])
        NSTREAM = 2

        def stream(sid, hps):
            names = ["r", "lw", "k", "v", "a", "b"]
            inp = {n: Rot(p, "sc%d_" % sid + "in_" + n, [128, 512], F32, 2) for n in names}
            cum = Rot(p, "sc%d_" % sid + "cum", [128, 128], F32, 2)
            cpv = Rot(p, "sc%d_" % sid + "cpv", [128, 128], F32, 2)
            eP = Rot(p, "sc%d_" % sid + "eP", [128, 128], F32, 3)
            eN = Rot(p, "sc%d_" % sid + "eN", [128, 128], F32, 2)
            eV = Rot(p, "sc%d_" % sid + "eV", [128, 128], F32, 2)
            AR = Rot(p, "sc%d_" % sid + "AR", [128, 256], F32, 2)
            bT = Rot(p, "sc%d_" % sid + "bT", [128, 128], F32, 2)
            kT = Rot(p, "sc%d_" % sid + "kT", [128, 128], F32, 2)
            PadA = Rot(p, "sc%d_" % sid + "PadA", [128, 4, 128], F32, 2)
            PadB = Rot(p, "sc%d_" % sid + "PadB", [128, 4, 128], F32, 2)
            PZA = Rot(p, "sc%d_" % sid + "PZA", [128, 2, 128], F32, 2)
            PZB = Rot(p, "sc%d_" % sid + "PZB", [128, 2, 128], F32, 2)
            for rot in (PadA, PadB, PZA, PZB):
                for t, r in zip(rot.t, rot.r):
                    p.op("pool", lambda e, t=t: e.memset(t[:], 0.0), w=[r])
            ZcA = Rot(p, "sc%d_" % sid + "ZcA", [128, 128], F32, 2)
            ZcB = Rot(p, "sc%d_" % sid + "ZcB", [128, 128], F32, 2)
            XP = Rot(p, "sc%d_" % sid + "XP", [128, 256], F32, 4)
            YP = Rot(p, "sc%d_" % sid + "YP", [128, 256], F32, 4)
            Xp = Rot(p, "sc%d_" % sid + "Xp", [128, 128], F32, 4)
            Lp = Rot(p, "sc%d_" % sid + "Lp", [128, 128], F32, 4)
            RH = Rot(p, "sc%d_" % sid + "RH", [128, 128], F32, 2)
            YH = Rot(p, "sc%d_" % sid + "YH", [128, 128], F32, 2)
            IG = Rot(p, "sc%d_" % sid + "IG", [128, 128], F32, 4)
            NN = Rot(p, "sc%d_" % sid + "NN", [128, 128], F32, 4)
            Zbd = sb("sc%d_" % sid + "Zbd", [128, 128])
            Zbd_r = p.res("sc%d_Zbd" % sid)
            yo = Rot(p, "sc%d_" % sid + "yo", [128, 512], F32, 2)
            ci = [0]

            def evac(out, in_, r, w):
                ci[0] += 1
                if ci[0] % 2 == 0:
                    p.op("act", lambda e: e.copy(out=out, in_=in_), r=r, w=w)
                else:
                    p.op("dve", lambda e: e.tensor_copy(out=out, in_=in_), r=r, w=w)

            for hp in hps:
                rows = slice(hp * 128, (hp + 1) * 128)
                p.op("pool", lambda e: e.memset(Zbd[:], 0.0), w=[Zbd_r])
                for p4 in range(NP4):
                    cols4 = slice(p4 * 512, (p4 + 1) * 512)
                    tin = {}
                    for n in names:
                        t, r = inp[n].get()
                        p.dma(t[:], dd[n][rows, cols4], w=[r])
                        tin[n] = (t, r)
                    yo_t, yo_r = yo.get()
                    for u in range(4):
                        c = slice(u * 128, (u + 1) * 128)
                        lw_t, lw_r = tin["lw"]
                        cum_t, cum_r = cum.get()
                        for ch in range(2):
                            cc = slice(u * 128 + ch * 64, u * 128 + ch * 64 + 64)
                            oc = slice(ch * 64, ch * 64 + 64)
                            p.op("dve", lambda e, cc=cc, oc=oc, cum_t=cum_t, lw_t=lw_t: e.tensor_tensor_scan(
                                out=cum_t[:, oc], data0=cs.ones_f[:, oc], data1=lw_t[:, cc], initial=0.0, op0=ALU.mult, op1=ALU.add),
                                r=[lw_r, cs.r], w=[cum_r])
                        cpv_t, cpv_r = cpv.get()
                        p.op("pool", lambda e, cpv_t=cpv_t, cum_t=cum_t, lw_t=lw_t, c=c: e.tensor_tensor(out=cpv_t[:], in0=cum_t[:], in1=lw_t[:, c], op=ALU.subtract),
                             r=[cum_r, lw_r], w=[cpv_r])
                        eP_t, eP_r = eP.get()
                        eN_t, eN_r = eN.get()
                        eV_t, eV_r = eV.get()
                        p.op("act", lambda e, eP_t=eP_t, cum_t=cum_t: e.activation(out=eP_t[:], in_=cum_t[:], func=AF.Exp), r=[cum_r], w=[eP_r])
                        p.op("act", lambda e, eN_t=eN_t, cum_t=cum_t: e.activation(out=eN_t[:], in_=cum_t[:], func=AF.Exp, scale=-1.0), r=[cum_r], w=[eN_r])
                        p.op("act", lambda e, eV_t=eV_t, cpv_t=cpv_t: e.activation(out=eV_t[:], in_=cpv_t[:], func=AF.Exp), r=[cpv_r], w=[eV_r])
                        AR_t, AR_r = AR.get()
                        bT_t, bT_r = bT.get()
                        kT_t, kT_r = kT.get()
                        a_t, a_r = tin["a"]
                        r_t, r_r = tin["r"]
                        b_t, b_r = tin["b"]
                        k_t, k_r = tin["k"]
                        v_t, v_r = tin["v"]
                        p.op("dve", lambda e, AR_t=AR_t, a_t=a_t, eV_t=eV_t, c=c: e.tensor_tensor(out=AR_t[:, 0:128], in0=a_t[:, c], in1=eV_t[:], op=ALU.mult), r=[a_r, eV_r], w=[AR_r])
                        p.op("dve", lambda e, AR_t=AR_t, r_t=r_t, eP_t=eP_t, c=c: e.tensor_tensor(out=AR_t[:, 128:256], in0=r_t[:, c], in1=eP_t[:], op=ALU.mult), r=[r_r, eP_r], w=[AR_r])
                        p.op("dve", lambda e, bT_t=bT_t, b_t=b_t, eN_t=eN_t, c=c: e.tensor_tensor(out=bT_t[:], in0=b_t[:, c], in1=eN_t[:], op=ALU.mult), r=[b_r, eN_r], w=[bT_r])
                        p.op("pool", lambda e, kT_t=kT_t, k_t=k_t, eN_t=eN_t, c=c: e.tensor_tensor(out=kT_t[:], in0=k_t[:, c], in1=eN_t[:], op=ALU.mult), r=[k_r, eN_r], w=[kT_r])
                        yield
                        ps, pr = cx.bank()
                        srcs = [(AR_t[:, 0:128], AR_r), (bT_t[:], bT_r), (kT_t[:], kT_r), (v_t[:, c], v_r)]
                        for j, (src, sr) in enumerate(srcs):
                            p.op("pe", lambda e, j=j, src=src, ps=ps: e.transpose(out=ps[:, j * 128:(j + 1) * 128], in_=src, identity=cs.ident[:]), r=[sr, cs.r], w=[pr])
                        PA, PA_r = PadA.get()
                        PB, PB_r = PadB.get()
                        psv = ps[:, 0:512].rearrange("p (j c) -> p j c", j=4)
                        p.op("act", lambda e, PA=PA, psv=psv: e.copy(out=PA[:, :, 0:64], in_=psv[:, :, 0:64]), r=[pr], w=[PA_r])
                        p.op("dve", lambda e, PB=PB, psv=psv: e.tensor_copy(out=PB[:, :, 64:128], in_=psv[:, :, 64:128]), r=[pr], w=[PB_r])
                        Zcs = [ZcA.get(), ZcB.get()]
                        p.op("act", lambda e, ps=ps: e.copy(out=Zcs[0][0][:, 0:64], in_=ps[:, 0:64]), r=[pr], w=[Zcs[0][1]])
                        p.op("dve", lambda e, ps=ps: e.tensor_copy(out=Zcs[1][0][:, 0:64], in_=ps[:, 64:128]), r=[pr], w=[Zcs[1][1]])
                        yield
                        PZ = [PZA.get(), PZB.get()]
                        Pad = [(PA, PA_r), (PB, PB_r)]
                        XPs, YPs = [], []
                        for h in range(2):
                            hs = slice(h * 64, (h + 1) * 64)
                            Zc_t, Zc_r = Zcs[h]
                            ps1, pr1 = cx.bank()
                            p.op("pe", lambda e, ps1=ps1, hs=hs: e.matmul(ps1[:, 0:256], bT_t[hs, :], AR_t[hs, :], start=True, stop=True), r=[bT_r, AR_r], w=[pr1])
                            XP_t, XP_r = XP.get()
                            p.op("dve", lambda e, XP_t=XP_t, ps1=ps1: e.tensor_tensor(out=XP_t[:], in0=ps1[:, 0:256], in1=mU2[:], op=ALU.mult), r=[pr1, mr], w=[XP_r])
                            ps2, pr2 = cx.bank()
                            p.op("pe", lambda e, ps2=ps2, hs=hs: e.matmul(ps2[:, 0:256], kT_t[hs, :], AR_t[hs, :], start=True, stop=True), r=[kT_r, AR_r], w=[pr2])
                            YP_t, YP_r = YP.get()
                            p.op("dve", lambda e, YP_t=YP_t, ps2=ps2: e.tensor_tensor(out=YP_t[:], in0=ps2[:, 0:256], in1=mU2[:], op=ALU.mult), r=[pr2, mr], w=[YP_r])
                            ps3, pr3 = cx.bank()
                            p.op("pe", lambda e, ps3=ps3, hs=hs: e.matmul(ps3[:, 0:128], AR_t[hs, 0:128], bT_t[hs, :], start=True, stop=True), r=[bT_r, AR_r], w=[pr3])
                            L_t, L_r = Lp.get()
                            p.op("dve", lambda e, L_t=L_t, ps3=ps3: e.tensor_tensor(out=L_t[:], in0=ps3[:, 0:128], in1=mL[:], op=ALU.mult), r=[pr3, mr], w=[L_r])
                            XPs.append((XP_t, XP_r))
                            YPs.append((YP_t, YP_r))
                            yield
                            Pd, Pd_r = Pad[h]
                            ps4, pr4 = cx.bank()
                            p.op("pe", lambda e, ps4=ps4, YP_t=YP_t, Pd=Pd, hs=hs: e.matmul(ps4[:, 0:64], YP_t[:, 0:128], Pd[:, 3, hs], start=True, stop=True), r=[YP_r, Pd_r], w=[pr4])
                            evac(Zc_t[:, 64:128], ps4[:, 0:64], [pr4], [Zc_r])
                            X_t, X_r = XP_t[:, 0:128], XP_r
                            Lc_t, Lc_r = L_t[:], L_r
                            PZ_t, PZ_r = PZ[h]
                            for j in range(6):
                                yield
                                psa, pra = cx.bank()
                                p.op("pe", lambda e, psa=psa, X_t=X_t, Zc_t=Zc_t: e.matmul(psa[:, 0:128], X_t, Zc_t[:], start=True, stop=True), r=[X_r, Zc_r], w=[pra])
                                if j < 5:
                                    psx, prx = cx.bank()
                                    p.op("pe", lambda e, psx=psx, X_t=X_t, Lc_t=Lc_t: e.matmul(psx[:, 0:128], Lc_t, X_t, start=True, stop=True), r=[X_r, Lc_r], w=[prx])
                                    psl, prl = cx.bank()
                                    p.op("pe", lambda e, psl=psl, X_t=X_t, Lc_t=Lc_t: e.matmul(psl[:, 0:128], X_t, Lc_t, start=True, stop=True), r=[X_r, Lc_r], w=[prl])
                                    p.op("dve", lambda e, psa=psa, Zc_t=Zc_t: e.tensor_tensor(out=Zc_t[:], in0=psa[:, 0:128], in1=Zc_t[:], op=ALU.add), r=[pra, Zc_r], w=[Zc_r])
                                    Xn, Xn_r = Xp.get()
                                    Ln, Ln_r = Lp.get()
                                    evac(Xn[:], psx[:, 0:128], [prx], [Xn_r])
                                    evac(Ln[:], psl[:, 0:128], [prl], [Ln_r])
                                    X_t, X_r, Lc_t, Lc_r = Xn[:], Xn_r, Ln[:], Ln_r
                                else:
                                    p.op("dve", lambda e, psa=psa, Zc_t=Zc_t, PZ_t=PZ_t, hs=hs: e.tensor_tensor(
                                        out=PZ_t[:, :, hs], in0=psa[:, 0:128].rearrange("p (j c) -> p j c", j=2),
                                        in1=Zc_t[:].rearrange("p (j c) -> p j c", j=2), op=ALU.add), r=[pra, Zc_r], w=[PZ_r])
                        yield
                        psr_, prr = cx.bank()
                        for h in range(2):
                            p.op("pe", lambda e, h=h, psr_=psr_: e.matmul(psr_[:, 0:128], PZ[h][0][:, 0, :], XPs[h][0][:, 128:256], start=(h == 0), stop=(h == 1)),
                                 r=[PZ[h][1], XPs[h][1]], w=[prr])
                        RH_t, RH_r = RH.get()
                        p.op("dve", lambda e, RH_t=RH_t, psr_=psr_, AR_t=AR_t: e.tensor_tensor(out=RH_t[:], in0=psr_[:, 0:128], in1=AR_t[:, 128:256], op=ALU.add), r=[prr, AR_r], w=[RH_r])
                        psy, pry = cx.bank()
                        for h in range(2):
                            p.op("pe", lambda e, h=h, psy=psy: e.matmul(psy[:, 0:128], PZ[h][0][:, 1, :], XPs[h][0][:, 128:256], start=(h == 0), stop=False),
                                 r=[PZ[h][1], XPs[h][1]], w=[pry])
                            p.op("pe", lambda e, h=h, psy=psy: e.matmul(psy[:, 0:128], Pad[h][0][:, 3, :], YPs[h][0][:, 128:256], start=False, stop=(h == 1)),
                                 r=[Pad[h][1], YPs[h][1]], w=[pry])
                        YH_t, YH_r = YH.get()
                        evac(YH_t[:], psy[:, 0:128], [pry], [YH_r])
                        yield
                        IGs, NNs = [], []
                        for ch in range(2):
                            tk = slice(ch * 64, ch * 64 + 64)
                            psg, prg = cx.bank()
                            for h in range(2):
                                p.op("pe", lambda e, h=h, psg=psg, tk=tk: e.matmul(psg[:, 0:128], PZ[h][0][tk, 0, :], Pad[h][0][tk, 1, :], start=(h == 0), stop=(h == 1)),
                                     r=[PZ[h][1], Pad[h][1]], w=[prg])
                            IG_t, IG_r = IG.get()
                            p.op("dve", lambda e, IG_t=IG_t, psg=psg: e.tensor_tensor(out=IG_t[:], in0=psg[:, 0:128], in1=cs.ident[:], op=ALU.add), r=[prg, cs.r], w=[IG_r])
                            psn, prn = cx.bank()
                            for h in range(2):
                                p.op("pe", lambda e, h=h, psn=psn, tk=tk: e.matmul(psn[:, 0:128], Pad[h][0][tk, 1, :], PZ[h][0][tk, 1, :], start=(h == 0), stop=False),
                                     r=[PZ[h][1], Pad[h][1]], w=[prn])
                                p.op("pe", lambda e, h=h, psn=psn, tk=tk: e.matmul(psn[:, 0:128], Pad[h][0][tk, 2, :], Pad[h][0][tk, 3, :], start=False, stop=(h == 1)),
                                     r=[Pad[h][1]], w=[prn])
                            NN_t, NN_r = NN.get()
                            wc = eP_t[:, ch * 64 + 63:ch * 64 + 64]
                            p.op("dve", lambda e, NN_t=NN_t, psn=psn, wc=wc: e.tensor_scalar_mul(out=NN_t[:], in0=psn[:, 0:128], scalar1=wc), r=[prn, eP_r], w=[NN_r])
                            IGs.append((IG_t, IG_r))
                            NNs.append((NN_t, NN_r, wc))
                        for ch in range(2):
                            yield
                            tcol = slice(ch * 64, ch * 64 + 64)
                            ocol = slice(u * 128 + ch * 64, u * 128 + ch * 64 + 64)
                            psq, prq = cx.bank()
                            p.op("pe", lambda e, psq=psq, tcol=tcol, RH_t=RH_t: e.matmul(psq[:, 0:64], Zbd[:], RH_t[:, tcol], start=True, stop=True), r=[Zbd_r, RH_r], w=[prq])
                            p.op("dve", lambda e, psq=psq, tcol=tcol, ocol=ocol, YH_t=YH_t, yo_t=yo_t: e.tensor_tensor(out=yo_t[:, ocol], in0=psq[:, 0:64], in1=YH_t[:, tcol], op=ALU.add),
                                 r=[prq, YH_r], w=[yo_r])
                            psz, prz = cx.bank()
                            IG_t, IG_r = IGs[ch]
                            NN_t, NN_r, wc = NNs[ch]
                            p.op("pe", lambda e, psz=psz, IG_t=IG_t: e.matmul(psz[:, 0:128], IG_t[:], Zbd[:], start=True, stop=True), r=[IG_r, Zbd_r], w=[prz])
                            p.op("dve", lambda e, psz=psz, NN_t=NN_t, wc=wc: e.scalar_tensor_tensor(out=Zbd[:], in0=psz[:, 0:128], scalar=wc, in1=NN_t[:], op0=ALU.mult, op1=ALU.add),
                                 r=[prz, NN_r, eP_r], w=[Zbd_r])
                    out_dma(p, dd["y"][rows, cols4], yo_t[:], yo_r, "sc_y")


        gens = [stream(s, list(range(s, H // 2, NSTREAM))) for s in range(NSTREAM)]
        while gens:
            for g in list(gens):
                try:
                    next(g)
                except StopIteration:
                    gens.remove(g)


def stage_kvq(p, cx, cs, T, kind, layer, x_d, vec, vec_r, g_col, mod_sb, mod_r, W_d, gain_sb, gain_r, dd, fb_sb=None, fb_r=None, Wf_d=None):
    nc = p.nc
    NTL = T // TG
    with p.scope() as es2:
        sb = p.sb
        gm = sb("kq_gm", [128, KC])
        gm_r = p.res("kq_gm")
        if kind == "kv":
            sh_c, sc_c = 192, 200
        else:
            sh_c, sc_c = layer * 48, layer * 48 + 8
        p.op("dve", lambda e: e.scalar_tensor_tensor(out=gm[:], in0=mod_sb[:, sc_c:sc_c + 8], scalar=1.0, in1=vec[:, g_col:g_col + KC],
                                                    op0=ALU.add, op1=ALU.mult), r=[vec_r, mod_r], w=[gm_r])
        shift = mod_sb[:, sh_c:sh_c + 8]
        xt = Rot(p, "kq_x", [128, KC, TG], F32, 2)
        h = sb("kq_h", [128, KC, TG], BF16)
        h_r = p.res("kq_h")
        scr = norm_scratch(p, "kqn")
        pw = ProjW(p, "kq_w")
        f1 = Rot(p, "kq_f1", [128, TG], F32, 3)
        ob = Rot(p, "kq_ob", [128, TG], F32, 4)
        xv = x_d.rearrange("(c p) t -> p c t", p=128)
        if kind == "kv":
            wf_s = sb("kq_wfs", [128, KC, 16])
            wf = sb("kq_wf", [128, KC, 16], BF16)
            wf_r = p.res("kq_wf")
            p.dma(wf_s[:], Wf_d.rearrange("(kc p) m -> p kc m", p=128)[:, :, 2 * D:2 * D + 16], w=[wf_r])
            p.op("pool", lambda e: e.tensor_copy(out=wf[:], in_=wf_s[:]), r=[wf_r], w=[wf_r])
            lf = Rot(p, "kq_lf", [16, TG], F32, 2)
        gscale = 1.0 if kind == "kv" else 0.125
        for ti in range(NTL):
            cols = slice(ti * TG, (ti + 1) * TG)
            x_t, x_r = xt.get()
            p.dma(x_t[:], xv[:, :, cols], w=[x_r])
            emit_norm(p, cx, x_t, x_r, gm, shift, gm_r, h, h_r, scr)
            for o2 in range(2 * KC // 2):
                wb, wr = pw.load(W_d, o2 * 256)
                for oi in range(2):
                    oc = o2 * 2 + oi
                    ps, pr = proj_mm(p, cx, wb, wr, oi * 128, h, h_r)
                    if oc < KC:
                        sq, sq_r = f1.get()
                        p.op("act", lambda e, sq=sq, ps=ps: e.activation(out=sq[:], in_=ps[:, 0:TG], func=AF.Square), r=[pr], w=[sq_r])
                        ps2, pr2 = head_sum(p, cx, cs, cs.bo64, sq, sq_r)
                        p.op("act", lambda e, sq=sq, ps2=ps2: e.activation(out=sq[:], in_=ps2[:, 0:TG], func=AF.Sqrt, bias=cx.eps_t[:, 0:1], scale=1.0),
                             r=[pr2, cx.r_const], w=[sq_r])
                        p.op("dve", lambda e, sq=sq: e.reciprocal(out=sq[:], in_=sq[:]), r=[sq_r], w=[sq_r])
                        o, o_r = ob.get()
                        p.op("dve", lambda e, o=o, ps=ps, sq=sq: e.scalar_tensor_tensor(out=o[:], in0=ps[:, 0:TG], scalar=gain_sb, in1=sq[:], op0=ALU.mult, op1=ALU.mult),
                             r=[pr, sq_r, gain_r], w=[o_r])
                        if gscale != 1.0:
                            p.op("pool", lambda e, o=o: e.tensor_scalar_mul(out=o[:], in0=o[:], scalar1=gscale), r=[o_r], w=[o_r])
                        dst = dd["ksh"] if kind == "kv" else dd["q"]
                        out_dma(p, dst[oc * 128:(oc + 1) * 128, cols], o[:], o_r, "kq_o")
                    else:
                        o, o_r = ob.get()
                        if kind == "kv":
                            p.op("act", lambda e, o=o, ps=ps: e.copy(out=o[:], in_=ps[:, 0:TG]), r=[pr], w=[o_r])
                            out_dma(p, dd["vsh"][(oc - KC) * 128:(oc - KC + 1) * 128, cols], o[:], o_r, "kq_o")
                        else:
                            p.op("act", lambda e, o=o, ps=ps: e.activation(out=o[:], in_=ps[:, 0:TG], func=AF.Sigmoid), r=[pr], w=[o_r])
                            out_dma(p, dd["sig"][(oc - KC) * 128:(oc - KC + 1) * 128, cols], o[:], o_r, "kq_o")
            if kind == "kv":
                ps, pr = cx.bank()
                for kc in range(KC):
                    p.op("pe", lambda e, kc=kc, ps=ps: e.matmul(ps[0:16, 0:TG], wf[:, kc, :], h[:, kc, :], start=(kc == 0), stop=(kc == KC - 1)), r=[wf_r, h_r], w=[pr])
                l, l_r = lf.get()
                p.op("act", lambda e, l=l, ps=ps: e.activation(out=l[:], in_=ps[0:16, 0:TG], func=AF.Exp, bias=fb_sb, scale=-1.0), r=[pr, fb_r], w=[l_r])
                p.op("act", lambda e, l=l: e.activation(out=l[:], in_=l[:], func=AF.Ln, bias=cx.eps_t[0:16, 2:3], scale=1.0), r=[l_r, cx.r_const], w=[l_r])
                p.op("pool", lambda e, l=l: e.tensor_scalar_mul(out=l[:], in0=l[:], scalar1=-1.0), r=[l_r], w=[l_r])
                out_dma(p, dd["logf"][:, cols], l[:], l_r, "kq_lf")


def stage_fprep(p, cx, T, dd):
    nc = p.nc
    FB = min(2048, T)
    with p.scope() as es2:
        sb = p.sb
        ones = sb("fp_ones", [16, FB])
        cr = p.res("fp_c")
        p.op("pool", lambda e: e.memset(ones[:], 1.0), w=[cr])
        lf = Rot(p, "fp_lf", [16, FB], F32, 2)
        F = Rot(p, "fp_F", [16, FB], F32, 2)
        r1 = Rot(p, "fp_r1", [16, FB], F32, 2)
        pp = Rot(p, "fp_pp", [16, 3, FB], BF16, 2)
        pn = Rot(p, "fp_pn", [16, 3, FB], BF16, 2)
        prev = None
        for bi in range(T // FB):
            cols = slice(bi * FB, (bi + 1) * FB)
            l, l_r = lf.get()
            p.dma(l[:], dd["logf"][:, cols], w=[l_r])
            f, f_r = F.get()
            init = 0.0 if prev is None else prev[0][:, FB - 1:FB]
            rr = [l_r, cr] + ([] if prev is None else [prev[1]])
            p.op("dve", lambda e, f=f, l=l, init=init: e.tensor_tensor_scan(out=f[:], data0=ones[:], data1=l[:], initial=init, op0=ALU.mult, op1=ALU.add), r=rr, w=[f_r])
            prev = (f, f_r)
            q, q_r = pp.get()
            n, n_r = pn.get()
            r_, r_r = r1.get()
            p.op("act", lambda e, q=q, f=f: e.copy(out=q[:, 0, :], in_=f[:]), r=[f_r], w=[q_r])
            p.op("dve", lambda e, r_=r_, f=f, q=q: e.tensor_tensor(out=r_[:], in0=f[:], in1=q[:, 0, :], op=ALU.subtract), r=[f_r, q_r], w=[r_r])
            p.op("act", lambda e, q=q, r_=r_: e.copy(out=q[:, 1, :], in_=r_[:]), r=[r_r], w=[q_r])
            p.op("dve", lambda e, r_=r_, q=q: e.tensor_tensor(out=r_[:], in0=r_[:], in1=q[:, 1, :], op=ALU.subtract), r=[r_r, q_r], w=[r_r])
            p.op("act", lambda e, q=q, r_=r_: e.copy(out=q[:, 2, :], in_=r_[:]), r=[r_r], w=[q_r])
            p.op("pool", lambda e, n=n, q=q: e.tensor_scalar_mul(out=n[:], in0=q[:], scalar1=-1.0), r=[q_r], w=[n_r])
            out_dma(p, dd["fpos"][:, :, cols], q[:], q_r, "fp_p")
            out_dma(p, dd["fneg"][:, :, cols], n[:], n_r, "fp_n")


def stage_attn(p, cx, cs, T, dd):
    nc = p.nc
    NQT = T // 512
    NKB = T // 128
    LB = min(2048, T)
    with p.scope() as es2:
        sb = p.sb
        cx.rot = [0, 1, 2, 3, 4, 5]
        obank = [(cx.ps[6], cx.psr[6]), (cx.ps[7], cx.psr[7])]
        mr = p.res("at_masks")
        onesb = sb("at_onesb", [128, 512], BF16)
        p.op("pool", lambda e: e.memset(onesb[:], 1.0), w=[mr])
        masks = []
        for o in range(4):
            m = sb("at_mask%d" % o, [128, 512], BF16)
            p.op("pool", lambda e, m=m, o=o: e.affine_select(out=m[:], in_=onesb[:], pattern=[[1, 512]], compare_op=ALU.is_ge, fill=0.0,
                                                          base=-128 * o, channel_multiplier=-1), r=[mr], w=[mr])
            masks.append(m)
        KA = Rot(p, "at_KA", [70, T], BF16, 2)
        QA = Rot(p, "at_QA", [70, T], BF16, 2)
        VP = Rot(p, "at_VP", [128, NKB, 65], BF16, 2)
        for rot in (KA, QA):
            for t, r in zip(rot.t, rot.r):
                p.op("pool", lambda e, t=t: e.memset(t[64:70, :], 1.0), w=[r])
        for t, r in zip(VP.t, VP.r):
            p.op("pool", lambda e, t=t: e.memset(t[:, :, 64:65], 1.0), w=[r])
        stg = Rot(p, "at_stg", [64, LB], F32, 3)
        pT = Rot(p, "at_pT", [128, 512], BF16, 6)
        clp = Rot(p, "at_clp", [128, 512], F32, 2)
        osb = Rot(p, "at_osb", [65, 512], F32, 2)
        rc = Rot(p, "at_rc", [65, 512], F32, 2)
        oo = Rot(p, "at_oo", [64, 512], F32, 2)
        ci = [0]
        DEPTH = 3
        pend = []
        nqt_done = [0]

        def flush_one():
            ob_, ob_r, VP_t, VP_r, kb, nkb, pt, pt_r, fin = pend.pop(0)
            p.op("pe", lambda e, ob_=ob_, kb=kb, pt=pt, nkb=nkb, VP_t=VP_t: e.matmul(ob_[0:65, 0:512], VP_t[:, kb, :], pt[:], start=(kb == 0), stop=(kb == nkb - 1)),
                 r=[VP_r, pt_r], w=[ob_r])
            if fin is not None:
                rows, qc = fin
                os_, os_r = osb.get()
                p.op("act", lambda e, os_=os_, ob_=ob_: e.copy(out=os_[:], in_=ob_[0:65, 0:512]), r=[ob_r], w=[os_r])
                rc_, rc_r = rc.get()
                p.op("dve", lambda e, rc_=rc_, os_=os_: e.reciprocal(out=rc_[64:65, :], in_=os_[64:65, :]), r=[os_r], w=[rc_r])
                ps, pr = cx.bank()
                p.op("pe", lambda e, ps=ps, rc_=rc_: e.matmul(ps[0:64, 0:512], cs.ones_f[64:65, 0:64], rc_[64:65, :], start=True, stop=True), r=[rc_r, cs.r], w=[pr])
                o_, o_r = oo.get()
                p.op("dve", lambda e, o_=o_, os_=os_, ps=ps: e.tensor_tensor(out=o_[:], in0=os_[0:64, :], in1=ps[0:64, 0:512], op=ALU.mult), r=[os_r, pr], w=[o_r])
                out_dma(p, dd["o"][rows, qc], o_[:], o_r, "at_o")

        for h in range(H):
            rows = slice(h * 64, (h + 1) * 64)
            KA_t, KA_r = KA.get()
            QA_t, QA_r = QA.get()
            VP_t, VP_r = VP.get()
            for bi in range(T // LB):
                cols = slice(bi * LB, (bi + 1) * LB)
                for src, dst, dst_r in ((dd["ksh"], KA_t, KA_r), (dd["q"], QA_t, QA_r)):
                    s, s_r = stg.get()
                    p.dma(s[:], src[rows, cols], w=[s_r])
                    ci[0] += 1
                    if ci[0] % 2 == 0:
                        p.op("act", lambda e, s=s, dst=dst, cols=cols: e.copy(out=dst[0:64, cols], in_=s[:]), r=[s_r], w=[dst_r])
                    else:
                        p.op("pool", lambda e, s=s, dst=dst, cols=cols: e.tensor_copy(out=dst[0:64, cols], in_=s[:]), r=[s_r], w=[dst_r])
                s, s_r = stg.get()
                p.dma(s[:], dd["vsh"][rows, cols], w=[s_r])
                GS = min(8, LB // 128)
                for g8 in range(LB // 128 // GS):
                    ps, pr = cx.bank()
                    for j in range(GS):
                        kb = g8 * GS + j
                        p.op("pe", lambda e, ps=ps, j=j, kb=kb, s=s: e.transpose(out=ps[:, j * 64:(j + 1) * 64], in_=s[:, kb * 128:(kb + 1) * 128], identity=cs.ident[0:64, 0:64]),
                             r=[s_r, cs.r], w=[pr])
                    kb0 = bi * (LB // 128) + g8 * GS
                    p.op("dve", lambda e, ps=ps, kb0=kb0, VP_t=VP_t, GS=GS: e.tensor_copy(out=VP_t[:, kb0:kb0 + GS, 0:64], in_=ps[:, 0:GS * 64].rearrange("p (j c) -> p j c", j=GS)),
                         r=[pr], w=[VP_r])
            p.dma(QA_t[64:67, :], dd["fpos"][h], w=[QA_r])
            p.dma(KA_t[67:70, :], dd["fneg"][h], w=[KA_r])
            for qt in range(NQT):
                qc = slice(qt * 512, (qt + 1) * 512)
                ob_, ob_r = obank[nqt_done[0] % 2]
                nqt_done[0] += 1
                nkb = 4 * (qt + 1)
                for kb in range(nkb):
                    ps, pr = cx.bank()
                    p.op("pe", lambda e, ps=ps, kb=kb, qc=qc, KA_t=KA_t, QA_t=QA_t: e.matmul(ps[:, 0:512], KA_t[:, kb * 128:(kb + 1) * 128], QA_t[:, qc], start=True, stop=True), r=[KA_r, QA_r], w=[pr])
                    pt, pt_r = pT.get()
                    if kb >= 4 * qt:
                        cl, cl_r = clp.get()
                        p.op("dve", lambda e, ps=ps, cl=cl: e.tensor_scalar_min(out=cl[:], in0=ps[:, 0:512], scalar1=20.0), r=[pr], w=[cl_r])
                        p.op("act", lambda e, cl=cl, pt=pt: e.activation(out=pt[:], in_=cl[:], func=AF.Exp), r=[cl_r], w=[pt_r])
                        mk_ = masks[kb - 4 * qt]
                        p.op("pool", lambda e, pt=pt, mk_=mk_: e.tensor_tensor(out=pt[:], in0=pt[:], in1=mk_[:], op=ALU.mult), r=[pt_r, mr], w=[pt_r])
                    else:
                        p.op("act", lambda e, ps=ps, pt=pt: e.activation(out=pt[:], in_=ps[:, 0:512], func=AF.Exp), r=[pr], w=[pt_r])
                    last = (kb == nkb - 1)
                    pend.append((ob_, ob_r, VP_t, VP_r, kb, nkb, pt, pt_r, (rows, qc) if last else None))
                    while len(pend) > DEPTH:
                        flush_one()
        while pend:
            flush_one()
        cx.rot = list(range(8))


def stage_wcast(p, cx, jobs):
    with p.scope() as es2:
        engs = ["act", "dve", "pool"]
        ci = 0
        stgs = {}
        for src, dst, bc in jobs:
            kci = src.shape[0] // 128
            key = (kci, bc)
            if key not in stgs:
                stgs[key] = (Rot(p, "wc_s%d_%d" % key, [128, kci, bc], F32, 2), Rot(p, "wc_b%d_%d" % key, [128, kci, bc], BF16, 2))
            srot, brot = stgs[key]
            v = src.rearrange("(kc p) m -> p kc m", p=128)
            for j in range(src.shape[1] // bc):
                s_, s_r = srot.get()
                b_, b_r = brot.get()
                p.dma(s_[:], v[:, :, j * bc:(j + 1) * bc], w=[s_r])
                copy_op(p, engs[ci % 3], b_[:], s_[:], r=[s_r], w=[b_r])
                ci += 1
                p.dma(dst[j], b_[:], r=[b_r], w=[p.res("wc_o")], key="o_" + b_r.name)


NV = 2 * len(RW_VECS) * KC + 2 * 2 * KC + 2 * KC + 8
W_SHAPES = {
    "mod_w": [4, D, 6 * D], "mlp_up": [4, D, DFF], "mlp_down": [4, DFF, D],
    "rw_wr": [2, D, D], "rw_wk": [2, D, D], "rw_wv": [2, D, D], "rw_wo": [2, D, D],
    "rw_w1": [2, D, 64], "rw_w2": [2, 64, D], "rw_a1": [2, D, 64], "rw_a2": [2, 64, D],
    "rw_g1": [2, D, 160], "rw_g2": [2, 160, D], "rw_v1": [1, D, 32], "rw_v2": [1, 32, D],
    "kv_mod_w": [D, 2 * D], "kv_w": [D, 2 * D + 16], "fx_wqg": [2, D, 2 * D], "fx_wo": [2, D, D],
}


def build_program(T, stages=None, dump=()):
    nc = bass.Bass("TRN2", target_bir_lowering=False)
    ein = lambda n, s, d=F32: nc.dram_tensor(n, list(s), d, kind="ExternalInput").ap()
    xT = ein("xT", [D, T])
    cT = ein("cT", [128, KC])
    modb = ein("modb", [128, NMODC])
    vecs_d = ein("vecs", [128, NV])
    nfb_d = ein("nfb", [16, 1])
    Wd = {n: ein(n, s) for n, s in W_SHAPES.items()}
    outT = nc.dram_tensor("outT", [D, T], F32, kind="ExternalOutput").ap()
    _cnt = [0]

    def idr(n, s, d=F32):
        _cnt[0] += 1
        return nc.dram_tensor("scr%02d_%s" % (_cnt[0], n), list(s), d).ap()
    dd = {n: idr(n, [D, T]) for n in ["xa", "xb", "r", "lw", "k", "v", "a", "b", "g", "bonus", "vfirst", "y", "ksh", "vsh", "q", "sig", "o"]}
    dd["logf"] = idr("logf", [16, T])
    dd["fpos"] = idr("fpos", [16, 3, T], BF16)
    dd["fneg"] = idr("fneg", [16, 3, T], BF16)
    dump_out = {n: nc.dram_tensor("dump_" + n, list(dd[n].shape), dd[n].dtype, kind="ExternalOutput").ap() for n in dump}
    with ExitStack() as es:
        p = Prog(nc, es)
        cx = Ctx(p)
        cs = Consts(p, cx)
        vec = p.sb("vecs_sb", [128, NV])
        vec_r = p.res("vecs")
        p.dma(vec[:], vecs_d, w=[vec_r])
        nfb = p.sb("nfb_sb", [16, 1])
        nfb_r = p.res("nfb")
        p.dma(nfb[:], nfb_d, w=[nfb_r])
        p.op("pool", lambda e: e.tensor_scalar_mul(out=nfb[:], in0=nfb[:], scalar1=-1.0), r=[nfb_r], w=[nfb_r])
        mod_sb = p.sb("mod_sb", [128, NMODC])
        mod_r = p.res("mod")
        stage_mod(p, cx, cT, Wd["mod_w"], Wd["kv_mod_w"], modb, mod_sb, mod_r)
        jobs = []
        Wb = {}

        def mkb(name, src, bc):
            kin, mm_ = src.shape
            t = nc.dram_tensor("wbf_" + name, [mm_ // bc, 128, kin // 128, bc], BF16).ap()
            jobs.append((src, t, bc))
            return t
        for i in range(4):
            Wb["up%d" % i] = mkb("up%d" % i, Wd["mlp_up"][i], 256)
            Wb["dn%d" % i] = mkb("dn%d" % i, Wd["mlp_down"][i], 128)
        for i in range(2):
            for n_ in ("wr", "wk", "wv", "wo"):
                Wb["%s%d" % (n_, i)] = mkb("%s%d" % (n_, i), Wd["rw_" + n_][i], 256)
            Wb["qg%d" % i] = mkb("qg%d" % i, Wd["fx_wqg"][i], 256)
            Wb["fo%d" % i] = mkb("fo%d" % i, Wd["fx_wo"][i], 256)
        Wb["kv"] = mkb("kv", Wd["kv_w"][:, 0:2 * D], 256)
        p.barrier()
        stage_wcast(p, cx, jobs)
        nrw = len(RW_VECS) * KC
        vcs = []
        for i in range(2):
            vcs.append({n: i * nrw + j * KC for j, n in enumerate(RW_VECS)})
        for i in range(2):
            vcs.append({"norm_mix_g": 2 * nrw + i * 2 * KC, "norm_mlp_g": 2 * nrw + i * 2 * KC + KC})
        c_kvg = 2 * nrw + 4 * KC
        c_fin = c_kvg + KC
        c_gain = c_fin + KC
        xcur, xnext = xT, dd["xa"]
        nstage = 0

        def want(name):
            return stages is None or name in stages

        for i in range(2):
            W = {"wr": Wb["wr%d" % i], "wk": Wb["wk%d" % i], "wv": Wb["wv%d" % i], "w1": Wd["rw_w1"][i], "w2": Wd["rw_w2"][i],
                 "a1": Wd["rw_a1"][i], "a2": Wd["rw_a2"][i], "g1": Wd["rw_g1"][i], "g2": Wd["rw_g2"][i]}
            if i > 0:
                W["v1"] = Wd["rw_v1"][0]
                W["v2"] = Wd["rw_v2"][0]
            if want("pre%d" % i):
                p.barrier()
                stage_pre_rwkv(p, cx, cs, T, i, xcur, vec, vec_r, vcs[i], mod_sb, mod_r, W, dd)
            if want("scan%d" % i):
                p.barrier()
                stage_scan(p, cx, cs, T, dd)
            if want("post%d" % i):
                p.barrier()
                stage_post(p, cx, cs, T, i, "rwkv", xcur, xnext, vec, vec_r, vcs[i], mod_sb, mod_r, Wb["wo%d" % i], Wb["up%d" % i], Wb["dn%d" % i], dd)
                xcur, xnext = xnext, (dd["xb"] if xnext is dd["xa"] else dd["xa"])
        if want("kv"):
            p.barrier()
            stage_kvq(p, cx, cs, T, "kv", 0, xcur, vec, vec_r, c_kvg, mod_sb, mod_r, Wb["kv"], vec[:, c_gain:c_gain + 1], vec_r, dd, fb_sb=nfb[:, 0:1], fb_r=nfb_r, Wf_d=Wd["kv_w"])
            p.barrier()
            stage_fprep(p, cx, T, dd)
        for j in range(2):
            i = 2 + j
            if want("preq%d" % i):
                p.barrier()
                stage_kvq(p, cx, cs, T, "q", i, xcur, vec, vec_r, vcs[i]["norm_mix_g"], mod_sb, mod_r, Wb["qg%d" % j], vec[:, c_gain + 1 + j:c_gain + 2 + j], vec_r, dd)
            if want("attn%d" % i):
                p.barrier()
                stage_attn(p, cx, cs, T, dd)
            if want("post%d" % i):
                p.barrier()
                last = (i == 3)
                stage_post(p, cx, cs, T, i, "fox", xcur, outT if last else xnext, vec, vec_r, vcs[i], mod_sb, mod_r, Wb["fo%d" % j], Wb["up%d" % i], Wb["dn%d" % i], dd,
                           final_g=vec[:, c_fin:c_fin + KC] if last else None)
                xcur, xnext = xnext, (dd["xb"] if xnext is dd["xa"] else dd["xa"])
        if dump:
            p.barrier()
            for n in dump:
                p.dma(dump_out[n], dd[n], w=[p.res("dump_" + n)])
        p.emit()
        stats = p.stats
    return nc, stats


def fm(v):
    return np.ascontiguousarray(np.asarray(v, np.float32).reshape(-1, 128).T)


def host_inputs(inp, b, T):
    vec_parts = []
    for i in range(2):
        tab = {"norm_mix_g": inp["norm_mix_g"][i], "norm_mlp_g": inp["norm_mlp_g"][i], "w0": inp["rw_w0"][i], "a0": inp["rw_a0"][i],
               "v0": inp["rw_v0"][0], "kk": inp["rw_kk"][i], "ka": inp["rw_ka"][i], "rk": inp["rw_rk"][i].reshape(-1),
               "lnw": inp["rw_lnw"][i], "lnb": inp["rw_lnb"][i]}
        for j in range(6):
            tab["mu%d" % j] = inp["rw_mu"][i, j]
        for n in RW_VECS:
            vec_parts.append(fm(tab[n]))
    for i in (2, 3):
        vec_parts.append(fm(inp["norm_mix_g"][i]))
        vec_parts.append(fm(inp["norm_mlp_g"][i]))
    vec_parts.append(fm(inp["kv_norm_g"]))
    vec_parts.append(fm(inp["final_g"]))
    gains = np.zeros((128, 8), np.float32)
    gains[:, 0] = np.tile(np.asarray(inp["kv_kg"], np.float32), 2)
    gains[:, 1] = np.tile(np.asarray(inp["fx_qg"][0], np.float32), 2)
    gains[:, 2] = np.tile(np.asarray(inp["fx_qg"][1], np.float32), 2)
    vec_parts.append(gains)
    vecs = np.ascontiguousarray(np.concatenate(vec_parts, axis=1))
    assert vecs.shape == (128, NV), vecs.shape
    modb = np.concatenate([fm(inp["mod_b"][i]) for i in range(4)] + [fm(inp["kv_mod_b"])], axis=1)
    m = {
        "xT": np.ascontiguousarray(np.asarray(inp["x"][b, :T], np.float32).T),
        "cT": fm(inp["c"][b]),
        "modb": np.ascontiguousarray(modb),
        "vecs": vecs,
        "nfb": np.ascontiguousarray(np.asarray(inp["kv_fb"], np.float32).reshape(16, 1)),
    }
    for n in W_SHAPES:
        m[n] = np.ascontiguousarray(np.asarray(inp[n], np.float32))
    return m


def kernel(**inputs):
    T = 8192
    nc, _ = build_program(T)
    inp = {k: np.asarray(v) for k, v in inputs.items()}
    in_maps = [host_inputs(inp, b, T) for b in range(2)]
    res = run_bass_kernel_spmd(nc, in_maps, core_ids=[0, 1])
    out = np.stack([np.asarray(res.results[b]["outT"]).T for b in range(2)])
    return np.ascontiguousarray(out.astype(np.float32))
```

```python
import numpy as np
from contextlib import ExitStack
import concourse.bass as bass
import concourse.mybir as mybir
from concourse.bass_utils import run_bass_kernel_spmd

F32 = mybir.dt.float32
BF16 = mybir.dt.bfloat16
ALU = mybir.AluOpType
AF = mybir.ActivationFunctionType
AX = mybir.AxisListType

SAME_ENGINE_SYNC = True


import types as _types


def _snapshot(fn):
    cl = fn.__closure__
    if not cl:
        return fn
    cells = []
    for c in cl:
        try:
            cells.append(_types.CellType(c.cell_contents))
        except ValueError:
            cells.append(c)
    g = _types.FunctionType(fn.__code__, fn.__globals__, fn.__name__, fn.__defaults__, tuple(cells))
    g.__kwdefaults__ = fn.__kwdefaults__
    return g


class Res:
    __slots__ = ("name", "w", "r", "excl")

    def __init__(self, name):
        self.name = name
        self.w = None
        self.r = []
        self.excl = False


class Op:
    __slots__ = ("eng", "fn", "deps", "is_dma", "sem", "count", "signal", "idx", "line", "alldeps")


class Prog:
    ENGS = ("pe", "act", "dve", "pool", "sp")

    def __init__(self, nc, es):
        self.nc = nc
        self.es = es
        self.ops = []
        self.dma_keys = {}
        self.nres = 0
        self._uid = 0
        self.fence = {}
        self.scopes = []
        self.barriers = []

    def sb(self, name, shape, dt=F32):
        es = self.scopes[-1] if self.scopes else self.es
        self._uid += 1
        return es.enter_context(self.nc.sbuf_tensor("%s_u%d" % (name, self._uid), list(shape), dt))

    def scope(self):
        import contextlib

        @contextlib.contextmanager
        def cm():
            with ExitStack() as es2:
                self.scopes.append(es2)
                try:
                    yield es2
                finally:
                    self.scopes.pop()
        return cm()

    def ps(self, name, shape, dt=F32):
        return self.es.enter_context(self.nc.psum_tensor(name, list(shape), dt))

    def res(self, name=None):
        self.nres += 1
        return Res(name or ("r%d" % self.nres))

    def resl(self, n, name="r"):
        return [self.res("%s%d" % (name, i)) for i in range(n)]

    def op(self, eng, fn, r=(), w=(), dma_key=None):
        o = Op()
        o.eng = eng
        o.fn = _snapshot(fn)
        o.is_dma = dma_key is not None
        o.signal = False
        o.sem = dma_key
        o.count = 0
        o.idx = len(self.ops)
        import sys as _sys
        o.line = _sys._getframe(1).f_lineno
        ex = [x for x in r if x.excl]
        if ex:
            w = list(w) + [x for x in ex if x not in w]
            r = [x for x in r if not x.excl]
        deps = {}
        for x in r:
            if x.w is not None:
                deps[x.w.idx] = (x.w, "raw")
        for x in w:
            if x.w is not None:
                deps[x.w.idx] = (x.w, "waw")
            for rd in x.r:
                if rd.idx not in deps:
                    deps[rd.idx] = (rd, "war")
        fd = []
        if self.fence.get(eng):
            for d in self.fence[eng]:
                if d.is_dma or d.eng != eng or eng == "pool":
                    fd.append(d)
            self.fence[eng] = None
        for d, kind in deps.values():
            if d is o:
                continue
            if d.is_dma:
                fd.append(d)
            elif d.eng != eng:
                fd.append(d)
            else:
                if o.is_dma or eng == "pool" or (SAME_ENGINE_SYNC and kind != "war" and eng != "pe"):
                    fd.append(d)
        o.deps = fd
        o.alldeps = [(d.idx, d.eng, d.line, k) for d, k in deps.values()]
        for d in fd:
            d.signal = True
        for x in r:
            x.r.append(o)
        for x in w:
            x.w = o
            x.r = []
        self.ops.append(o)
        return o

    def barrier(self):
        self.barriers.append(len(self.ops))
        last = {}
        for o in self.ops:
            if o.is_dma:
                last[("dma", o.sem)] = o
            else:
                last[o.eng] = o
        ops = list(last.values())
        for e in self.ENGS:
            self.fence[e] = list(ops)

    def dma(self, out, in_, r=(), w=(), key=None, q="sp"):
        if key is None:
            key = w[0].name
        return self.op(q, lambda e, out=out, in_=in_: e.dma_start(out=out, in_=in_), r=r, w=w, dma_key=key)

    def emit(self):
        nc = self.nc
        es = self.es
        ROT = 60000
        eng_sems = {e: [] for e in self.ENGS}
        dma_sem = {}
        dma_cnt = {}
        dma_uses = {}
        cnt = {e: 0 for e in self.ENGS}
        nsem = 0
        all_dma_sems = []
        final_cnt = {}
        free_sems = []
        bset = set(self.barriers)
        for o in self.ops:
            if o.idx in bset:
                for k in list(dma_sem.keys()):
                    free_sems.append((dma_sem.pop(k), dma_cnt.pop(k)))
            if o.is_dma:
                k = o.sem
                if k in dma_sem and dma_cnt[k] > 60000:
                    dma_sem.pop(k)
                    dma_cnt.pop(k)
                if k not in dma_sem:
                    while free_sems and free_sems[0][1] > 50000:
                        free_sems.pop(0)
                    if free_sems:
                        dma_sem[k], dma_cnt[k] = free_sems.pop(0)
                    else:
                        dma_sem[k] = es.enter_context(nc.semaphore("dsem_%d" % nsem))
                        dma_cnt[k] = 0
                        nsem += 1
                        all_dma_sems.append(dma_sem[k])
                dma_cnt[k] += 16
                final_cnt[dma_sem[k].num] = (dma_sem[k], dma_cnt[k])
                o.sem = dma_sem[k]
                o.count = dma_cnt[k]
            elif o.signal:
                n = cnt[o.eng]
                cnt[o.eng] += 1
                si = n // ROT
                if si >= len(eng_sems[o.eng]):
                    eng_sems[o.eng].append(es.enter_context(nc.semaphore("sem_%s%d" % (o.eng, si))))
                    nsem += 1
                o.sem = eng_sems[o.eng][si]
                o.count = n % ROT + 1
        self.stats = dict(cnt)
        self.stats["n_ops"] = len(self.ops)
        self.stats["n_sems"] = nsem
        per = {e: [o for o in self.ops if o.eng == e] for e in self.ENGS}
        final_dma = list(final_cnt.values())

        def run(e, ename):
            seen = {}
            nw = 0
            for o in per[ename]:
                need = {}
                for d in o.deps:
                    s = d.sem
                    if need.get(s.num, (None, 0))[1] < d.count:
                        need[s.num] = (s, d.count)
                for s, c in need.values():
                    if seen.get(s.num, 0) < c:
                        e.wait_ge(s, c)
                        seen[s.num] = c
                        nw += 1
                ins = o.fn(e)
                if o.is_dma:
                    ins.then_inc(o.sem, 16)
                elif o.signal:
                    ins.then_inc(o.sem, 1)
            if ename == "sp":
                for s, c in final_dma:
                    e.wait_ge(s, c)
            self.stats["waits_" + ename] = nw

        with nc.Block() as block:
            block.sync(lambda e: run(e, "sp"))
            block.tensor(lambda e: run(e, "pe"))
            block.scalar(lambda e: run(e, "act"))
            block.vector(lambda e: run(e, "dve"))
            block.gpsimd(lambda e: run(e, "pool"))


D = 1024
KC = 8
NT = 2048
TG = 512
NTG = NT // TG
DFF = 4096
FC = DFF // 128
NORM_EPS = 1e-6
GN_EPS = 64e-5


class Ctx:
    def __init__(self, p):
        self.p = p
        nc = p.nc
        self.ps = [p.ps("psb%d" % i, [128, 512], F32) for i in range(8)]
        self.psr = [p.res("psb%d" % i) for i in range(8)]
        for r_ in self.psr:
            r_.excl = True
        self.psi = 0
        self.rot = list(range(8))
        self.ones_bf = p.sb("ones_bf", [128, 128], BF16)
        self.r_const = p.res("consts")
        self.eps_t = p.sb("eps_t", [128, 4], F32)
        self.ones_f32 = p.sb("ones_f32c", [128, 128], F32)
        p.op("pool", lambda e: e.memset(self.ones_f32[:], 1.0), w=[self.r_const])
        p.op("pool", lambda e: e.memset(self.ones_bf[:], 1.0), w=[self.r_const])
        p.op("pool", lambda e: e.memset(self.eps_t[:, 0:1], NORM_EPS), w=[self.r_const])
        p.op("pool", lambda e: e.memset(self.eps_t[:, 1:2], GN_EPS), w=[self.r_const])
        p.op("pool", lambda e: e.memset(self.eps_t[:, 2:3], 1.0), w=[self.r_const])
        p.op("pool", lambda e: e.memset(self.eps_t[:, 3:4], 0.0), w=[self.r_const])

    def bank(self):
        self.psi = (self.psi + 1) % len(self.rot)
        i = self.rot[self.psi]
        return self.ps[i], self.psr[i]


class WStream:
    def __init__(self, p, name, shape, nbuf=2, cast_engs=("pool",), direct=False):
        self.p = p
        self.shape = shape
        self.nbuf = nbuf
        if not direct:
            self.stg = [p.sb("%s_stg%d" % (name, i), shape, F32) for i in range(nbuf)]
            self.stg_r = [p.res("%s_stg%d" % (name, i)) for i in range(nbuf)]
        self.bf = [p.sb("%s_bf%d" % (name, i), shape, BF16) for i in range(nbuf)]
        self.bf_r = [p.res("%s_bf%d" % (name, i)) for i in range(nbuf)]
        self.i = 0
        self.cast_engs = cast_engs
        self.ci = 0

    def load(self, src_ap, sl=None):
        p = self.p
        i = self.i
        self.i = (i + 1) % self.nbuf
        bf, br = self.bf[i], self.bf_r[i]
        idx = sl if sl is not None else tuple(slice(None) for _ in self.shape)
        if src_ap.dtype == BF16:
            p.dma(bf[idx], src_ap, w=[br])
            return bf, br
        stg, sr = self.stg[i], self.stg_r[i]
        p.dma(stg[idx], src_ap, w=[sr])
        ce = self.cast_engs[self.ci % len(self.cast_engs)]
        self.ci += 1
        copy_op(p, ce, bf[idx], stg[idx], r=[sr], w=[br])
        return bf, br


def copy_op(p, eng, out, in_, r, w):
    if eng == "act":
        return p.op("act", lambda e: e.copy(out=out, in_=in_), r=r, w=w)
    return p.op(eng, lambda e: e.tensor_copy(out=out, in_=in_), r=r, w=w)


def emit_norm(p, cx, x_sb, x_r, gmul, shift, vec_r, h_out, h_r, scr):
    ts = slice(0, TG)
    sq, sq_r = scr["sq"]
    rstd, rstd_r = scr["rstd"]
    tmp, tmp_r = scr["tmp"]
    p.op("act", lambda e: e.activation(out=sq[:], in_=x_sb[:, :, ts], func=AF.Square), r=[x_r], w=[sq_r])
    ps, pr = cx.bank()
    for kc in range(KC):
        p.op("pe", lambda e, kc=kc: e.matmul(ps[:, 0:TG], cx.ones_bf[:], sq[:, kc, :], start=(kc == 0), stop=(kc == KC - 1)),
             r=[sq_r, cx.r_const], w=[pr])
    p.op("act", lambda e: e.activation(out=rstd[:], in_=ps[:, 0:TG], func=AF.Sqrt, bias=cx.eps_t[:, 0:1], scale=1.0 / D),
         r=[pr, cx.r_const], w=[rstd_r])
    p.op("dve", lambda e: e.reciprocal(out=rstd[:], in_=rstd[:]), r=[rstd_r], w=[rstd_r])
    for kc in range(KC):
        if shift is None:
            p.op("dve", lambda e, kc=kc: e.scalar_tensor_tensor(out=h_out[:, kc, :], in0=x_sb[:, kc, ts], scalar=gmul[:, kc:kc + 1],
                                                               in1=rstd[:], op0=ALU.mult, op1=ALU.mult),
                 r=[x_r, rstd_r, vec_r], w=[h_r])
        else:
            t2, t2r = tmp[kc % 2], tmp_r[kc % 2]
            p.op("dve", lambda e, kc=kc, t2=t2: e.scalar_tensor_tensor(out=t2[:], in0=x_sb[:, kc, ts], scalar=gmul[:, kc:kc + 1],
                                                                      in1=rstd[:], op0=ALU.mult, op1=ALU.mult),
                 r=[x_r, rstd_r, vec_r], w=[t2r])
            p.op("act", lambda e, kc=kc, t2=t2: e.activation(out=h_out[:, kc, :], in_=t2[:], func=AF.Identity,
                                                            bias=shift[:, kc:kc + 1], scale=1.0),
                 r=[t2r, vec_r], w=[h_r])


def norm_scratch(p, name):
    return {
        "sq": (p.sb(name + "_sq", [128, KC, TG], BF16), p.res(name + "_sq")),
        "rstd": (p.sb(name + "_rstd", [128, TG], F32), p.res(name + "_rstd")),
        "tmp": ([p.sb(name + "_tmp%d" % i, [128, TG], F32) for i in range(2)], [p.res(name + "_tmp%d" % i) for i in range(2)]),
    }


def emit_mlp(p, cx, xt, x_r, gm, gm_r, shift, vec_r, up_d, down_d, st):
    h, h_r, a, a_r, relu, relu_r, wu, wd, scr = st["h"], st["h_r"], st["a"], st["a_r"], st["relu"], st["relu_r"], st["wu"], st["wd"], st["scr"]
    emit_norm(p, cx, xt, x_r, gm[:, 0:KC], shift, vec_r, h, h_r, scr)
    for f2 in range(FC // 2):
        wb, wr = wu.load(up_d[f2])
        for fi in range(2):
            fc = f2 * 2 + fi
            ps, pr = cx.bank()
            for kc in range(KC):
                p.op("pe", lambda e, kc=kc, ps=ps, wb=wb, fi=fi: e.matmul(ps[:, 0:TG], wb[:, kc, fi * 128:(fi + 1) * 128], h[:, kc, :],
                                                                      start=(kc == 0), stop=(kc == KC - 1)),
                     r=[wr, h_r], w=[pr])
            ri = st["ri"]
            st["ri"] += 1
            rl, rr = relu[ri % 2], relu_r[ri % 2]
            p.op("act", lambda e, ps=ps, rl=rl: e.activation(out=rl[:], in_=ps[:, 0:TG], func=AF.Relu), r=[pr], w=[rr])
            p.op("dve", lambda e, ps=ps, rl=rl, fc=fc: e.scalar_tensor_tensor(
                out=a[:, fc, :], in0=ps[:, 0:TG], scalar=0.0, in1=rl[:], op0=ALU.max, op1=ALU.mult),
                r=[pr, rr], w=[a_r])
    for oc in range(KC):
        ps, pr = cx.bank()
        wb, wr = wd.load(down_d[oc])
        for fc in range(FC):
            p.op("pe", lambda e, fc=fc, ps=ps, wb=wb: e.matmul(ps[:, 0:TG], wb[:, fc, :], a[:, fc, :],
                                                             start=(fc == 0), stop=(fc == FC - 1)),
                 r=[wr, a_r], w=[pr])
        xs = xt[:, oc, :]
        p.op("dve", lambda e, ps=ps, xs=xs, oc=oc: e.scalar_tensor_tensor(out=xs, in0=ps[:, 0:TG], scalar=gm[:, KC + oc:KC + oc + 1],
                                                                      in1=xs, op0=ALU.mult, op1=ALU.add),
             r=[pr, gm_r, x_r], w=[x_r])


def mlp_state(p):
    return {
        "h": p.sb("mlp_h", [128, KC, TG], BF16), "h_r": p.res("mlp_h"),
        "a": p.sb("mlp_a", [128, FC, TG], BF16), "a_r": p.res("mlp_a"),
        "relu": [p.sb("mlp_relu%d" % i, [128, TG], F32) for i in range(2)], "relu_r": p.resl(2, "mlp_relu"),
        "wu": WStream(p, "wup", [128, KC, 256], nbuf=3, direct=True),
        "wd": WStream(p, "wdn", [128, FC, 128], nbuf=2, direct=True),
        "scr": norm_scratch(p, "mlpn"), "ri": 0,
    }


def mod_gm(p, name, vec, vec_r, col_g, col_sc, col_gt):
    gm = p.sb(name, [128, 2 * KC], F32)
    gm_r = p.res(name)
    p.op("dve", lambda e: e.scalar_tensor_tensor(out=gm[:, 0:KC], in0=vec[:, col_sc:col_sc + KC], scalar=1.0,
                                                in1=vec[:, col_g:col_g + KC], op0=ALU.add, op1=ALU.mult), r=[vec_r], w=[gm_r])
    if col_gt is not None:
        p.op("dve", lambda e: e.tensor_scalar_add(out=gm[:, KC:2 * KC], in0=vec[:, col_gt:col_gt + KC], scalar1=1.0), r=[vec_r], w=[gm_r])
    return gm, gm_r


import os
PRE_STOP = int(os.environ.get('PRE_STOP', '99'))
H = 16
HD = 64
NMODC = 208
LW_SCALE = -0.6065306597126334


def vec_cols(names):
    return {n: i * KC for i, n in enumerate(names)}


RW_VECS = ["norm_mix_g", "norm_mlp_g", "mu0", "mu1", "mu2", "mu3", "mu4", "mu5", "w0", "a0", "v0", "kk", "ka", "rk", "lnw", "lnb"]
FX_VECS = ["norm_mix_g", "norm_mlp_g"]


class Consts:
    def __init__(self, p, cx):
        self.r = p.res("consts2")
        self.bo1 = p.sb("bo1", [128, 128], F32)
        self.bo64 = p.sb("bo64", [128, 128], F32)
        self.ident = p.sb("ident", [128, 128], F32)
        self.ones_f = p.sb("ones_f", [128, 128], F32)
        for t, v in ((self.bo1, 1.0), (self.bo64, 1.0 / 64)):
            p.op("pool", lambda e, t=t: e.memset(t[:], 0.0), w=[self.r])
            p.op("pool", lambda e, t=t, v=v: e.memset(t[0:64, 0:64], v), w=[self.r])
            p.op("pool", lambda e, t=t, v=v: e.memset(t[64:128, 64:128], v), w=[self.r])
        p.op("pool", lambda e: e.memset(self.ones_f[:], 1.0), w=[self.r])
        p.op("pool", lambda e: e.affine_select(out=self.ident[:], in_=self.ones_f[:], pattern=[[1, 128]], compare_op=ALU.is_equal,
                                               fill=0.0, base=0, channel_multiplier=-1), r=[self.r], w=[self.r])


def stage_mod(p, cx, cT_d, modw_d, kvmodw_d, modb_d, mod_sb, mod_r):
    with p.scope() as es2:
        nc = p.nc
        sb = p.sb
        cT = sb("mod_cT", [128, KC])
        ca = sb("mod_ca", [128, KC, 128])
        bias = sb("mod_bias", [128, NMODC])
        r_c = p.res("mod_c")
        r_b = p.res("mod_b")
        p.dma(cT[:], cT_d, w=[r_c])
        p.dma(bias[:], modb_d, w=[r_b])
        sig = sb("mod_sig", [128, KC])
        p.op("act", lambda e: e.activation(out=sig[:], in_=cT[:], func=AF.Sigmoid), r=[r_c], w=[r_c])
        p.op("dve", lambda e: e.tensor_tensor(out=sig[:], in0=cT[:], in1=sig[:], op=ALU.mult), r=[r_c], w=[r_c])
        for kc in range(KC):
            p.op("dve", lambda e, kc=kc: e.tensor_scalar_mul(out=ca[:, kc, :], in0=cx_ones(cx), scalar1=sig[:, kc:kc + 1]), r=[r_c, cx.r_const], w=[r_c])
        wst = [sb("mod_w%d" % i, [128, KC, 512]) for i in range(2)]
        wr = p.resl(2, "mod_w")
        blocks = []
        for i in range(4):
            v = modw_d[i].rearrange("(kc p) m -> p kc m", p=128)
            for j in range(12):
                blocks.append((v[:, :, j * 512:(j + 1) * 512], i * 48 + j * 4))
        v = kvmodw_d.rearrange("(kc p) m -> p kc m", p=128)
        for j in range(4):
            blocks.append((v[:, :, j * 512:(j + 1) * 512], 192 + j * 4))
        for bi, (src, c0) in enumerate(blocks):
            w, r = wst[bi % 2], wr[bi % 2]
            p.dma(w[:], src, w=[r])
            ps, pr = cx.bank()
            for j in range(4):
                for kc in range(KC):
                    p.op("pe", lambda e, w=w, j=j, kc=kc, ps=ps: e.matmul(ps[:, j * 128:(j + 1) * 128], w[:, kc, j * 128:(j + 1) * 128], ca[:, kc, :],
                                                                       start=(kc == 0), stop=(kc == KC - 1)),
                         r=[r, r_c], w=[pr])
            psv = ps[:, 0:512].rearrange("p (j c) -> p j c", j=4)
            p.op("dve", lambda e, psv=psv, c0=c0: e.tensor_tensor(out=mod_sb[:, c0:c0 + 4], in0=psv[:, :, 0], in1=bias[:, c0:c0 + 4], op=ALU.add), r=[pr, r_b], w=[mod_r])


def cx_ones(cx):
    return cx.ones_f32[:]


class ProjW:
    def __init__(self, p, name):
        self.ws = WStream(p, name, [128, KC, 256], nbuf=3, direct=True)
        self.p = p

    def load(self, W_d, c0, n=256):
        return self.ws.load(W_d[c0 // 256])


def proj_mm(p, cx, wb, wr, off, h, h_r, ncols=128):
    ps, pr = cx.bank()
    for kc in range(KC):
        p.op("pe", lambda e, kc=kc: e.matmul(ps[0:ncols, 0:TG], wb[:, kc, off:off + ncols], h[:, kc, :], start=(kc == 0), stop=(kc == KC - 1)),
             r=[wr, h_r], w=[pr])
    return ps, pr


def head_sum(p, cx, cs, mat, src, src_r):
    ps, pr = cx.bank()
    p.op("pe", lambda e: e.matmul(ps[:, 0:TG], mat[:], src[:], start=True, stop=True), r=[src_r, cs.r], w=[pr])
    return ps, pr


class Rot:
    def __init__(self, p, name, shape, dt, n):
        self.t = [p.sb("%s%d" % (name, i), shape, dt) for i in range(n)]
        self.r = [p.res("%s%d" % (name, i)) for i in range(n)]
        self.i = 0

    def get(self):
        i = self.i
        self.i = (i + 1) % len(self.t)
        return self.t[i], self.r[i]


OUT_Q = os.environ.get("OUT_Q", "sp")


def out_dma(p, dst, src, src_r, key, dst_r=None):
    p.dma(dst, src, r=[src_r], w=[dst_r if dst_r is not None else p.res("o_" + key)], key="o_" + src_r.name, q=OUT_Q)


def stage_pre_rwkv(p, cx, cs, T, layer, x_d, vec, vec_r, vc, mod_sb, mod_r, W, dd):
    nc = p.nc
    NTL = T // TG
    mc = layer * 48
    with p.scope() as es2:
        sb = p.sb
        gm = sb("pr_gm", [128, 2 * KC])
        gm_r = p.res("pr_gm")
        g0 = vc["norm_mix_g"]
        p.op("dve", lambda e: e.scalar_tensor_tensor(out=gm[:, 0:KC], in0=mod_sb[:, mc + 8:mc + 16], scalar=1.0, in1=vec[:, g0:g0 + KC],
                                                    op0=ALU.add, op1=ALU.mult), r=[vec_r, mod_r], w=[gm_r])
        ka0 = vc["ka"]
        p.op("dve", lambda e: e.tensor_scalar(out=gm[:, KC:2 * KC], in0=vec[:, ka0:ka0 + KC], scalar1=-1.0, scalar2=1.0, op0=ALU.mult, op1=ALU.add),
             r=[vec_r], w=[gm_r])
        shift = mod_sb[:, mc:mc + 8]
        lw_r = p.res("pr_lora")
        specs = [("a1", W["a1"].rearrange("(kc p) m -> p kc m", p=128), [128, KC, 64]),
                 ("w1", W["w1"].rearrange("(kc p) m -> p kc m", p=128), [128, KC, 64]),
                 ("g1", W["g1"].rearrange("(kc p) m -> p kc m", p=128), [128, KC, 160]),
                 ("a2", W["a2"], [64, D]), ("w2", W["w2"], [64, D]),
                 ("g2a", W["g2"][0:128, :], [128, D]), ("g2b", W["g2"][128:160, :], [32, D])]
        if layer > 0:
            specs += [("v1", W["v1"].rearrange("(kc p) m -> p kc m", p=128), [128, KC, 32]), ("v2", W["v2"], [32, D])]
        lt_ = {n: sb("prb_" + n, shp, BF16) for n, _, shp in specs}
        with p.scope() as stg_es:
            for n, src, shp in specs:
                st = p.sb("prs_" + n, shp)
                rr = p.res("prs_" + n)
                p.dma(st[:], src, w=[rr])
                p.op("pool", lambda e, n=n, st=st: e.tensor_copy(out=lt_[n][:], in_=st[:]), r=[rr], w=[lw_r])
        a1, w1, g1, a2, w2, g2a, g2b = (lt_[n] for n in ("a1", "w1", "g1", "a2", "w2", "g2a", "g2b"))
        if layer > 0:
            v1, v2 = lt_["v1"], lt_["v2"]
        p.barrier()
        xt = Rot(p, "pr_x", [128, KC, TG], F32, 1)
        hh = sb("pr_h", [128, KC, TG + 1])
        hh_r = p.res("pr_h")
        xx = sb("pr_xx", [128, KC, TG], BF16)
        xx_r = p.res("pr_xx")
        xj = Rot(p, "pr_xj", [128, KC, TG], BF16, 1)
        a_t = sb("pr_a", [128, KC, TG])
        a_r = p.res("pr_a")
        k_t = sb("pr_k", [128, KC, TG])
        k_r = p.res("pr_k")
        v_t = sb("pr_v", [128, KC, TG])
        v_r = p.res("pr_v")
        scr = norm_scratch(p, "prn")
        pw = ProjW(p, "pr_w")
        f1 = Rot(p, "pr_f1", [128, TG], F32, 2)
        f2 = Rot(p, "pr_f2", [128, TG], F32, 2)
        ob = Rot(p, "pr_ob", [128, TG], F32, 3)
        obn = Rot(p, "pr_obn", [128, TG], F32, int(os.environ.get("OBN", "2")))
        if os.environ.get("PRE_INIT"):
            for t_, r_ in zip(obn.t, obn.r):
                p.op("pool", lambda e, t_=t_: e.memset(t_[:], 0.0), w=[r_])
        lt = Rot(p, "pr_lt", [128, TG], BF16, 2)
        xv = x_d.rearrange("(c p) t -> p c t", p=128)
        p.op("pool", lambda e: e.memset(hh[:, :, 0:1], 0.0), w=[hh_r])
        if os.environ.get("PRE_PAD"):
            dr = p.res("padr")
            for i_ in range(int(os.environ["PRE_PAD"])):
                p.op("dve", lambda e: e.tensor_scalar_mul(out=xx[:, 0, 0:8], in0=xx[:, 0, 0:8], scalar1=1.0), r=[dr], w=[dr])

        def mix_dve(j):
            t, r = xj.get()
            m0 = vc["mu%d" % j]
            for kc in range(KC):
                p.op("dve", lambda e, kc=kc: e.scalar_tensor_tensor(
                    out=t[:, kc, :], in0=xx[:, kc, :], scalar=vec[:, m0 + kc:m0 + kc + 1], in1=hh[:, kc, 1:TG + 1], op0=ALU.mult, op1=ALU.add),
                    r=[xx_r, hh_r, vec_r], w=[r])
            return t, r

        def lora1(xm, xm_r, w1t, R, func):
            ps, pr = cx.bank()
            for kc in range(KC):
                p.op("pe", lambda e, kc=kc: e.matmul(ps[0:R, 0:TG], w1t[:, kc, 0:R], xm[:, kc, :], start=(kc == 0), stop=(kc == KC - 1)),
                     r=[lw_r, xm_r], w=[pr])
            t, r = lt.get()
            p.op("act", lambda e: e.activation(out=t[0:R, :], in_=ps[0:R, 0:TG], func=func), r=[pr], w=[r])
            return t, r

        for ti in range(NTL):
            cols = slice(ti * TG, (ti + 1) * TG)
            x_t, x_r = xt.get()
            p.dma(x_t[:], xv[:, :, cols], w=[x_r])
            if ti > 0:
                p.op("pool", lambda e: e.tensor_copy(out=hh[:, :, 0:1], in_=hh[:, :, TG:TG + 1]), r=[hh_r], w=[hh_r])
            emit_norm(p, cx, x_t, x_r, gm[:, 0:KC], shift, gm_r, hh[:, :, 1:TG + 1], hh_r, scr)
            p.op("dve", lambda e: e.tensor_tensor(out=xx[:], in0=hh[:, :, 0:TG], in1=hh[:, :, 1:TG + 1], op=ALU.subtract), r=[hh_r], w=[xx_r])
            if PRE_STOP <= 0:
                continue
            xm, xm_r = mix_dve(4)
            ta, ta_r = lora1(xm, xm_r, a1, 64, AF.Identity)
            for oc in range(KC):
                ps, pr = cx.bank()
                p.op("pe", lambda e, oc=oc, ps=ps: e.matmul(ps[:, 0:TG], a2[:, oc * 128:(oc + 1) * 128], ta[0:64, :], start=True, stop=True),
                     r=[lw_r, ta_r], w=[pr])
                c0 = vc["a0"] + oc
                p.op("act", lambda e, oc=oc, ps=ps, c0=c0: e.activation(out=a_t[:, oc, :], in_=ps[:, 0:TG], func=AF.Sigmoid, bias=vec[:, c0:c0 + 1], scale=1.0),
                     r=[pr, vec_r], w=[a_r])
            if PRE_STOP <= 1:
                continue
            xm, xm_r = mix_dve(2)
            for o2 in range(KC // 2):
                wb, wr = pw.load(W["wk"], o2 * 256)
                for oi in range(2):
                    oc = o2 * 2 + oi
                    ps, pr = proj_mm(p, cx, wb, wr, oi * 128, xm, xm_r)
                    kkr, kkr_r = f1.get()
                    ck = vc["kk"] + oc
                    p.op("dve", lambda e, ps=ps, kkr=kkr, ck=ck: e.tensor_scalar_mul(out=kkr[:], in0=ps[:, 0:TG], scalar1=vec[:, ck:ck + 1]), r=[pr, vec_r], w=[kkr_r])
                    sq, sq_r = f2.get()
                    p.op("act", lambda e, kkr=kkr, sq=sq: e.activation(out=sq[:], in_=kkr[:], func=AF.Square), r=[kkr_r], w=[sq_r])
                    ps2, pr2 = head_sum(p, cx, cs, cs.bo1, sq, sq_r)
                    rn, rn_r = f2.get()
                    p.op("act", lambda e, ps2=ps2, rn=rn: e.activation(out=rn[:], in_=ps2[:, 0:TG], func=AF.Sqrt), r=[pr2], w=[rn_r])
                    p.op("dve", lambda e, rn=rn: e.tensor_scalar_max(out=rn[:], in0=rn[:], scalar1=1e-12), r=[rn_r], w=[rn_r])
                    p.op("dve", lambda e, rn=rn: e.reciprocal(out=rn[:], in_=rn[:]), r=[rn_r], w=[rn_r])
                    nk, nk_r = ob.get()
                    p.op("dve", lambda e, nk=nk, kkr=kkr, rn=rn: e.scalar_tensor_tensor(out=nk[:], in0=kkr[:], scalar=-1.0, in1=rn[:], op0=ALU.mult, op1=ALU.mult),
                         r=[kkr_r, rn_r], w=[nk_r])
                    out_dma(p, dd["a"][oc * 128:(oc + 1) * 128, cols], nk[:], nk_r, "pr_oa")
                    bt, bt_r = ob.get()
                    p.op("dve", lambda e, bt=bt, nk=nk, oc=oc: e.scalar_tensor_tensor(out=bt[:], in0=nk[:], scalar=-1.0, in1=a_t[:, oc, :], op0=ALU.mult, op1=ALU.mult),
                         r=[nk_r, a_r], w=[bt_r])
                    out_dma(p, dd["b"][oc * 128:(oc + 1) * 128, cols], bt[:], bt_r, "pr_ob")
                    tm, tm_r = f1.get()
                    cka = vc["ka"] + oc
                    p.op("dve", lambda e, tm=tm, oc=oc, cka=cka: e.tensor_scalar(out=tm[:], in0=a_t[:, oc, :], scalar1=vec[:, cka:cka + 1],
                                                                             scalar2=gm[:, KC + oc:KC + oc + 1], op0=ALU.mult, op1=ALU.add),
                         r=[a_r, vec_r, gm_r], w=[tm_r])
                    p.op("dve", lambda e, tm=tm, ps=ps, oc=oc: e.tensor_tensor(out=k_t[:, oc, :], in0=ps[:, 0:TG], in1=tm[:], op=ALU.mult),
                         r=[pr, tm_r], w=[k_r])
            out_dma(p, dd["k"].rearrange("(c p) t -> p c t", p=128)[:, :, cols], k_t[:], k_r, "pr_ok")
            if PRE_STOP <= 2:
                continue
            xm, xm_r = mix_dve(3)
            if layer > 0:
                tv, tv_r = lora1(xm, xm_r, v1, 32, AF.Identity)
            for o2 in range(KC // 2):
                wb, wr = pw.load(W["wv"], o2 * 256)
                for oi in range(2):
                    oc = o2 * 2 + oi
                    ps, pr = proj_mm(p, cx, wb, wr, oi * 128, xm, xm_r)
                    if layer == 0:
                        p.op("act", lambda e, ps=ps, oc=oc: e.copy(out=v_t[:, oc, :], in_=ps[:, 0:TG]), r=[pr], w=[v_r])
                    else:
                        ps2, pr2 = cx.bank()
                        p.op("pe", lambda e, oc=oc, ps2=ps2: e.matmul(ps2[:, 0:TG], v2[:, oc * 128:(oc + 1) * 128], tv[0:32, :], start=True, stop=True),
                             r=[lw_r, tv_r], w=[pr2])
                        gt, gt_r = f1.get()
                        c0 = vc["v0"] + oc
                        p.op("act", lambda e, gt=gt, ps2=ps2, c0=c0: e.activation(out=gt[:], in_=ps2[:, 0:TG], func=AF.Sigmoid, bias=vec[:, c0:c0 + 1], scale=1.0),
                             r=[pr2, vec_r], w=[gt_r])
                        vf, vf_r = f2.get()
                        p.dma(vf[:], dd["vfirst"][oc * 128:(oc + 1) * 128, cols], w=[vf_r])
                        p.op("dve", lambda e, vf=vf, ps=ps: e.tensor_tensor(out=vf[:], in0=vf[:], in1=ps[:, 0:TG], op=ALU.subtract), r=[vf_r, pr], w=[vf_r])
                        p.op("dve", lambda e, vf=vf, gt=gt: e.tensor_tensor(out=vf[:], in0=vf[:], in1=gt[:], op=ALU.mult), r=[vf_r, gt_r], w=[vf_r])
                        p.op("dve", lambda e, vf=vf, ps=ps, oc=oc: e.tensor_tensor(out=v_t[:, oc, :], in0=vf[:], in1=ps[:, 0:TG], op=ALU.add), r=[vf_r, pr], w=[v_r])
            out_dma(p, dd["v"].rearrange("(c p) t -> p c t", p=128)[:, :, cols], v_t[:], v_r, "pr_ov")
            if layer == 0:
                out_dma(p, dd["vfirst"].rearrange("(c p) t -> p c t", p=128)[:, :, cols], v_t[:], v_r, "pr_ovf")
            if PRE_STOP <= 3:
                continue
            xm, xm_r = mix_dve(0)
            for o2 in range(KC // 2):
                wb, wr = pw.load(W["wr"], o2 * 256)
                for oi in range(2):
                    oc = o2 * 2 + oi
                    ps, pr = proj_mm(p, cx, wb, wr, oi * 128, xm, xm_r)
                    rt, rt_r = ob.get()
                    p.op("act", lambda e, rt=rt, ps=ps: e.copy(out=rt[:], in_=ps[:, 0:TG]), r=[pr], w=[rt_r])
                    out_dma(p, dd["r"][oc * 128:(oc + 1) * 128, cols], rt[:], rt_r, "pr_or")
                    if os.environ.get("PRE_NOBONUS"):
                        continue
                    rk, rk_r = f1.get()
                    crk = vc["rk"] + oc
                    p.op("dve", lambda e, rk=rk, ps=ps, oc=oc, crk=crk: e.scalar_tensor_tensor(out=rk[:], in0=ps[:, 0:TG], scalar=vec[:, crk:crk + 1],
                                                                                          in1=k_t[:, oc, :], op0=ALU.mult, op1=ALU.mult),
                         r=[pr, vec_r, k_r], w=[rk_r])
                    if os.environ.get("PRE_NOBONUS") == "2":
                        continue
                    ps2, pr2 = head_sum(p, cx, cs, cs.bo1, rk, rk_r)
                    if os.environ.get("PRE_NOBONUS") == "3":
                        continue
                    bn, bn_r = obn.get()
                    p.op("dve", lambda e, bn=bn, ps2=ps2, oc=oc: e.tensor_tensor(out=bn[:], in0=ps2[:, 0:TG], in1=v_t[:, oc, :], op=ALU.mult),
                         r=[pr2, v_r], w=[bn_r])
                    if os.environ.get("PRE_NOBONUS") == "5":
                        p.op("pool", lambda e, bn=bn: e.tensor_scalar_mul(out=bn[:], in0=bn[:], scalar1=1.0), r=[bn_r], w=[bn_r])
                    elif os.environ.get("PRE_NOBONUS") == "8":
                        b2, b2_r = ob.get()
                        p.op("pool", lambda e, bn=bn, b2=b2: e.tensor_copy(out=b2[:], in_=bn[:]), r=[bn_r], w=[b2_r])
                        out_dma(p, dd["bonus"][oc * 128:(oc + 1) * 128, cols], b2[:], b2_r, "pr_obn")
                    elif os.environ.get("PRE_NOBONUS") == "6":
                        out_dma(p, dd["bonus"][oc * 128:(oc + 1) * 128, cols], rt[:], rt_r, "pr_obn")
                    elif os.environ.get("PRE_NOBONUS") != "4":
                        out_dma(p, dd["bonus"][oc * 128:(oc + 1) * 128, cols], bn[:], bn_r, "pr_obn")
            if PRE_STOP <= 4:
                continue
            xm, xm_r = mix_dve(1)
            tw, tw_r = lora1(xm, xm_r, w1, 64, AF.Tanh)
            for oc in range(KC):
                ps, pr = cx.bank()
                p.op("pe", lambda e, oc=oc, ps=ps: e.matmul(ps[:, 0:TG], w2[:, oc * 128:(oc + 1) * 128], tw[0:64, :], start=True, stop=True),
                     r=[lw_r, tw_r], w=[pr])
                wt, wt_r = ob.get()
                c0 = vc["w0"] + oc
                p.op("act", lambda e, wt=wt, ps=ps, c0=c0: e.activation(out=wt[:], in_=ps[:, 0:TG], func=AF.Sigmoid, bias=vec[:, c0:c0 + 1], scale=1.0),
                     r=[pr, vec_r], w=[wt_r])
                p.op("pool", lambda e, wt=wt: e.tensor_scalar_mul(out=wt[:], in0=wt[:], scalar1=LW_SCALE), r=[wt_r], w=[wt_r])
                out_dma(p, dd["lw"][oc * 128:(oc + 1) * 128, cols], wt[:], wt_r, "pr_ow")
            if PRE_STOP <= 5:
                continue
            xm, xm_r = mix_dve(5)
            tg0, tg0_r = lora1(xm, xm_r, g1, 128, AF.Sigmoid)
            ps, pr = cx.bank()
            for kc in range(KC):
                p.op("pe", lambda e, kc=kc, ps=ps: e.matmul(ps[0:32, 0:TG], g1[:, kc, 128:160], xm[:, kc, :], start=(kc == 0), stop=(kc == KC - 1)),
                     r=[lw_r, xm_r], w=[pr])
            tg1, tg1_r = lt.get()
            p.op("act", lambda e, ps=ps: e.activation(out=tg1[0:32, :], in_=ps[0:32, 0:TG], func=AF.Sigmoid), r=[pr], w=[tg1_r])
            for oc in range(KC):
                ps, pr = cx.bank()
                p.op("pe", lambda e, oc=oc, ps=ps: e.matmul(ps[:, 0:TG], g2a[:, oc * 128:(oc + 1) * 128], tg0[:, :], start=True, stop=False),
                     r=[lw_r, tg0_r], w=[pr])
                p.op("pe", lambda e, oc=oc, ps=ps: e.matmul(ps[:, 0:TG], g2b[:, oc * 128:(oc + 1) * 128], tg1[0:32, :], start=False, stop=True),
                     r=[lw_r, tg1_r], w=[pr])
                gt, gt_r = ob.get()
                p.op("act", lambda e, gt=gt, ps=ps: e.copy(out=gt[:], in_=ps[:, 0:TG]), r=[pr], w=[gt_r])
                out_dma(p, dd["g"][oc * 128:(oc + 1) * 128, cols], gt[:], gt_r, "pr_og")


def stage_post(p, cx, cs, T, layer, variant, x_in_d, x_out_d, vec, vec_r, vc, mod_sb, mod_r, wo_d, up_d, down_d, dd, final_g=None):
    nc = p.nc
    NTL = T // TG
    mc = layer * 48
    with p.scope() as es2:
        sb = p.sb
        gm1 = sb("po_gm1", [128, KC])
        gm1_r = p.res("po_gm1")
        p.op("dve", lambda e: e.tensor_scalar_add(out=gm1[:], in0=mod_sb[:, mc + 16:mc + 24], scalar1=1.0), r=[mod_r], w=[gm1_r])
        gm2 = sb("po_gm2", [128, 2 * KC])
        gm2_r = p.res("po_gm2")
        g0 = vc["norm_mlp_g"]
        p.op("dve", lambda e: e.scalar_tensor_tensor(out=gm2[:, 0:KC], in0=mod_sb[:, mc + 32:mc + 40], scalar=1.0, in1=vec[:, g0:g0 + KC],
                                                    op0=ALU.add, op1=ALU.mult), r=[vec_r, mod_r], w=[gm2_r])
        p.op("dve", lambda e: e.tensor_scalar_add(out=gm2[:, KC:2 * KC], in0=mod_sb[:, mc + 40:mc + 48], scalar1=1.0), r=[mod_r], w=[gm2_r])
        shift2 = mod_sb[:, mc + 24:mc + 32]
        st = mlp_state(p)
        xt = Rot(p, "po_x", [128, KC, TG], F32, 1)
        z = sb("po_z", [128, KC, TG], BF16)
        z_r = p.res("po_z")
        i1 = Rot(p, "po_i1", [128, TG], F32, 2)
        i3 = Rot(p, "po_i3", [128, TG], F32, 2)
        if variant == "rwkv":
            i2 = Rot(p, "po_i2", [128, TG], F32, 2)
            f1 = Rot(p, "po_f1", [128, TG], F32, 2)
        pw = ProjW(p, "po_w")
        xv = x_in_d.rearrange("(c p) t -> p c t", p=128)
        ov = x_out_d.rearrange("(c p) t -> p c t", p=128)
        for ti in range(NTL):
            cols = slice(ti * TG, (ti + 1) * TG)
            x_t, x_r = xt.get()
            p.dma(x_t[:], xv[:, :, cols], w=[x_r])
            for oc in range(KC):
                rows = slice(oc * 128, (oc + 1) * 128)
                if variant == "rwkv":
                    y, y_r = i1.get()
                    p.dma(y[:], dd["y"][rows, cols], w=[y_r])
                    bn, bn_r = i2.get()
                    p.dma(bn[:], dd["bonus"][rows, cols], w=[bn_r])
                    g, g_r = i3.get()
                    p.dma(g[:], dd["g"][rows, cols], w=[g_r])
                    ps, pr = head_sum(p, cx, cs, cs.bo64, y, y_r)
                    p.op("dve", lambda e, y=y, ps=ps: e.tensor_tensor(out=y[:], in0=y[:], in1=ps[:, 0:TG], op=ALU.subtract), r=[y_r, pr], w=[y_r])
                    sq, sq_r = f1.get()
                    p.op("act", lambda e, y=y, sq=sq: e.activation(out=sq[:], in_=y[:], func=AF.Square), r=[y_r], w=[sq_r])
                    ps2, pr2 = head_sum(p, cx, cs, cs.bo64, sq, sq_r)
                    p.op("act", lambda e, sq=sq, ps2=ps2: e.activation(out=sq[:], in_=ps2[:, 0:TG], func=AF.Sqrt, bias=cx.eps_t[:, 1:2], scale=1.0),
                         r=[pr2, cx.r_const], w=[sq_r])
                    p.op("dve", lambda e, sq=sq: e.reciprocal(out=sq[:], in_=sq[:]), r=[sq_r], w=[sq_r])
                    p.op("dve", lambda e, y=y, sq=sq: e.tensor_tensor(out=y[:], in0=y[:], in1=sq[:], op=ALU.mult), r=[y_r, sq_r], w=[y_r])
                    cw, cb = vc["lnw"] + oc, vc["lnb"] + oc
                    p.op("dve", lambda e, y=y, cw=cw, cb=cb: e.tensor_scalar(out=y[:], in0=y[:], scalar1=vec[:, cw:cw + 1], scalar2=vec[:, cb:cb + 1],
                                                                         op0=ALU.mult, op1=ALU.add), r=[y_r, vec_r], w=[y_r])
                    p.op("pool", lambda e, y=y, bn=bn: e.tensor_tensor(out=y[:], in0=y[:], in1=bn[:], op=ALU.add), r=[y_r, bn_r], w=[y_r])
                    p.op("pool", lambda e, y=y, g=g, oc=oc: e.tensor_tensor(out=z[:, oc, :], in0=y[:], in1=g[:], op=ALU.mult), r=[y_r, g_r], w=[z_r])
                else:
                    o, o_r = i1.get()
                    p.dma(o[:], dd["o"][rows, cols], w=[o_r])
                    g, g_r = i3.get()
                    p.dma(g[:], dd["sig"][rows, cols], w=[g_r])
                    p.op("pool", lambda e, o=o, g=g, oc=oc: e.tensor_tensor(out=z[:, oc, :], in0=o[:], in1=g[:], op=ALU.mult), r=[o_r, g_r], w=[z_r])
            for o2 in range(KC // 2):
                wb, wr = pw.load(wo_d, o2 * 256)
                for oi in range(2):
                    oc = o2 * 2 + oi
                    ps, pr = proj_mm(p, cx, wb, wr, oi * 128, z, z_r)
                    xs = x_t[:, oc, :]
                    p.op("dve", lambda e, ps=ps, xs=xs, oc=oc: e.scalar_tensor_tensor(out=xs, in0=ps[:, 0:TG], scalar=gm1[:, oc:oc + 1], in1=xs,
                                                                                  op0=ALU.mult, op1=ALU.add), r=[pr, gm1_r, x_r], w=[x_r])
            emit_mlp(p, cx, x_t, x_r, gm2, gm2_r, shift2, mod_r, up_d, down_d, st)
            if final_g is not None:
                emit_norm(p, cx, x_t, x_r, final_g, None, vec_r, x_t, x_r, st["scr"])
                out_dma(p, ov[:, :, cols], x_t[:], x_r, "po_out")
            else:
                out_dma(p, ov[:, :, cols], x_t[:], x_r, "po_out")


def stage_scan(p, cx, cs, T, dd):
    nc = p.nc
    NP4 = T // 256
    with p.scope() as es2:
        sb = p.sb
        mr = p.res("sc_masks")
        mU2 = sb("sc_mU2", [128, 256])
        mL = sb("sc_mL", [128, 128])
        p.op("pool", lambda e: e.affine_select(out=mU2[:, 0:128], in_=cs.ones_f[:], pattern=[[1, 128]], compare_op=ALU.is_ge, fill=0.0, base=-1, channel_multiplier=-1), r=[cs.r], w=[mr])
        p.op("pool", lambda e: e.affine_select(out=mU2[:, 128:256], in_=cs.ones_f[:], pattern=[[1, 128]], compare_op=ALU.is_ge, fill=0.0, base=0, channel_multiplier=-1), r=[cs.r], w=[mr])
        p.op("pool", lambda e: e.affine_select(out=mL[:], in_=cs.ones_f[:], pattern=[[-1, 128]], compare_op=ALU.is_ge, fill=0.0, base=-1, channel_multiplier=1), r=[cs.r], w=[mr])
        p.op("pool", lambda e: e.memset(mU2[0:64, 64:128], 0.0), w=[mr])
        p.op("pool", lambda e: e.memset(mU2[0:64, 192:256], 0.0), w=[mr])
        p.op("pool", lambda e: e.memset(mL[64:128, 0:64], 0.0), w=[mr])
        NSTREAM = 3

        def stream(sid, hps):
            names = ["r", "lw", "k", "v", "a", "b"]
            inp = {n: Rot(p, "sc%d_" % sid + "in_" + n, [128, 256], F32, 2) for n in names}
            cum = Rot(p, "sc%d_" % sid + "cum", [128, 128], F32, 2)
            cpv = Rot(p, "sc%d_" % sid + "cpv", [128, 128], F32, 2)
            eP = Rot(p, "sc%d_" % sid + "eP", [128, 128], F32, 3)
            eN = Rot(p, "sc%d_" % sid + "eN", [128, 128], F32, 2)
            eV = Rot(p, "sc%d_" % sid + "eV", [128, 128], F32, 2)
            AR = Rot(p, "sc%d_" % sid + "AR", [128, 256], F32, 2)
            bT = Rot(p, "sc%d_" % sid + "bT", [128, 128], F32, 2)
            kT = Rot(p, "sc%d_" % sid + "kT", [128, 128], F32, 2)
            PadA = Rot(p, "sc%d_" % sid + "PadA", [128, 4, 128], F32, 2)
            PadB = Rot(p, "sc%d_" % sid + "PadB", [128, 4, 128], F32, 2)
            PZA = Rot(p, "sc%d_" % sid + "PZA", [128, 2, 128], F32, 2)
            PZB = Rot(p, "sc%d_" % sid + "PZB", [128, 2, 128], F32, 2)
            for rot in (PadA, PadB, PZA, PZB):
                for t, r in zip(rot.t, rot.r):
                    p.op("pool", lambda e, t=t: e.memset(t[:], 0.0), w=[r])
            ZcA = Rot(p, "sc%d_" % sid + "ZcA", [128, 128], F32, 2)
            ZcB = Rot(p, "sc%d_" % sid + "ZcB", [128, 128], F32, 2)
            XP = Rot(p, "sc%d_" % sid + "XP", [128, 256], F32, 3)
            YP = Rot(p, "sc%d_" % sid + "YP", [128, 256], F32, 3)
            Xp = Rot(p, "sc%d_" % sid + "Xp", [128, 128], F32, 4)
            Lp = Rot(p, "sc%d_" % sid + "Lp", [128, 128], F32, 4)
            RH = Rot(p, "sc%d_" % sid + "RH", [128, 128], F32, 2)
            YH = Rot(p, "sc%d_" % sid + "YH", [128, 128], F32, 2)
            IG = Rot(p, "sc%d_" % sid + "IG", [128, 128], F32, 4)
            NN = Rot(p, "sc%d_" % sid + "NN", [128, 128], F32, 4)
            Zbd = sb("sc%d_" % sid + "Zbd", [128, 128])
            Zbd_r = p.res("sc%d_Zbd" % sid)
            yo = Rot(p, "sc%d_" % sid + "yo", [128, 256], F32, 2)
            ci = [0]

            def evac(out, in_, r, w):
                ci[0] += 1
                if ci[0] % 2 == 0:
                    p.op("act", lambda e: e.copy(out=out, in_=in_), r=r, w=w)
                else:
                    p.op("dve", lambda e: e.tensor_copy(out=out, in_=in_), r=r, w=w)

            for hp in hps:
                rows = slice(hp * 128, (hp + 1) * 128)
                p.op("pool", lambda e: e.memset(Zbd[:], 0.0), w=[Zbd_r])
                for p4 in range(NP4):
                    cols4 = slice(p4 * 256, (p4 + 1) * 256)
                    tin = {}
                    for n in names:
                        t, r = inp[n].get()
                        p.dma(t[:], dd[n][rows, cols4], w=[r])
                        tin[n] = (t, r)
                    yo_t, yo_r = yo.get()
                    for u in range(2):
                        c = slice(u * 128, (u + 1) * 128)
                        lw_t, lw_r = tin["lw"]
                        cum_t, cum_r = cum.get()
                        for ch in range(2):
                            cc = slice(u * 128 + ch * 64, u * 128 + ch * 64 + 64)
                            oc = slice(ch * 64, ch * 64 + 64)
                            p.op("dve", lambda e, cc=cc, oc=oc, cum_t=cum_t, lw_t=lw_t: e.tensor_tensor_scan(
                                out=cum_t[:, oc], data0=cs.ones_f[:, oc], data1=lw_t[:, cc], initial=0.0, op0=ALU.mult, op1=ALU.add),
                                r=[lw_r, cs.r], w=[cum_r])
                        cpv_t, cpv_r = cpv.get()
                        p.op("pool", lambda e, cpv_t=cpv_t, cum_t=cum_t, lw_t=lw_t, c=c: e.tensor_tensor(out=cpv_t[:], in0=cum_t[:], in1=lw_t[:, c], op=ALU.subtract),
                             r=[cum_r, lw_r], w=[cpv_r])
                        eP_t, eP_r = eP.get()
                        eN_t, eN_r = eN.get()
                        eV_t, eV_r = eV.get()
                        p.op("act", lambda e, eP_t=eP_t, cum_t=cum_t: e.activation(out=eP_t[:], in_=cum_t[:], func=AF.Exp), r=[cum_r], w=[eP_r])
                        p.op("act", lambda e, eN_t=eN_t, cum_t=cum_t: e.activation(out=eN_t[:], in_=cum_t[:], func=AF.Exp, scale=-1.0), r=[cum_r], w=[eN_r])
                        p.op("act", lambda e, eV_t=eV_t, cpv_t=cpv_t: e.activation(out=eV_t[:], in_=cpv_t[:], func=AF.Exp), r=[cpv_r], w=[eV_r])
                        AR_t, AR_r = AR.get()
                        bT_t, bT_r = bT.get()
                        kT_t, kT_r = kT.get()
                        a_t, a_r = tin["a"]
                        r_t, r_r = tin["r"]
                        b_t, b_r = tin["b"]
                        k_t, k_r = tin["k"]
                        v_t, v_r = tin["v"]
                        p.op("dve", lambda e, AR_t=AR_t, a_t=a_t, eV_t=eV_t, c=c: e.tensor_tensor(out=AR_t[:, 0:128], in0=a_t[:, c], in1=eV_t[:], op=ALU.mult), r=[a_r, eV_r], w=[AR_r])
                        p.op("dve", lambda e, AR_t=AR_t, r_t=r_t, eP_t=eP_t, c=c: e.tensor_tensor(out=AR_t[:, 128:256], in0=r_t[:, c], in1=eP_t[:], op=ALU.mult), r=[r_r, eP_r], w=[AR_r])
                        p.op("dve", lambda e, bT_t=bT_t, b_t=b_t, eN_t=eN_t, c=c: e.tensor_tensor(out=bT_t[:], in0=b_t[:, c], in1=eN_t[:], op=ALU.mult), r=[b_r, eN_r], w=[bT_r])
                        p.op("pool", lambda e, kT_t=kT_t, k_t=k_t, eN_t=eN_t, c=c: e.tensor_tensor(out=kT_t[:], in0=k_t[:, c], in1=eN_t[:], op=ALU.mult), r=[k_r, eN_r], w=[kT_r])
                        yield
                        ps, pr = cx.bank()
                        srcs = [(AR_t[:, 0:128], AR_r), (bT_t[:], bT_r), (kT_t[:], kT_r), (v_t[:, c], v_r)]
                        for j, (src, sr) in enumerate(srcs):
                            p.op("pe", lambda e, j=j, src=src, ps=ps: e.transpose(out=ps[:, j * 128:(j + 1) * 128], in_=src, identity=cs.ident[:]), r=[sr, cs.r], w=[pr])
                        PA, PA_r = PadA.get()
                        PB, PB_r = PadB.get()
                        psv = ps[:, 0:512].rearrange("p (j c) -> p j c", j=4)
                        p.op("act", lambda e, PA=PA, psv=psv: e.copy(out=PA[:, :, 0:64], in_=psv[:, :, 0:64]), r=[pr], w=[PA_r])
                        p.op("dve", lambda e, PB=PB, psv=psv: e.tensor_copy(out=PB[:, :, 64:128], in_=psv[:, :, 64:128]), r=[pr], w=[PB_r])
                        Zcs = [ZcA.get(), ZcB.get()]
                        p.op("act", lambda e, ps=ps: e.copy(out=Zcs[0][0][:, 0:64], in_=ps[:, 0:64]), r=[pr], w=[Zcs[0][1]])
                        p.op("dve", lambda e, ps=ps: e.tensor_copy(out=Zcs[1][0][:, 0:64], in_=ps[:, 64:128]), r=[pr], w=[Zcs[1][1]])
                        yield
                        PZ = [PZA.get(), PZB.get()]
                        Pad = [(PA, PA_r), (PB, PB_r)]
                        XPs, YPs = [], []
                        for h in range(2):
                            hs = slice(h * 64, (h + 1) * 64)
                            Zc_t, Zc_r = Zcs[h]
                            ps1, pr1 = cx.bank()
                            p.op("pe", lambda e, ps1=ps1, hs=hs: e.matmul(ps1[:, 0:256], bT_t[hs, :], AR_t[hs, :], start=True, stop=True), r=[bT_r, AR_r], w=[pr1])
                            XP_t, XP_r = XP.get()
                            p.op("dve", lambda e, XP_t=XP_t, ps1=ps1: e.tensor_tensor(out=XP_t[:], in0=ps1[:, 0:256], in1=mU2[:], op=ALU.mult), r=[pr1, mr], w=[XP_r])
                            ps2, pr2 = cx.bank()
                            p.op("pe", lambda e, ps2=ps2, hs=hs: e.matmul(ps2[:, 0:256], kT_t[hs, :], AR_t[hs, :], start=True, stop=True), r=[kT_r, AR_r], w=[pr2])
                            YP_t, YP_r = YP.get()
                            p.op("dve", lambda e, YP_t=YP_t, ps2=ps2: e.tensor_tensor(out=YP_t[:], in0=ps2[:, 0:256], in1=mU2[:], op=ALU.mult), r=[pr2, mr], w=[YP_r])
                            ps3, pr3 = cx.bank()
                            p.op("pe", lambda e, ps3=ps3, hs=hs: e.matmul(ps3[:, 0:128], AR_t[hs, 0:128], bT_t[hs, :], start=True, stop=True), r=[bT_r, AR_r], w=[pr3])
                            L_t, L_r = Lp.get()
                            p.op("dve", lambda e, L_t=L_t, ps3=ps3: e.tensor_tensor(out=L_t[:], in0=ps3[:, 0:128], in1=mL[:], op=ALU.mult), r=[pr3, mr], w=[L_r])
                            XPs.append((XP_t, XP_r))
                            YPs.append((YP_t, YP_r))
                            yield
                            Pd, Pd_r = Pad[h]
                            ps4, pr4 = cx.bank()
                            p.op("pe", lambda e, ps4=ps4, YP_t=YP_t, Pd=Pd, hs=hs: e.matmul(ps4[:, 0:64], YP_t[:, 0:128], Pd[:, 3, hs], start=True, stop=True), r=[YP_r, Pd_r], w=[pr4])
                            evac(Zc_t[:, 64:128], ps4[:, 0:64], [pr4], [Zc_r])
                            X_t, X_r = XP_t[:, 0:128], XP_r
                            Lc_t, Lc_r = L_t[:], L_r
                            PZ_t, PZ_r = PZ[h]
                            for j in range(6):
                                yield
                                psa, pra = cx.bank()
                                p.op("pe", lambda e, psa=psa, X_t=X_t, Zc_t=Zc_t: e.matmul(psa[:, 0:128], X_t, Zc_t[:], start=True, stop=True), r=[X_r, Zc_r], w=[pra])
                                if j < 5:
                                    psx, prx = cx.bank()
                                    p.op("pe", lambda e, psx=psx, X_t=X_t, Lc_t=Lc_t: e.matmul(psx[:, 0:128], Lc_t, X_t, start=True, stop=True), r=[X_r, Lc_r], w=[prx])
                                    psl, prl = cx.bank()
                                    p.op("pe", lambda e, psl=psl, X_t=X_t, Lc_t=Lc_t: e.matmul(psl[:, 0:128], X_t, Lc_t, start=True, stop=True), r=[X_r, Lc_r], w=[prl])
                                    p.op("dve", lambda e, psa=psa, Zc_t=Zc_t: e.tensor_tensor(out=Zc_t[:], in0=psa[:, 0:128], in1=Zc_t[:], op=ALU.add), r=[pra, Zc_r], w=[Zc_r])
                                    Xn, Xn_r = Xp.get()
                                    Ln, Ln_r = Lp.get()
                                    evac(Xn[:], psx[:, 0:128], [prx], [Xn_r])
                                    evac(Ln[:], psl[:, 0:128], [prl], [Ln_r])
                                    X_t, X_r, Lc_t, Lc_r = Xn[:], Xn_r, Ln[:], Ln_r
                                else:
                                    p.op("dve", lambda e, psa=psa, Zc_t=Zc_t, PZ_t=PZ_t, hs=hs: e.tensor_tensor(
                                        out=PZ_t[:, :, hs], in0=psa[:, 0:128].rearrange("p (j c) -> p j c", j=2),
                                        in1=Zc_t[:].rearrange("p (j c) -> p j c", j=2), op=ALU.add), r=[pra, Zc_r], w=[PZ_r])
                        yield
                        psr_, prr = cx.bank()
                        for h in range(2):
                            p.op("pe", lambda e, h=h, psr_=psr_: e.matmul(psr_[:, 0:128], PZ[h][0][:, 0, :], XPs[h][0][:, 128:256], start=(h == 0), stop=(h == 1)),
                                 r=[PZ[h][1], XPs[h][1]], w=[prr])
                        RH_t, RH_r = RH.get()
                        p.op("dve", lambda e, RH_t=RH_t, psr_=psr_, AR_t=AR_t: e.tensor_tensor(out=RH_t[:], in0=psr_[:, 0:128], in1=AR_t[:, 128:256], op=ALU.add), r=[prr, AR_r], w=[RH_r])
                        psy, pry = cx.bank()
                        for h in range(2):
                            p.op("pe", lambda e, h=h, psy=psy: e.matmul(psy[:, 0:128], PZ[h][0][:, 1, :], XPs[h][0][:, 128:256], start=(h == 0), stop=False),
                                 r=[PZ[h][1], XPs[h][1]], w=[pry])
                            p.op("pe", lambda e, h=h, psy=psy: e.matmul(psy[:, 0:128], Pad[h][0][:, 3, :], YPs[h][0][:, 128:256], start=False, stop=(h == 1)),
                                 r=[Pad[h][1], YPs[h][1]], w=[pry])
                        YH_t, YH_r = YH.get()
                        evac(YH_t[:], psy[:, 0:128], [pry], [YH_r])
                        yield
                        IGs, NNs = [], []
                        for ch in range(2):
                            tk = slice(ch * 64, ch * 64 + 64)
                            psg, prg = cx.bank()
                            for h in range(2):
                                p.op("pe", lambda e, h=h, psg=psg, tk=tk: e.matmul(psg[:, 0:128], PZ[h][0][tk, 0, :], Pad[h][0][tk, 1, :], start=(h == 0), stop=(h == 1)),
                                     r=[PZ[h][1], Pad[h][1]], w=[prg])
                            IG_t, IG_r = IG.get()
                            p.op("dve", lambda e, IG_t=IG_t, psg=psg: e.tensor_tensor(out=IG_t[:], in0=psg[:, 0:128], in1=cs.ident[:], op=ALU.add), r=[prg, cs.r], w=[IG_r])
                            psn, prn = cx.bank()
                            for h in range(2):
                                p.op("pe", lambda e, h=h, psn=psn, tk=tk: e.matmul(psn[:, 0:128], Pad[h][0][tk, 1, :], PZ[h][0][tk, 1, :], start=(h == 0), stop=False),
                                     r=[PZ[h][1], Pad[h][1]], w=[prn])
                                p.op("pe", lambda e, h=h, psn=psn, tk=tk: e.matmul(psn[:, 0:128], Pad[h][0][tk, 2, :], Pad[h][0][tk, 3, :], start=False, stop=(h == 1)),
                                     r=[Pad[h][1]], w=[prn])
                            NN_t, NN_r = NN.get()
                            wc = eP_t[:, ch * 64 + 63:ch * 64 + 64]
                            p.op("dve", lambda e, NN_t=NN_t, psn=psn, wc=wc: e.tensor_scalar_mul(out=NN_t[:], in0=psn[:, 0:128], scalar1=wc), r=[prn, eP_r], w=[NN_r])
                            IGs.append((IG_t, IG_r))
                            NNs.append((NN_t, NN_r, wc))
                        for ch in range(2):
                            yield
                            tcol = slice(ch * 64, ch * 64 + 64)
                            ocol = slice(u * 128 + ch * 64, u * 128 + ch * 64 + 64)
                            psq, prq = cx.bank()
                            p.op("pe", lambda e, psq=psq, tcol=tcol, RH_t=RH_t: e.matmul(psq[:, 0:64], Zbd[:], RH_t[:, tcol], start=True, stop=True), r=[Zbd_r, RH_r], w=[prq])
                            p.op("dve", lambda e, psq=psq, tcol=tcol, ocol=ocol, YH_t=YH_t, yo_t=yo_t: e.tensor_tensor(out=yo_t[:, ocol], in0=psq[:, 0:64], in1=YH_t[:, tcol], op=ALU.add),
                                 r=[prq, YH_r], w=[yo_r])
                            psz, prz = cx.bank()
                            IG_t, IG_r = IGs[ch]
                            NN_t, NN_r, wc = NNs[ch]
                            p.op("pe", lambda e, psz=psz, IG_t=IG_t: e.matmul(psz[:, 0:128], IG_t[:], Zbd[:], start=True, stop=True), r=[IG_r, Zbd_r], w=[prz])
                            p.op("dve", lambda e, psz=psz, NN_t=NN_t, wc=wc: e.scalar_tensor_tensor(out=Zbd[:], in0=psz[:, 0:128], scalar=wc, in1=NN_t[:], op0=ALU.mult, op1=ALU.add),
                                 r=[prz, NN_r, eP_r], w=[Zbd_r])
                    out_dma(p, dd["y"][rows, cols4], yo_t[:], yo_r, "sc_y")


        gens = [stream(s, list(range(s, H // 2, NSTREAM))) for s in range(NSTREAM)]
        while gens:
            for g in list(gens):
                try:
                    next(g)
                except StopIteration:
                    gens.remove(g)


def stage_kvq(p, cx, cs, T, kind, layer, x_d, vec, vec_r, g_col, mod_sb, mod_r, W_d, gain_sb, gain_r, dd, fb_sb=None, fb_r=None, Wf_d=None):
    nc = p.nc
    NTL = T // TG
    with p.scope() as es2:
        sb = p.sb
        gm = sb("kq_gm", [128, KC])
        gm_r = p.res("kq_gm")
        if kind == "kv":
            sh_c, sc_c = 192, 200
        else:
            sh_c, sc_c = layer * 48, layer * 48 + 8
        p.op("dve", lambda e: e.scalar_tensor_tensor(out=gm[:], in0=mod_sb[:, sc_c:sc_c + 8], scalar=1.0, in1=vec[:, g_col:g_col + KC],
                                                    op0=ALU.add, op1=ALU.mult), r=[vec_r, mod_r], w=[gm_r])
        shift = mod_sb[:, sh_c:sh_c + 8]
        xt = Rot(p, "kq_x", [128, KC, TG], F32, 2)
        h = sb("kq_h", [128, KC, TG], BF16)
        h_r = p.res("kq_h")
        scr = norm_scratch(p, "kqn")
        pw = ProjW(p, "kq_w")
        f1 = Rot(p, "kq_f1", [128, TG], F32, 3)
        ob = Rot(p, "kq_ob", [128, TG], F32, 4)
        xv = x_d.rearrange("(c p) t -> p c t", p=128)
        if kind == "kv":
            wf_s = sb("kq_wfs", [128, KC, 16])
            wf = sb("kq_wf", [128, KC, 16], BF16)
            wf_r = p.res("kq_wf")
            p.dma(wf_s[:], Wf_d.rearrange("(kc p) m -> p kc m", p=128)[:, :, 2 * D:2 * D + 16], w=[wf_r])
            p.op("pool", lambda e: e.tensor_copy(out=wf[:], in_=wf_s[:]), r=[wf_r], w=[wf_r])
            lf = Rot(p, "kq_lf", [16, TG], F32, 2)
        gscale = 1.0 if kind == "kv" else 0.125
        for ti in range(NTL):
            cols = slice(ti * TG, (ti + 1) * TG)
            x_t, x_r = xt.get()
            p.dma(x_t[:], xv[:, :, cols], w=[x_r])
            emit_norm(p, cx, x_t, x_r, gm, shift, gm_r, h, h_r, scr)
            for o2 in range(2 * KC // 2):
                wb, wr = pw.load(W_d, o2 * 256)
                for oi in range(2):
                    oc = o2 * 2 + oi
                    ps, pr = proj_mm(p, cx, wb, wr, oi * 128, h, h_r)
                    if oc < KC:
                        sq, sq_r = f1.get()
                        p.op("act", lambda e, sq=sq, ps=ps: e.activation(out=sq[:], in_=ps[:, 0:TG], func=AF.Square), r=[pr], w=[sq_r])
                        ps2, pr2 = head_sum(p, cx, cs, cs.bo64, sq, sq_r)
                        p.op("act", lambda e, sq=sq, ps2=ps2: e.activation(out=sq[:], in_=ps2[:, 0:TG], func=AF.Sqrt, bias=cx.eps_t[:, 0:1], scale=1.0),
                             r=[pr2, cx.r_const], w=[sq_r])
                        p.op("dve", lambda e, sq=sq: e.reciprocal(out=sq[:], in_=sq[:]), r=[sq_r], w=[sq_r])
                        o, o_r = ob.get()
                        p.op("dve", lambda e, o=o, ps=ps, sq=sq: e.scalar_tensor_tensor(out=o[:], in0=ps[:, 0:TG], scalar=gain_sb, in1=sq[:], op0=ALU.mult, op1=ALU.mult),
                             r=[pr, sq_r, gain_r], w=[o_r])
                        if gscale != 1.0:
                            p.op("pool", lambda e, o=o: e.tensor_scalar_mul(out=o[:], in0=o[:], scalar1=gscale), r=[o_r], w=[o_r])
                        dst = dd["ksh"] if kind == "kv" else dd["q"]
                        out_dma(p, dst[oc * 128:(oc + 1) * 128, cols], o[:], o_r, "kq_o")
                    else:
                        o, o_r = ob.get()
                        if kind == "kv":
                            p.op("act", lambda e, o=o, ps=ps: e.copy(out=o[:], in_=ps[:, 0:TG]), r=[pr], w=[o_r])
                            out_dma(p, dd["vsh"][(oc - KC) * 128:(oc - KC + 1) * 128, cols], o[:], o_r, "kq_o")
                        else:
                            p.op("act", lambda e, o=o, ps=ps: e.activation(out=o[:], in_=ps[:, 0:TG], func=AF.Sigmoid), r=[pr], w=[o_r])
                            out_dma(p, dd["sig"][(oc - KC) * 128:(oc - KC + 1) * 128, cols], o[:], o_r, "kq_o")
            if kind == "kv":
                ps, pr = cx.bank()
                for kc in range(KC):
                    p.op("pe", lambda e, kc=kc, ps=ps: e.matmul(ps[0:16, 0:TG], wf[:, kc, :], h[:, kc, :], start=(kc == 0), stop=(kc == KC - 1)), r=[wf_r, h_r], w=[pr])
                l, l_r = lf.get()
                p.op("act", lambda e, l=l, ps=ps: e.activation(out=l[:], in_=ps[0:16, 0:TG], func=AF.Exp, bias=fb_sb, scale=-1.0), r=[pr, fb_r], w=[l_r])
                p.op("act", lambda e, l=l: e.activation(out=l[:], in_=l[:], func=AF.Ln, bias=cx.eps_t[0:16, 2:3], scale=1.0), r=[l_r, cx.r_const], w=[l_r])
                p.op("pool", lambda e, l=l: e.tensor_scalar_mul(out=l[:], in0=l[:], scalar1=-1.0), r=[l_r], w=[l_r])
                out_dma(p, dd["logf"][:, cols], l[:], l_r, "kq_lf")


def stage_fprep(p, cx, T, dd):
    nc = p.nc
    FB = min(2048, T)
    with p.scope() as es2:
        sb = p.sb
        ones = sb("fp_ones", [16, FB])
        cr = p.res("fp_c")
        p.op("pool", lambda e: e.memset(ones[:], 1.0), w=[cr])
        lf = Rot(p, "fp_lf", [16, FB], F32, 2)
        F = Rot(p, "fp_F", [16, FB], F32, 2)
        r1 = Rot(p, "fp_r1", [16, FB], F32, 2)
        pp = Rot(p, "fp_pp", [16, 3, FB], BF16, 2)
        pn = Rot(p, "fp_pn", [16, 3, FB], BF16, 2)
        prev = None
        for bi in range(T // FB):
            cols = slice(bi * FB, (bi + 1) * FB)
            l, l_r = lf.get()
            p.dma(l[:], dd["logf"][:, cols], w=[l_r])
            f, f_r = F.get()
            init = 0.0 if prev is None else prev[0][:, FB - 1:FB]
            rr = [l_r, cr] + ([] if prev is None else [prev[1]])
            p.op("dve", lambda e, f=f, l=l, init=init: e.tensor_tensor_scan(out=f[:], data0=ones[:], data1=l[:], initial=init, op0=ALU.mult, op1=ALU.add), r=rr, w=[f_r])
            prev = (f, f_r)
            q, q_r = pp.get()
            n, n_r = pn.get()
            r_, r_r = r1.get()
            p.op("act", lambda e, q=q, f=f: e.copy(out=q[:, 0, :], in_=f[:]), r=[f_r], w=[q_r])
            p.op("dve", lambda e, r_=r_, f=f, q=q: e.tensor_tensor(out=r_[:], in0=f[:], in1=q[:, 0, :], op=ALU.subtract), r=[f_r, q_r], w=[r_r])
            p.op("act", lambda e, q=q, r_=r_: e.copy(out=q[:, 1, :], in_=r_[:]), r=[r_r], w=[q_r])
            p.op("dve", lambda e, r_=r_, q=q: e.tensor_tensor(out=r_[:], in0=r_[:], in1=q[:, 1, :], op=ALU.subtract), r=[r_r, q_r], w=[r_r])
            p.op("act", lambda e, q=q, r_=r_: e.copy(out=q[:, 2, :], in_=r_[:]), r=[r_r], w=[q_r])
            p.op("pool", lambda e, n=n, q=q: e.tensor_scalar_mul(out=n[:], in0=q[:], scalar1=-1.0), r=[q_r], w=[n_r])
            out_dma(p, dd["fpos"][:, :, cols], q[:], q_r, "fp_p")
            out_dma(p, dd["fneg"][:, :, cols], n[:], n_r, "fp_n")


def stage_attn(p, cx, cs, T, dd):
    nc = p.nc
    NQT = T // 512
    NKB = T // 128
    LB = min(2048, T)
    with p.scope() as es2:
        sb = p.sb
        cx.rot = [0, 1, 2, 3, 4, 5]
        obank = [(cx.ps[6], cx.psr[6]), (cx.ps[7], cx.psr[7])]
        mr = p.res("at_masks")
        onesb = sb("at_onesb", [128, 512], BF16)
        p.op("pool", lambda e: e.memset(onesb[:], 1.0), w=[mr])
        masks = []
        for o in range(4):
            m = sb("at_mask%d" % o, [128, 512], BF16)
            p.op("pool", lambda e, m=m, o=o: e.affine_select(out=m[:], in_=onesb[:], pattern=[[1, 512]], compare_op=ALU.is_ge, fill=0.0,
                                                          base=-128 * o, channel_multiplier=-1), r=[mr], w=[mr])
            masks.append(m)
        KA = Rot(p, "at_KA", [70, T], BF16, 2)
        QA = Rot(p, "at_QA", [70, T], BF16, 2)
        VP = Rot(p, "at_VP", [128, NKB, 65], BF16, 2)
        for rot in (KA, QA):
            for t, r in zip(rot.t, rot.r):
                p.op("pool", lambda e, t=t: e.memset(t[64:70, :], 1.0), w=[r])
        for t, r in zip(VP.t, VP.r):
            p.op("pool", lambda e, t=t: e.memset(t[:, :, 64:65], 1.0), w=[r])
        stg = Rot(p, "at_stg", [64, LB], F32, 3)
        pT = Rot(p, "at_pT", [128, 512], BF16, 7)
        clp = Rot(p, "at_clp", [128, 512], F32, 2)
        osb = Rot(p, "at_osb", [65, 512], F32, 2)
        rc = Rot(p, "at_rc", [65, 512], F32, 2)
        oo = Rot(p, "at_oo", [64, 512], F32, 2)
        ci = [0]
        DEPTH = 4
        pend = []
        nqt_done = [0]

        def flush_one():
            ob_, ob_r, VP_t, VP_r, kb, nkb, pt, pt_r, fin = pend.pop(0)
            p.op("pe", lambda e, ob_=ob_, kb=kb, pt=pt, nkb=nkb, VP_t=VP_t: e.matmul(ob_[0:65, 0:512], VP_t[:, kb, :], pt[:], start=(kb == 0), stop=(kb == nkb - 1)),
                 r=[VP_r, pt_r], w=[ob_r])
            if fin is not None:
                rows, qc = fin
                os_, os_r = osb.get()
                p.op("act", lambda e, os_=os_, ob_=ob_: e.copy(out=os_[:], in_=ob_[0:65, 0:512]), r=[ob_r], w=[os_r])
                rc_, rc_r = rc.get()
                p.op("dve", lambda e, rc_=rc_, os_=os_: e.reciprocal(out=rc_[64:65, :], in_=os_[64:65, :]), r=[os_r], w=[rc_r])
                ps, pr = cx.bank()
                p.op("pe", lambda e, ps=ps, rc_=rc_: e.matmul(ps[0:64, 0:512], cs.ones_f[64:65, 0:64], rc_[64:65, :], start=True, stop=True), r=[rc_r, cs.r], w=[pr])
                o_, o_r = oo.get()
                p.op("dve", lambda e, o_=o_, os_=os_, ps=ps: e.tensor_tensor(out=o_[:], in0=os_[0:64, :], in1=ps[0:64, 0:512], op=ALU.mult), r=[os_r, pr], w=[o_r])
                out_dma(p, dd["o"][rows, qc], o_[:], o_r, "at_o")

        for h in range(H):
            rows = slice(h * 64, (h + 1) * 64)
            KA_t, KA_r = KA.get()
            QA_t, QA_r = QA.get()
            VP_t, VP_r = VP.get()
            for bi in range(T // LB):
                cols = slice(bi * LB, (bi + 1) * LB)
                for src, dst, dst_r in ((dd["ksh"], KA_t, KA_r), (dd["q"], QA_t, QA_r)):
                    s, s_r = stg.get()
                    p.dma(s[:], src[rows, cols], w=[s_r])
                    ci[0] += 1
                    if ci[0] % 2 == 0:
                        p.op("act", lambda e, s=s, dst=dst, cols=cols: e.copy(out=dst[0:64, cols], in_=s[:]), r=[s_r], w=[dst_r])
                    else:
                        p.op("pool", lambda e, s=s, dst=dst, cols=cols: e.tensor_copy(out=dst[0:64, cols], in_=s[:]), r=[s_r], w=[dst_r])
                s, s_r = stg.get()
                p.dma(s[:], dd["vsh"][rows, cols], w=[s_r])
                GS = min(8, LB // 128)
                for g8 in range(LB // 128 // GS):
                    ps, pr = cx.bank()
                    for j in range(GS):
                        kb = g8 * GS + j
                        p.op("pe", lambda e, ps=ps, j=j, kb=kb, s=s: e.transpose(out=ps[:, j * 64:(j + 1) * 64], in_=s[:, kb * 128:(kb + 1) * 128], identity=cs.ident[0:64, 0:64]),
                             r=[s_r, cs.r], w=[pr])
                    kb0 = bi * (LB // 128) + g8 * GS
                    p.op("dve", lambda e, ps=ps, kb0=kb0, VP_t=VP_t, GS=GS: e.tensor_copy(out=VP_t[:, kb0:kb0 + GS, 0:64], in_=ps[:, 0:GS * 64].rearrange("p (j c) -> p j c", j=GS)),
                         r=[pr], w=[VP_r])
            p.dma(QA_t[64:67, :], dd["fpos"][h], w=[QA_r])
            p.dma(KA_t[67:70, :], dd["fneg"][h], w=[KA_r])
            for qt in range(NQT):
                qc = slice(qt * 512, (qt + 1) * 512)
                ob_, ob_r = obank[nqt_done[0] % 2]
                nqt_done[0] += 1
                nkb = 4 * (qt + 1)
                for kb in range(nkb):
                    ps, pr = cx.bank()
                    p.op("pe", lambda e, ps=ps, kb=kb, qc=qc, KA_t=KA_t, QA_t=QA_t: e.matmul(ps[:, 0:512], KA_t[:, kb * 128:(kb + 1) * 128], QA_t[:, qc], start=True, stop=True), r=[KA_r, QA_r], w=[pr])
                    pt, pt_r = pT.get()
                    if kb >= 4 * qt:
                        cl, cl_r = clp.get()
                        p.op("dve", lambda e, ps=ps, cl=cl: e.tensor_scalar_min(out=cl[:], in0=ps[:, 0:512], scalar1=20.0), r=[pr], w=[cl_r])
                        p.op("act", lambda e, cl=cl, pt=pt: e.activation(out=pt[:], in_=cl[:], func=AF.Exp), r=[cl_r], w=[pt_r])
                        mk_ = masks[kb - 4 * qt]
                        p.op("pool", lambda e, pt=pt, mk_=mk_: e.tensor_tensor(out=pt[:], in0=pt[:], in1=mk_[:], op=ALU.mult), r=[pt_r, mr], w=[pt_r])
                    else:
                        p.op("act", lambda e, ps=ps, pt=pt: e.activation(out=pt[:], in_=ps[:, 0:512], func=AF.Exp), r=[pr], w=[pt_r])
                    last = (kb == nkb - 1)
                    pend.append((ob_, ob_r, VP_t, VP_r, kb, nkb, pt, pt_r, (rows, qc) if last else None))
                    while len(pend) > DEPTH:
                        flush_one()
        while pend:
            flush_one()
        cx.rot = list(range(8))


def stage_wcast(p, cx, jobs):
    with p.scope() as es2:
        engs = ["act", "dve", "pool"]
        ci = 0
        stgs = {}
        for src, dst, bc in jobs:
            kci = src.shape[0] // 128
            key = (kci, bc)
            if key not in stgs:
                stgs[key] = (Rot(p, "wc_s%d_%d" % key, [128, kci, bc], F32, 2), Rot(p, "wc_b%d_%d" % key, [128, kci, bc], BF16, 2))
            srot, brot = stgs[key]
            v = src.rearrange("(kc p) m -> p kc m", p=128)
            for j in range(src.shape[1] // bc):
                s_, s_r = srot.get()
                b_, b_r = brot.get()
                p.dma(s_[:], v[:, :, j * bc:(j + 1) * bc], w=[s_r])
                copy_op(p, engs[ci % 3], b_[:], s_[:], r=[s_r], w=[b_r])
                ci += 1
                p.dma(dst[j], b_[:], r=[b_r], w=[p.res("wc_o")], key="o_" + b_r.name)


NV = 2 * len(RW_VECS) * KC + 2 * 2 * KC + 2 * KC + 8
W_SHAPES = {
    "mod_w": [4, D, 6 * D], "mlp_up": [4, D, DFF], "mlp_down": [4, DFF, D],
    "rw_wr": [2, D, D], "rw_wk": [2, D, D], "rw_wv": [2, D, D], "rw_wo": [2, D, D],
    "rw_w1": [2, D, 64], "rw_w2": [2, 64, D], "rw_a1": [2, D, 64], "rw_a2": [2, 64, D],
    "rw_g1": [2, D, 160], "rw_g2": [2, 160, D], "rw_v1": [1, D, 32], "rw_v2": [1, 32, D],
    "kv_mod_w": [D, 2 * D], "kv_w": [D, 2 * D + 16], "fx_wqg": [2, D, 2 * D], "fx_wo": [2, D, D],
}


def build_program(T, stages=None, dump=()):
    nc = bass.Bass("TRN2", target_bir_lowering=False)
    ein = lambda n, s, d=F32: nc.dram_tensor(n, list(s), d, kind="ExternalInput").ap()
    xT = ein("xT", [D, T])
    cT = ein("cT", [128, KC])
    modb = ein("modb", [128, NMODC])
    vecs_d = ein("vecs", [128, NV])
    nfb_d = ein("nfb", [16, 1])
    Wd = {n: ein(n, s) for n, s in W_SHAPES.items()}
    outT = nc.dram_tensor("outT", [D, T], F32, kind="ExternalOutput").ap()
    _cnt = [0]

    def idr(n, s, d=F32):
        _cnt[0] += 1
        return nc.dram_tensor("scr%02d_%s" % (_cnt[0], n), list(s), d).ap()
    dd = {n: idr(n, [D, T]) for n in ["xa", "xb", "r", "lw", "k", "v", "a", "b", "g", "bonus", "vfirst", "y", "ksh", "vsh", "q", "sig", "o"]}
    dd["logf"] = idr("logf", [16, T])
    dd["fpos"] = idr("fpos", [16, 3, T], BF16)
    dd["fneg"] = idr("fneg", [16, 3, T], BF16)
    dump_out = {n: nc.dram_tensor("dump_" + n, list(dd[n].shape), dd[n].dtype, kind="ExternalOutput").ap() for n in dump}
    with ExitStack() as es:
        p = Prog(nc, es)
        cx = Ctx(p)
        cs = Consts(p, cx)
        vec = p.sb("vecs_sb", [128, NV])
        vec_r = p.res("vecs")
        p.dma(vec[:], vecs_d, w=[vec_r])
        nfb = p.sb("nfb_sb", [16, 1])
        nfb_r = p.res("nfb")
        p.dma(nfb[:], nfb_d, w=[nfb_r])
        p.op("pool", lambda e: e.tensor_scalar_mul(out=nfb[:], in0=nfb[:], scalar1=-1.0), r=[nfb_r], w=[nfb_r])
        mod_sb = p.sb("mod_sb", [128, NMODC])
        mod_r = p.res("mod")
        stage_mod(p, cx, cT, Wd["mod_w"], Wd["kv_mod_w"], modb, mod_sb, mod_r)
        jobs = []
        Wb = {}

        def mkb(name, src, bc):
            kin, mm_ = src.shape
            t = nc.dram_tensor("wbf_" + name, [mm_ // bc, 128, kin // 128, bc], BF16).ap()
            jobs.append((src, t, bc))
            return t
        for i in range(4):
            Wb["up%d" % i] = mkb("up%d" % i, Wd["mlp_up"][i], 256)
            Wb["dn%d" % i] = mkb("dn%d" % i, Wd["mlp_down"][i], 128)
        for i in range(2):
            for n_ in ("wr", "wk", "wv", "wo"):
                Wb["%s%d" % (n_, i)] = mkb("%s%d" % (n_, i), Wd["rw_" + n_][i], 256)
            Wb["qg%d" % i] = mkb("qg%d" % i, Wd["fx_wqg"][i], 256)
            Wb["fo%d" % i] = mkb("fo%d" % i, Wd["fx_wo"][i], 256)
        Wb["kv"] = mkb("kv", Wd["kv_w"][:, 0:2 * D], 256)
        p.barrier()
        stage_wcast(p, cx, jobs)
        nrw = len(RW_VECS) * KC
        vcs = []
        for i in range(2):
            vcs.append({n: i * nrw + j * KC for j, n in enumerate(RW_VECS)})
        for i in range(2):
            vcs.append({"norm_mix_g": 2 * nrw + i * 2 * KC, "norm_mlp_g": 2 * nrw + i * 2 * KC + KC})
        c_kvg = 2 * nrw + 4 * KC
        c_fin = c_kvg + KC
        c_gain = c_fin + KC
        xcur, xnext = xT, dd["xa"]
        nstage = 0

        def want(name):
            return stages is None or name in stages

        for i in range(2):
            W = {"wr": Wb["wr%d" % i], "wk": Wb["wk%d" % i], "wv": Wb["wv%d" % i], "w1": Wd["rw_w1"][i], "w2": Wd["rw_w2"][i],
                 "a1": Wd["rw_a1"][i], "a2": Wd["rw_a2"][i], "g1": Wd["rw_g1"][i], "g2": Wd["rw_g2"][i]}
            if i > 0:
                W["v1"] = Wd["rw_v1"][0]
                W["v2"] = Wd["rw_v2"][0]
            if want("pre%d" % i):
                p.barrier()
                stage_pre_rwkv(p, cx, cs, T, i, xcur, vec, vec_r, vcs[i], mod_sb, mod_r, W, dd)
            if want("scan%d" % i):
                p.barrier()
                stage_scan(p, cx, cs, T, dd)
            if want("post%d" % i):
                p.barrier()
                stage_post(p, cx, cs, T, i, "rwkv", xcur, xnext, vec, vec_r, vcs[i], mod_sb, mod_r, Wb["wo%d" % i], Wb["up%d" % i], Wb["dn%d" % i], dd)
                xcur, xnext = xnext, (dd["xb"] if xnext is dd["xa"] else dd["xa"])
        if want("kv"):
            p.barrier()
            stage_kvq(p, cx, cs, T, "kv", 0, xcur, vec, vec_r, c_kvg, mod_sb, mod_r, Wb["kv"], vec[:, c_gain:c_gain + 1], vec_r, dd, fb_sb=nfb[:, 0:1], fb_r=nfb_r, Wf_d=Wd["kv_w"])
            p.barrier()
            stage_fprep(p, cx, T, dd)
        for j in range(2):
            i = 2 + j
            if want("preq%d" % i):
                p.barrier()
                stage_kvq(p, cx, cs, T, "q", i, xcur, vec, vec_r, vcs[i]["norm_mix_g"], mod_sb, mod_r, Wb["qg%d" % j], vec[:, c_gain + 1 + j:c_gain + 2 + j], vec_r, dd)
            if want("attn%d" % i):
                p.barrier()
                stage_attn(p, cx, cs, T, dd)
            if want("post%d" % i):
                p.barrier()
                last = (i == 3)
                stage_post(p, cx, cs, T, i, "fox", xcur, outT if last else xnext, vec, vec_r, vcs[i], mod_sb, mod_r, Wb["fo%d" % j], Wb["up%d" % i], Wb["dn%d" % i], dd,
                           final_g=vec[:, c_fin:c_fin + KC] if last else None)
                xcur, xnext = xnext, (dd["xb"] if xnext is dd["xa"] else dd["xa"])
        if dump:
            p.barrier()
            for n in dump:
                p.dma(dump_out[n], dd[n], w=[p.res("dump_" + n)])
        p.emit()
        stats = p.stats
    return nc, stats


def fm(v):
    return np.ascontiguousarray(np.asarray(v, np.float32).reshape(-1, 128).T)


def host_inputs(inp, b, T):
    vec_parts = []
    for i in range(2):
        tab = {"norm_mix_g": inp["norm_mix_g"][i], "norm_mlp_g": inp["norm_mlp_g"][i], "w0": inp["rw_w0"][i], "a0": inp["rw_a0"][i],
               "v0": inp["rw_v0"][0], "kk": inp["rw_kk"][i], "ka": inp["rw_ka"][i], "rk": inp["rw_rk"][i].reshape(-1),
               "lnw": inp["rw_lnw"][i], "lnb": inp["rw_lnb"][i]}
        for j in range(6):
            tab["mu%d" % j] = inp["rw_mu"][i, j]
        for n in RW_VECS:
            vec_parts.append(fm(tab[n]))
    for i in (2, 3):
        vec_parts.append(fm(inp["norm_mix_g"][i]))
        vec_parts.append(fm(inp["norm_mlp_g"][i]))
    vec_parts.append(fm(inp["kv_norm_g"]))
    vec_parts.append(fm(inp["final_g"]))
    gains = np.zeros((128, 8), np.float32)
    gains[:, 0] = np.tile(np.asarray(inp["kv_kg"], np.float32), 2)
    gains[:, 1] = np.tile(np.asarray(inp["fx_qg"][0], np.float32), 2)
    gains[:, 2] = np.tile(np.asarray(inp["fx_qg"][1], np.float32), 2)
    vec_parts.append(gains)
    vecs = np.ascontiguousarray(np.concatenate(vec_parts, axis=1))
    assert vecs.shape == (128, NV), vecs.shape
    modb = np.concatenate([fm(inp["mod_b"][i]) for i in range(4)] + [fm(inp["kv_mod_b"])], axis=1)
    m = {
        "xT": np.ascontiguousarray(np.asarray(inp["x"][b, :T], np.float32).T),
        "cT": fm(inp["c"][b]),
        "modb": np.ascontiguousarray(modb),
        "vecs": vecs,
        "nfb": np.ascontiguousarray(np.asarray(inp["kv_fb"], np.float32).reshape(16, 1)),
    }
    for n in W_SHAPES:
        m[n] = np.ascontiguousarray(np.asarray(inp[n], np.float32))
    return m


def kernel(**inputs):
    T = 8192
    nc, _ = build_program(T)
    inp = {k: np.asarray(v) for k, v in inputs.items()}
    in_maps = [host_inputs(inp, b, T) for b in range(2)]
    res = run_bass_kernel_spmd(nc, in_maps, core_ids=[0, 1])
    out = np.stack([np.asarray(res.results[b]["outT"]).T for b in range(2)])
    return np.ascontiguousarray(out.astype(np.float32))
```

```python
import numpy as np
from contextlib import ExitStack
import concourse.bass as bass
import concourse.mybir as mybir
from concourse.bass_utils import run_bass_kernel_spmd

F32 = mybir.dt.float32
BF16 = mybir.dt.bfloat16
ALU = mybir.AluOpType
AF = mybir.ActivationFunctionType
AX = mybir.AxisListType

SAME_ENGINE_SYNC = True


import types as _types


def _snapshot(fn):
    cl = fn.__closure__
    if not cl:
        return fn
    cells = []
    for c in cl:
        try:
            cells.append(_types.CellType(c.cell_contents))
        except ValueError:
            cells.append(c)
    g = _types.FunctionType(fn.__code__, fn.__globals__, fn.__name__, fn.__defaults__, tuple(cells))
    g.__kwdefaults__ = fn.__kwdefaults__
    return g


class Res:
    __slots__ = ("name", "w", "r", "excl")

    def __init__(self, name):
        self.name = name
        self.w = None
        self.r = []
        self.excl = False


class Op:
    __slots__ = ("eng", "fn", "deps", "is_dma", "sem", "count", "signal", "idx", "line", "alldeps")


class Prog:
    ENGS = ("pe", "act", "dve", "pool", "sp")

    def __init__(self, nc, es):
        self.nc = nc
        self.es = es
        self.ops = []
        self.dma_keys = {}
        self.nres = 0
        self._uid = 0
        self.fence = {}
        self.scopes = []
        self.barriers = []

    def sb(self, name, shape, dt=F32):
        es = self.scopes[-1] if self.scopes else self.es
        self._uid += 1
        return es.enter_context(self.nc.sbuf_tensor("%s_u%d" % (name, self._uid), list(shape), dt))

    def scope(self):
        import contextlib

        @contextlib.contextmanager
        def cm():
            with ExitStack() as es2:
                self.scopes.append(es2)
                try:
                    yield es2
                finally:
                    self.scopes.pop()
        return cm()

    def ps(self, name, shape, dt=F32):
        return self.es.enter_context(self.nc.psum_tensor(name, list(shape), dt))

    def res(self, name=None):
        self.nres += 1
        return Res(name or ("r%d" % self.nres))

    def resl(self, n, name="r"):
        return [self.res("%s%d" % (name, i)) for i in range(n)]

    def op(self, eng, fn, r=(), w=(), dma_key=None):
        o = Op()
        o.eng = eng
        o.fn = _snapshot(fn)
        o.is_dma = dma_key is not None
        o.signal = False
        o.sem = dma_key
        o.count = 0
        o.idx = len(self.ops)
        import sys as _sys
        o.line = _sys._getframe(1).f_lineno
        ex = [x for x in r if x.excl]
        if ex:
            w = list(w) + [x for x in ex if x not in w]
            r = [x for x in r if not x.excl]
        deps = {}
        for x in r:
            if x.w is not None:
                deps[x.w.idx] = (x.w, "raw")
        for x in w:
            if x.w is not None:
                deps[x.w.idx] = (x.w, "waw")
            for rd in x.r:
                if rd.idx not in deps:
                    deps[rd.idx] = (rd, "war")
        fd = []
        if self.fence.get(eng):
            for d in self.fence[eng]:
                if d.is_dma or d.eng != eng or eng == "pool":
                    fd.append(d)
            self.fence[eng] = None
        for d, kind in deps.values():
            if d is o:
                continue
            if d.is_dma:
                fd.append(d)
            elif d.eng != eng:
                fd.append(d)
            else:
                if o.is_dma or eng == "pool" or (SAME_ENGINE_SYNC and kind != "war" and eng != "pe"):
                    fd.append(d)
        o.deps = fd
        o.alldeps = [(d.idx, d.eng, d.line, k) for d, k in deps.values()]
        for d in fd:
            d.signal = True
        for x in r:
            x.r.append(o)
        for x in w:
            x.w = o
            x.r = []
        self.ops.append(o)
        return o

    def barrier(self):
        self.barriers.append(len(self.ops))
        last = {}
        for o in self.ops:
            if o.is_dma:
                last[("dma", o.sem)] = o
            else:
                last[o.eng] = o
        ops = list(last.values())
        for e in self.ENGS:
            self.fence[e] = list(ops)

    def dma(self, out, in_, r=(), w=(), key=None, q="sp"):
        if key is None:
            key = w[0].name
        return self.op(q, lambda e, out=out, in_=in_: e.dma_start(out=out, in_=in_), r=r, w=w, dma_key=key)

    def emit(self):
        nc = self.nc
        es = self.es
        ROT = 60000
        eng_sems = {e: [] for e in self.ENGS}
        dma_sem = {}
        dma_cnt = {}
        dma_uses = {}
        cnt = {e: 0 for e in self.ENGS}
        nsem = 0
        all_dma_sems = []
        final_cnt = {}
        free_sems = []
        bset = set(self.barriers)
        for o in self.ops:
            if o.idx in bset:
                for k in list(dma_sem.keys()):
                    free_sems.append((dma_sem.pop(k), dma_cnt.pop(k)))
            if o.is_dma:
                k = o.sem
                if k in dma_sem and dma_cnt[k] > 60000:
                    dma_sem.pop(k)
                    dma_cnt.pop(k)
                if k not in dma_sem:
                    while free_sems and free_sems[0][1] > 50000:
                        free_sems.pop(0)
                    if free_sems:
                        dma_sem[k], dma_cnt[k] = free_sems.pop(0)
                    else:
                        dma_sem[k] = es.enter_context(nc.semaphore("dsem_%d" % nsem))
                        dma_cnt[k] = 0
                        nsem += 1
                        all_dma_sems.append(dma_sem[k])
                dma_cnt[k] += 16
                final_cnt[dma_sem[k].num] = (dma_sem[k], dma_cnt[k])
                o.sem = dma_sem[k]
                o.count = dma_cnt[k]
            elif o.signal:
                n = cnt[o.eng]
                cnt[o.eng] += 1
                si = n // ROT
                if si >= len(eng_sems[o.eng]):
                    eng_sems[o.eng].append(es.enter_context(nc.semaphore("sem_%s%d" % (o.eng, si))))
                    nsem += 1
                o.sem = eng_sems[o.eng][si]
                o.count = n % ROT + 1
        self.stats = dict(cnt)
        self.stats["n_ops"] = len(self.ops)
        self.stats["n_sems"] = nsem
        per = {e: [o for o in self.ops if o.eng == e] for e in self.ENGS}
        final_dma = list(final_cnt.values())

        def run(e, ename):
            seen = {}
            nw = 0
            for o in per[ename]:
                need = {}
                for d in o.deps:
                    s = d.sem
                    if need.get(s.num, (None, 0))[1] < d.count:
                        need[s.num] = (s, d.count)
                for s, c in need.values():
                    if seen.get(s.num, 0) < c:
                        e.wait_ge(s, c)
                        seen[s.num] = c
                        nw += 1
                ins = o.fn(e)
                if o.is_dma:
                    ins.then_inc(o.sem, 16)
                elif o.signal:
                    ins.then_inc(o.sem, 1)
            if ename == "sp":
                for s, c in final_dma:
                    e.wait_ge(s, c)
            self.stats["waits_" + ename] = nw

        with nc.Block() as block:
            block.sync(lambda e: run(e, "sp"))
            block.tensor(lambda e: run(e, "pe"))
            block.scalar(lambda e: run(e, "act"))
            block.vector(lambda e: run(e, "dve"))
            block.gpsimd(lambda e: run(e, "pool"))


D = 1024
KC = 8
NT = 2048
TG = 512
NTG = NT // TG
DFF = 4096
FC = DFF // 128
NORM_EPS = 1e-6
GN_EPS = 64e-5


class Ctx:
    def __init__(self, p):
        self.p = p
        nc = p.nc
        self.ps = [p.ps("psb%d" % i, [128, 512], F32) for i in range(8)]
        self.psr = [p.res("psb%d" % i) for i in range(8)]
        for r_ in self.psr:
            r_.excl = True
        self.psi = 0
        self.rot = list(range(8))
        self.ones_bf = p.sb("ones_bf", [128, 128], BF16)
        self.r_const = p.res("consts")
        self.eps_t = p.sb("eps_t", [128, 4], F32)
        self.ones_f32 = p.sb("ones_f32c", [128, 128], F32)
        p.op("pool", lambda e: e.memset(self.ones_f32[:], 1.0), w=[self.r_const])
        p.op("pool", lambda e: e.memset(self.ones_bf[:], 1.0), w=[self.r_const])
        p.op("pool", lambda e: e.memset(self.eps_t[:, 0:1], NORM_EPS), w=[self.r_const])
        p.op("pool", lambda e: e.memset(self.eps_t[:, 1:2], GN_EPS), w=[self.r_const])
        p.op("pool", lambda e: e.memset(self.eps_t[:, 2:3], 1.0), w=[self.r_const])
        p.op("pool", lambda e: e.memset(self.eps_t[:, 3:4], 0.0), w=[self.r_const])

    def bank(self):
        self.psi = (self.psi + 1) % len(self.rot)
        i = self.rot[self.psi]
        return self.ps[i], self.psr[i]


class WStream:
    def __init__(self, p, name, shape, nbuf=2, cast_engs=("pool",), direct=False):
        self.p = p
        self.shape = shape
        self.nbuf = nbuf
        if not direct:
            self.stg = [p.sb("%s_stg%d" % (name, i), shape, F32) for i in range(nbuf)]
            self.stg_r = [p.res("%s_stg%d" % (name, i)) for i in range(nbuf)]
        self.bf = [p.sb("%s_bf%d" % (name, i), shape, BF16) for i in range(nbuf)]
        self.bf_r = [p.res("%s_bf%d" % (name, i)) for i in range(nbuf)]
        self.i = 0
        self.cast_engs = cast_engs
        self.ci = 0

    def load(self, src_ap, sl=None):
        p = self.p
        i = self.i
        self.i = (i + 1) % self.nbuf
        bf, br = self.bf[i], self.bf_r[i]
        idx = sl if sl is not None else tuple(slice(None) for _ in self.shape)
        if src_ap.dtype == BF16:
            p.dma(bf[idx], src_ap, w=[br])
            return bf, br
        stg, sr = self.stg[i], self.stg_r[i]
        p.dma(stg[idx], src_ap, w=[sr])
        ce = self.cast_engs[self.ci % len(self.cast_engs)]
        self.ci += 1
        copy_op(p, ce, bf[idx], stg[idx], r=[sr], w=[br])
        return bf, br


def copy_op(p, eng, out, in_, r, w):
    if eng == "act":
        return p.op("act", lambda e: e.copy(out=out, in_=in_), r=r, w=w)
    return p.op(eng, lambda e: e.tensor_copy(out=out, in_=in_), r=r, w=w)


def emit_norm(p, cx, x_sb, x_r, gmul, shift, vec_r, h_out, h_r, scr):
    ts = slice(0, TG)
    sq, sq_r = scr["sq"]
    rstd, rstd_r = scr["rstd"]
    tmp, tmp_r = scr["tmp"]
    p.op("act", lambda e: e.activation(out=sq[:], in_=x_sb[:, :, ts], func=AF.Square), r=[x_r], w=[sq_r])
    ps, pr = cx.bank()
    for kc in range(KC):
        p.op("pe", lambda e, kc=kc: e.matmul(ps[:, 0:TG], cx.ones_bf[:], sq[:, kc, :], start=(kc == 0), stop=(kc == KC - 1)),
             r=[sq_r, cx.r_const], w=[pr])
    p.op("act", lambda e: e.activation(out=rstd[:], in_=ps[:, 0:TG], func=AF.Sqrt, bias=cx.eps_t[:, 0:1], scale=1.0 / D),
         r=[pr, cx.r_const], w=[rstd_r])
    p.op("dve", lambda e: e.reciprocal(out=rstd[:], in_=rstd[:]), r=[rstd_r], w=[rstd_r])
    for kc in range(KC):
        if shift is None:
            p.op("dve", lambda e, kc=kc: e.scalar_tensor_tensor(out=h_out[:, kc, :], in0=x_sb[:, kc, ts], scalar=gmul[:, kc:kc + 1],
                                                               in1=rstd[:], op0=ALU.mult, op1=ALU.mult),
                 r=[x_r, rstd_r, vec_r], w=[h_r])
        else:
            t2, t2r = tmp[kc % 2], tmp_r[kc % 2]
            p.op("dve", lambda e, kc=kc, t2=t2: e.scalar_tensor_tensor(out=t2[:], in0=x_sb[:, kc, ts], scalar=gmul[:, kc:kc + 1],
                                                                      in1=rstd[:], op0=ALU.mult, op1=ALU.mult),
                 r=[x_r, rstd_r, vec_r], w=[t2r])
            p.op("act", lambda e, kc=kc, t2=t2: e.activation(out=h_out[:, kc, :], in_=t2[:], func=AF.Identity,
                                                            bias=shift[:, kc:kc + 1], scale=1.0),
                 r=[t2r, vec_r], w=[h_r])


def norm_scratch(p, name):
    return {
        "sq": (p.sb(name + "_sq", [128, KC, TG], BF16), p.res(name + "_sq")),
        "rstd": (p.sb(name + "_rstd", [128, TG], F32), p.res(name + "_rstd")),
        "tmp": ([p.sb(name + "_tmp%d" % i, [128, TG], F32) for i in range(2)], [p.res(name + "_tmp%d" % i) for i in range(2)]),
    }


def emit_mlp(p, cx, xt, x_r, gm, gm_r, shift, vec_r, up_d, down_d, st):
    h, h_r, a, a_r, relu, relu_r, wu, wd, scr = st["h"], st["h_r"], st["a"], st["a_r"], st["relu"], st["relu_r"], st["wu"], st["wd"], st["scr"]
    emit_norm(p, cx, xt, x_r, gm[:, 0:KC], shift, vec_r, h, h_r, scr)
    for f2 in range(FC // 2):
        wb, wr = wu.load(up_d[f2])
        for fi in range(2):
            fc = f2 * 2 + fi
            ps, pr = cx.bank()
            for kc in range(KC):
                p.op("pe", lambda e, kc=kc, ps=ps, wb=wb, fi=fi: e.matmul(ps[:, 0:TG], wb[:, kc, fi * 128:(fi + 1) * 128], h[:, kc, :],
                                                                      start=(kc == 0), stop=(kc == KC - 1)),
                     r=[wr, h_r], w=[pr])
            ri = st["ri"]
            st["ri"] += 1
            rl, rr = relu[ri % 2], relu_r[ri % 2]
            p.op("act", lambda e, ps=ps, rl=rl: e.activation(out=rl[:], in_=ps[:, 0:TG], func=AF.Relu), r=[pr], w=[rr])
            p.op("dve", lambda e, ps=ps, rl=rl, fc=fc: e.scalar_tensor_tensor(
                out=a[:, fc, :], in0=ps[:, 0:TG], scalar=0.0, in1=rl[:], op0=ALU.max, op1=ALU.mult),
                r=[pr, rr], w=[a_r])
    for oc in range(KC):
        ps, pr = cx.bank()
        wb, wr = wd.load(down_d[oc])
        for fc in range(FC):
            p.op("pe", lambda e, fc=fc, ps=ps, wb=wb: e.matmul(ps[:, 0:TG], wb[:, fc, :], a[:, fc, :],
                                                             start=(fc == 0), stop=(fc == FC - 1)),
                 r=[wr, a_r], w=[pr])
        xs = xt[:, oc, :]
        p.op("dve", lambda e, ps=ps, xs=xs, oc=oc: e.scalar_tensor_tensor(out=xs, in0=ps[:, 0:TG], scalar=gm[:, KC + oc:KC + oc + 1],
                                                                      in1=xs, op0=ALU.mult, op1=ALU.add),
             r=[pr, gm_r, x_r], w=[x_r])


def mlp_state(p):
    return {
        "h": p.sb("mlp_h", [128, KC, TG], BF16), "h_r": p.res("mlp_h"),
        "a": p.sb("mlp_a", [128, FC, TG], BF16), "a_r": p.res("mlp_a"),
        "relu": [p.sb("mlp_relu%d" % i, [128, TG], F32) for i in range(2)], "relu_r": p.resl(2, "mlp_relu"),
        "wu": WStream(p, "wup", [128, KC, 256], nbuf=3, direct=True),
        "wd": WStream(p, "wdn", [128, FC, 128], nbuf=2, direct=True),
        "scr": norm_scratch(p, "mlpn"), "ri": 0,
    }


def mod_gm(p, name, vec, vec_r, col_g, col_sc, col_gt):
    gm = p.sb(name, [128, 2 * KC], F32)
    gm_r = p.res(name)
    p.op("dve", lambda e: e.scalar_tensor_tensor(out=gm[:, 0:KC], in0=vec[:, col_sc:col_sc + KC], scalar=1.0,
                                                in1=vec[:, col_g:col_g + KC], op0=ALU.add, op1=ALU.mult), r=[vec_r], w=[gm_r])
    if col_gt is not None:
        p.op("dve", lambda e: e.tensor_scalar_add(out=gm[:, KC:2 * KC], in0=vec[:, col_gt:col_gt + KC], scalar1=1.0), r=[vec_r], w=[gm_r])
    return gm, gm_r


import os
PRE_STOP = int(os.environ.get('PRE_STOP', '99'))
USE_F32R = os.environ.get("USE_F32R", "1") == "1"


def R32(ap):
    return ap.bitcast(mybir.dt.float32r) if USE_F32R else ap


H = 16
HD = 64
NMODC = 208
LW_SCALE = -0.6065306597126334


def vec_cols(names):
    return {n: i * KC for i, n in enumerate(names)}


RW_VECS = ["norm_mix_g", "norm_mlp_g", "mu0", "mu1", "mu2", "mu3", "mu4", "mu5", "w0", "a0", "v0", "kk", "ka", "rk", "lnw", "lnb"]
FX_VECS = ["norm_mix_g", "norm_mlp_g"]


class Consts:
    def __init__(self, p, cx):
        self.r = p.res("consts2")
        self.bo1 = p.sb("bo1", [128, 128], F32)
        self.bo64 = p.sb("bo64", [128, 128], F32)
        self.ident = p.sb("ident", [128, 128], F32)
        self.ones_f = p.sb("ones_f", [128, 128], F32)
        for t, v in ((self.bo1, 1.0), (self.bo64, 1.0 / 64)):
            p.op("pool", lambda e, t=t: e.memset(t[:], 0.0), w=[self.r])
            p.op("pool", lambda e, t=t, v=v: e.memset(t[0:64, 0:64], v), w=[self.r])
            p.op("pool", lambda e, t=t, v=v: e.memset(t[64:128, 64:128], v), w=[self.r])
        p.op("pool", lambda e: e.memset(self.ones_f[:], 1.0), w=[self.r])
        p.op("pool", lambda e: e.affine_select(out=self.ident[:], in_=self.ones_f[:], pattern=[[1, 128]], compare_op=ALU.is_equal,
                                               fill=0.0, base=0, channel_multiplier=-1), r=[self.r], w=[self.r])


def stage_mod(p, cx, cT_d, modw_d, kvmodw_d, modb_d, mod_sb, mod_r):
    with p.scope() as es2:
        nc = p.nc
        sb = p.sb
        cT = sb("mod_cT", [128, KC])
        ca = sb("mod_ca", [128, KC, 128])
        bias = sb("mod_bias", [128, NMODC])
        r_c = p.res("mod_c")
        r_b = p.res("mod_b")
        p.dma(cT[:], cT_d, w=[r_c])
        p.dma(bias[:], modb_d, w=[r_b])
        sig = sb("mod_sig", [128, KC])
        p.op("act", lambda e: e.activation(out=sig[:], in_=cT[:], func=AF.Sigmoid), r=[r_c], w=[r_c])
        p.op("dve", lambda e: e.tensor_tensor(out=sig[:], in0=cT[:], in1=sig[:], op=ALU.mult), r=[r_c], w=[r_c])
        for kc in range(KC):
            p.op("dve", lambda e, kc=kc: e.tensor_scalar_mul(out=ca[:, kc, :], in0=cx_ones(cx), scalar1=sig[:, kc:kc + 1]), r=[r_c, cx.r_const], w=[r_c])
        wst = [sb("mod_w%d" % i, [128, KC, 512]) for i in range(2)]
        wr = p.resl(2, "mod_w")
        blocks = []
        for i in range(4):
            v = modw_d[i].rearrange("(kc p) m -> p kc m", p=128)
            for j in range(12):
                blocks.append((v[:, :, j * 512:(j + 1) * 512], i * 48 + j * 4))
        v = kvmodw_d.rearrange("(kc p) m -> p kc m", p=128)
        for j in range(4):
            blocks.append((v[:, :, j * 512:(j + 1) * 512], 192 + j * 4))
        for bi, (src, c0) in enumerate(blocks):
            w, r = wst[bi % 2], wr[bi % 2]
            p.dma(w[:], src, w=[r])
            ps, pr = cx.bank()
            for j in range(4):
                for kc in range(KC):
                    p.op("pe", lambda e, w=w, j=j, kc=kc, ps=ps: e.matmul(ps[:, j * 128:(j + 1) * 128], w[:, kc, j * 128:(j + 1) * 128], ca[:, kc, :],
                                                                       start=(kc == 0), stop=(kc == KC - 1)),
                         r=[r, r_c], w=[pr])
            psv = ps[:, 0:512].rearrange("p (j c) -> p j c", j=4)
            p.op("dve", lambda e, psv=psv, c0=c0: e.tensor_tensor(out=mod_sb[:, c0:c0 + 4], in0=psv[:, :, 0], in1=bias[:, c0:c0 + 4], op=ALU.add), r=[pr, r_b], w=[mod_r])


def cx_ones(cx):
    return cx.ones_f32[:]


class ProjW:
    def __init__(self, p, name):
        self.ws = WStream(p, name, [128, KC, 256], nbuf=3, direct=True)
        self.p = p

    def load(self, W_d, c0, n=256):
        return self.ws.load(W_d[c0 // 256])


def proj_mm(p, cx, wb, wr, off, h, h_r, ncols=128):
    ps, pr = cx.bank()
    for kc in range(KC):
        p.op("pe", lambda e, kc=kc: e.matmul(ps[0:ncols, 0:TG], wb[:, kc, off:off + ncols], h[:, kc, :], start=(kc == 0), stop=(kc == KC - 1)),
             r=[wr, h_r], w=[pr])
    return ps, pr


def head_sum(p, cx, cs, mat, src, src_r):
    ps, pr = cx.bank()
    p.op("pe", lambda e: e.matmul(ps[:, 0:TG], mat[:], src[:], start=True, stop=True), r=[src_r, cs.r], w=[pr])
    return ps, pr


class Rot:
    def __init__(self, p, name, shape, dt, n):
        self.t = [p.sb("%s%d" % (name, i), shape, dt) for i in range(n)]
        self.r = [p.res("%s%d" % (name, i)) for i in range(n)]
        self.i = 0

    def get(self):
        i = self.i
        self.i = (i + 1) % len(self.t)
        return self.t[i], self.r[i]


OUT_Q = os.environ.get("OUT_Q", "sp")


def out_dma(p, dst, src, src_r, key, dst_r=None):
    p.dma(dst, src, r=[src_r], w=[dst_r if dst_r is not None else p.res("o_" + key)], key="o_" + src_r.name, q=OUT_Q)


def stage_pre_rwkv(p, cx, cs, T, layer, x_d, vec, vec_r, vc, mod_sb, mod_r, W, dd):
    nc = p.nc
    NTL = T // TG
    mc = layer * 48
    with p.scope() as es2:
        sb = p.sb
        gm = sb("pr_gm", [128, 2 * KC])
        gm_r = p.res("pr_gm")
        g0 = vc["norm_mix_g"]
        p.op("dve", lambda e: e.scalar_tensor_tensor(out=gm[:, 0:KC], in0=mod_sb[:, mc + 8:mc + 16], scalar=1.0, in1=vec[:, g0:g0 + KC],
                                                    op0=ALU.add, op1=ALU.mult), r=[vec_r, mod_r], w=[gm_r])
        ka0 = vc["ka"]
        p.op("dve", lambda e: e.tensor_scalar(out=gm[:, KC:2 * KC], in0=vec[:, ka0:ka0 + KC], scalar1=-1.0, scalar2=1.0, op0=ALU.mult, op1=ALU.add),
             r=[vec_r], w=[gm_r])
        shift = mod_sb[:, mc:mc + 8]
        lw_r = p.res("pr_lora")
        specs = [("a1", W["a1"].rearrange("(kc p) m -> p kc m", p=128), [128, KC, 64]),
                 ("w1", W["w1"].rearrange("(kc p) m -> p kc m", p=128), [128, KC, 64]),
                 ("g1", W["g1"].rearrange("(kc p) m -> p kc m", p=128), [128, KC, 160]),
                 ("a2", W["a2"], [64, D]), ("w2", W["w2"], [64, D]),
                 ("g2a", W["g2"][0:128, :], [128, D]), ("g2b", W["g2"][128:160, :], [32, D])]
        if layer > 0:
            specs += [("v1", W["v1"].rearrange("(kc p) m -> p kc m", p=128), [128, KC, 32]), ("v2", W["v2"], [32, D])]
        lt_ = {n: sb("prb_" + n, shp, BF16) for n, _, shp in specs}
        with p.scope() as stg_es:
            for n, src, shp in specs:
                st = p.sb("prs_" + n, shp)
                rr = p.res("prs_" + n)
                p.dma(st[:], src, w=[rr])
                p.op("pool", lambda e, n=n, st=st: e.tensor_copy(out=lt_[n][:], in_=st[:]), r=[rr], w=[lw_r])
        a1, w1, g1, a2, w2, g2a, g2b = (lt_[n] for n in ("a1", "w1", "g1", "a2", "w2", "g2a", "g2b"))
        if layer > 0:
            v1, v2 = lt_["v1"], lt_["v2"]
        p.barrier()
        xt = Rot(p, "pr_x", [128, KC, TG], F32, 1)
        hh = sb("pr_h", [128, KC, TG + 1])
        hh_r = p.res("pr_h")
        xx = sb("pr_xx", [128, KC, TG], BF16)
        xx_r = p.res("pr_xx")
        xj = Rot(p, "pr_xj", [128, KC, TG], BF16, 1)
        a_t = sb("pr_a", [128, KC, TG])
        a_r = p.res("pr_a")
        k_t = sb("pr_k", [128, KC, TG])
        k_r = p.res("pr_k")
        v_t = sb("pr_v", [128, KC, TG])
        v_r = p.res("pr_v")
        scr = norm_scratch(p, "prn")
        pw = ProjW(p, "pr_w")
        f1 = Rot(p, "pr_f1", [128, TG], F32, 2)
        f2 = Rot(p, "pr_f2", [128, TG], F32, 2)
        ob = Rot(p, "pr_ob", [128, TG], F32, 3)
        obn = Rot(p, "pr_obn", [128, TG], F32, int(os.environ.get("OBN", "2")))
        if os.environ.get("PRE_INIT"):
            for t_, r_ in zip(obn.t, obn.r):
                p.op("pool", lambda e, t_=t_: e.memset(t_[:], 0.0), w=[r_])
        lt = Rot(p, "pr_lt", [128, TG], BF16, 2)
        xv = x_d.rearrange("(c p) t -> p c t", p=128)
        p.op("pool", lambda e: e.memset(hh[:, :, 0:1], 0.0), w=[hh_r])
        if os.environ.get("PRE_PAD"):
            dr = p.res("padr")
            for i_ in range(int(os.environ["PRE_PAD"])):
                p.op("dve", lambda e: e.tensor_scalar_mul(out=xx[:, 0, 0:8], in0=xx[:, 0, 0:8], scalar1=1.0), r=[dr], w=[dr])

        def mix_dve(j):
            t, r = xj.get()
            m0 = vc["mu%d" % j]
            for kc in range(KC):
                p.op("dve", lambda e, kc=kc: e.scalar_tensor_tensor(
                    out=t[:, kc, :], in0=xx[:, kc, :], scalar=vec[:, m0 + kc:m0 + kc + 1], in1=hh[:, kc, 1:TG + 1], op0=ALU.mult, op1=ALU.add),
                    r=[xx_r, hh_r, vec_r], w=[r])
            return t, r

        def lora1(xm, xm_r, w1t, R, func):
            ps, pr = cx.bank()
            for kc in range(KC):
                p.op("pe", lambda e, kc=kc: e.matmul(ps[0:R, 0:TG], w1t[:, kc, 0:R], xm[:, kc, :], start=(kc == 0), stop=(kc == KC - 1)),
                     r=[lw_r, xm_r], w=[pr])
            t, r = lt.get()
            p.op("act", lambda e: e.activation(out=t[0:R, :], in_=ps[0:R, 0:TG], func=func), r=[pr], w=[r])
            return t, r

        for ti in range(NTL):
            cols = slice(ti * TG, (ti + 1) * TG)
            x_t, x_r = xt.get()
            p.dma(x_t[:], xv[:, :, cols], w=[x_r])
            if ti > 0:
                p.op("pool", lambda e: e.tensor_copy(out=hh[:, :, 0:1], in_=hh[:, :, TG:TG + 1]), r=[hh_r], w=[hh_r])
            emit_norm(p, cx, x_t, x_r, gm[:, 0:KC], shift, gm_r, hh[:, :, 1:TG + 1], hh_r, scr)
            p.op("dve", lambda e: e.tensor_tensor(out=xx[:], in0=hh[:, :, 0:TG], in1=hh[:, :, 1:TG + 1], op=ALU.subtract), r=[hh_r], w=[xx_r])
            if PRE_STOP <= 0:
                continue
            xm, xm_r = mix_dve(4)
            ta, ta_r = lora1(xm, xm_r, a1, 64, AF.Identity)
            for oc in range(KC):
                ps, pr = cx.bank()
                p.op("pe", lambda e, oc=oc, ps=ps: e.matmul(ps[:, 0:TG], a2[:, oc * 128:(oc + 1) * 128], ta[0:64, :], start=True, stop=True),
                     r=[lw_r, ta_r], w=[pr])
                c0 = vc["a0"] + oc
                p.op("act", lambda e, oc=oc, ps=ps, c0=c0: e.activation(out=a_t[:, oc, :], in_=ps[:, 0:TG], func=AF.Sigmoid, bias=vec[:, c0:c0 + 1], scale=1.0),
                     r=[pr, vec_r], w=[a_r])
            if PRE_STOP <= 1:
                continue
            xm, xm_r = mix_dve(2)
            for o2 in range(KC // 2):
                wb, wr = pw.load(W["wk"], o2 * 256)
                for oi in range(2):
                    oc = o2 * 2 + oi
                    ps, pr = proj_mm(p, cx, wb, wr, oi * 128, xm, xm_r)
                    kkr, kkr_r = f1.get()
                    ck = vc["kk"] + oc
                    p.op("dve", lambda e, ps=ps, kkr=kkr, ck=ck: e.tensor_scalar_mul(out=kkr[:], in0=ps[:, 0:TG], scalar1=vec[:, ck:ck + 1]), r=[pr, vec_r], w=[kkr_r])
                    sq, sq_r = f2.get()
                    p.op("act", lambda e, kkr=kkr, sq=sq: e.activation(out=sq[:], in_=kkr[:], func=AF.Square), r=[kkr_r], w=[sq_r])
                    ps2, pr2 = head_sum(p, cx, cs, cs.bo1, sq, sq_r)
                    rn, rn_r = f2.get()
                    p.op("act", lambda e, ps2=ps2, rn=rn: e.activation(out=rn[:], in_=ps2[:, 0:TG], func=AF.Sqrt), r=[pr2], w=[rn_r])
                    p.op("dve", lambda e, rn=rn: e.tensor_scalar_max(out=rn[:], in0=rn[:], scalar1=1e-12), r=[rn_r], w=[rn_r])
                    p.op("dve", lambda e, rn=rn: e.reciprocal(out=rn[:], in_=rn[:]), r=[rn_r], w=[rn_r])
                    nk, nk_r = ob.get()
                    p.op("dve", lambda e, nk=nk, kkr=kkr, rn=rn: e.scalar_tensor_tensor(out=nk[:], in0=kkr[:], scalar=-1.0, in1=rn[:], op0=ALU.mult, op1=ALU.mult),
                         r=[kkr_r, rn_r], w=[nk_r])
                    out_dma(p, dd["a"][oc * 128:(oc + 1) * 128, cols], nk[:], nk_r, "pr_oa")
                    bt, bt_r = ob.get()
                    p.op("dve", lambda e, bt=bt, nk=nk, oc=oc: e.scalar_tensor_tensor(out=bt[:], in0=nk[:], scalar=-1.0, in1=a_t[:, oc, :], op0=ALU.mult, op1=ALU.mult),
                         r=[nk_r, a_r], w=[bt_r])
                    out_dma(p, dd["b"][oc * 128:(oc + 1) * 128, cols], bt[:], bt_r, "pr_ob")
                    tm, tm_r = f1.get()
                    cka = vc["ka"] + oc
                    p.op("dve", lambda e, tm=tm, oc=oc, cka=cka: e.tensor_scalar(out=tm[:], in0=a_t[:, oc, :], scalar1=vec[:, cka:cka + 1],
                                                                             scalar2=gm[:, KC + oc:KC + oc + 1], op0=ALU.mult, op1=ALU.add),
                         r=[a_r, vec_r, gm_r], w=[tm_r])
                    p.op("dve", lambda e, tm=tm, ps=ps, oc=oc: e.tensor_tensor(out=k_t[:, oc, :], in0=ps[:, 0:TG], in1=tm[:], op=ALU.mult),
                         r=[pr, tm_r], w=[k_r])
            out_dma(p, dd["k"].rearrange("(c p) t -> p c t", p=128)[:, :, cols], k_t[:], k_r, "pr_ok")
            if PRE_STOP <= 2:
                continue
            xm, xm_r = mix_dve(3)
            if layer > 0:
                tv, tv_r = lora1(xm, xm_r, v1, 32, AF.Identity)
            for o2 in range(KC // 2):
                wb, wr = pw.load(W["wv"], o2 * 256)
                for oi in range(2):
                    oc = o2 * 2 + oi
                    ps, pr = proj_mm(p, cx, wb, wr, oi * 128, xm, xm_r)
                    if layer == 0:
                        p.op("act", lambda e, ps=ps, oc=oc: e.copy(out=v_t[:, oc, :], in_=ps[:, 0:TG]), r=[pr], w=[v_r])
                    else:
                        ps2, pr2 = cx.bank()
                        p.op("pe", lambda e, oc=oc, ps2=ps2: e.matmul(ps2[:, 0:TG], v2[:, oc * 128:(oc + 1) * 128], tv[0:32, :], start=True, stop=True),
                             r=[lw_r, tv_r], w=[pr2])
                        gt, gt_r = f1.get()
                        c0 = vc["v0"] + oc
                        p.op("act", lambda e, gt=gt, ps2=ps2, c0=c0: e.activation(out=gt[:], in_=ps2[:, 0:TG], func=AF.Sigmoid, bias=vec[:, c0:c0 + 1], scale=1.0),
                             r=[pr2, vec_r], w=[gt_r])
                        vf, vf_r = f2.get()
                        p.dma(vf[:], dd["vfirst"][oc * 128:(oc + 1) * 128, cols], w=[vf_r])
                        p.op("dve", lambda e, vf=vf, ps=ps: e.tensor_tensor(out=vf[:], in0=vf[:], in1=ps[:, 0:TG], op=ALU.subtract), r=[vf_r, pr], w=[vf_r])
                        p.op("dve", lambda e, vf=vf, gt=gt: e.tensor_tensor(out=vf[:], in0=vf[:], in1=gt[:], op=ALU.mult), r=[vf_r, gt_r], w=[vf_r])
                        p.op("dve", lambda e, vf=vf, ps=ps, oc=oc: e.tensor_tensor(out=v_t[:, oc, :], in0=vf[:], in1=ps[:, 0:TG], op=ALU.add), r=[vf_r, pr], w=[v_r])
            out_dma(p, dd["v"].rearrange("(c p) t -> p c t", p=128)[:, :, cols], v_t[:], v_r, "pr_ov")
            if layer == 0:
                out_dma(p, dd["vfirst"].rearrange("(c p) t -> p c t", p=128)[:, :, cols], v_t[:], v_r, "pr_ovf")
            if PRE_STOP <= 3:
                continue
            xm, xm_r = mix_dve(0)
            for o2 in range(KC // 2):
                wb, wr = pw.load(W["wr"], o2 * 256)
                for oi in range(2):
                    oc = o2 * 2 + oi
                    ps, pr = proj_mm(p, cx, wb, wr, oi * 128, xm, xm_r)
                    rt, rt_r = ob.get()
                    p.op("act", lambda e, rt=rt, ps=ps: e.copy(out=rt[:], in_=ps[:, 0:TG]), r=[pr], w=[rt_r])
                    out_dma(p, dd["r"][oc * 128:(oc + 1) * 128, cols], rt[:], rt_r, "pr_or")
                    if os.environ.get("PRE_NOBONUS"):
                        continue
                    rk, rk_r = f1.get()
                    crk = vc["rk"] + oc
                    p.op("dve", lambda e, rk=rk, ps=ps, oc=oc, crk=crk: e.scalar_tensor_tensor(out=rk[:], in0=ps[:, 0:TG], scalar=vec[:, crk:crk + 1],
                                                                                          in1=k_t[:, oc, :], op0=ALU.mult, op1=ALU.mult),
                         r=[pr, vec_r, k_r], w=[rk_r])
                    if os.environ.get("PRE_NOBONUS") == "2":
                        continue
                    ps2, pr2 = head_sum(p, cx, cs, cs.bo1, rk, rk_r)
                    if os.environ.get("PRE_NOBONUS") == "3":
                        continue
                    bn, bn_r = obn.get()
                    p.op("dve", lambda e, bn=bn, ps2=ps2, oc=oc: e.tensor_tensor(out=bn[:], in0=ps2[:, 0:TG], in1=v_t[:, oc, :], op=ALU.mult),
                         r=[pr2, v_r], w=[bn_r])
                    if os.environ.get("PRE_NOBONUS") == "5":
                        p.op("pool", lambda e, bn=bn: e.tensor_scalar_mul(out=bn[:], in0=bn[:], scalar1=1.0), r=[bn_r], w=[bn_r])
                    elif os.environ.get("PRE_NOBONUS") == "8":
                        b2, b2_r = ob.get()
                        p.op("pool", lambda e, bn=bn, b2=b2: e.tensor_copy(out=b2[:], in_=bn[:]), r=[bn_r], w=[b2_r])
                        out_dma(p, dd["bonus"][oc * 128:(oc + 1) * 128, cols], b2[:], b2_r, "pr_obn")
                    elif os.environ.get("PRE_NOBONUS") == "6":
                        out_dma(p, dd["bonus"][oc * 128:(oc + 1) * 128, cols], rt[:], rt_r, "pr_obn")
                    elif os.environ.get("PRE_NOBONUS") != "4":
                        out_dma(p, dd["bonus"][oc * 128:(oc + 1) * 128, cols], bn[:], bn_r, "pr_obn")
            if PRE_STOP <= 4:
                continue
            xm, xm_r = mix_dve(1)
            tw, tw_r = lora1(xm, xm_r, w1, 64, AF.Tanh)
            for oc in range(KC):
                ps, pr = cx.bank()
                p.op("pe", lambda e, oc=oc, ps=ps: e.matmul(ps[:, 0:TG], w2[:, oc * 128:(oc + 1) * 128], tw[0:64, :], start=True, stop=True),
                     r=[lw_r, tw_r], w=[pr])
                wt, wt_r = ob.get()
                c0 = vc["w0"] + oc
                p.op("act", lambda e, wt=wt, ps=ps, c0=c0: e.activation(out=wt[:], in_=ps[:, 0:TG], func=AF.Sigmoid, bias=vec[:, c0:c0 + 1], scale=1.0),
                     r=[pr, vec_r], w=[wt_r])
                p.op("pool", lambda e, wt=wt: e.tensor_scalar_mul(out=wt[:], in0=wt[:], scalar1=LW_SCALE), r=[wt_r], w=[wt_r])
                out_dma(p, dd["lw"][oc * 128:(oc + 1) * 128, cols], wt[:], wt_r, "pr_ow")
            if PRE_STOP <= 5:
                continue
            xm, xm_r = mix_dve(5)
            tg0, tg0_r = lora1(xm, xm_r, g1, 128, AF.Sigmoid)
            ps, pr = cx.bank()
            for kc in range(KC):
                p.op("pe", lambda e, kc=kc, ps=ps: e.matmul(ps[0:32, 0:TG], g1[:, kc, 128:160], xm[:, kc, :], start=(kc == 0), stop=(kc == KC - 1)),
                     r=[lw_r, xm_r], w=[pr])
            tg1, tg1_r = lt.get()
            p.op("act", lambda e, ps=ps: e.activation(out=tg1[0:32, :], in_=ps[0:32, 0:TG], func=AF.Sigmoid), r=[pr], w=[tg1_r])
            for oc in range(KC):
                ps, pr = cx.bank()
                p.op("pe", lambda e, oc=oc, ps=ps: e.matmul(ps[:, 0:TG], g2a[:, oc * 128:(oc + 1) * 128], tg0[:, :], start=True, stop=False),
                     r=[lw_r, tg0_r], w=[pr])
                p.op("pe", lambda e, oc=oc, ps=ps: e.matmul(ps[:, 0:TG], g2b[:, oc * 128:(oc + 1) * 128], tg1[0:32, :], start=False, stop=True),
                     r=[lw_r, tg1_r], w=[pr])
                gt, gt_r = ob.get()
                p.op("act", lambda e, gt=gt, ps=ps: e.copy(out=gt[:], in_=ps[:, 0:TG]), r=[pr], w=[gt_r])
                out_dma(p, dd["g"][oc * 128:(oc + 1) * 128, cols], gt[:], gt_r, "pr_og")


def stage_post(p, cx, cs, T, layer, variant, x_in_d, x_out_d, vec, vec_r, vc, mod_sb, mod_r, wo_d, up_d, down_d, dd, final_g=None):
    nc = p.nc
    NTL = T // TG
    mc = layer * 48
    with p.scope() as es2:
        sb = p.sb
        gm1 = sb("po_gm1", [128, KC])
        gm1_r = p.res("po_gm1")
        p.op("dve", lambda e: e.tensor_scalar_add(out=gm1[:], in0=mod_sb[:, mc + 16:mc + 24], scalar1=1.0), r=[mod_r], w=[gm1_r])
        gm2 = sb("po_gm2", [128, 2 * KC])
        gm2_r = p.res("po_gm2")
        g0 = vc["norm_mlp_g"]
        p.op("dve", lambda e: e.scalar_tensor_tensor(out=gm2[:, 0:KC], in0=mod_sb[:, mc + 32:mc + 40], scalar=1.0, in1=vec[:, g0:g0 + KC],
                                                    op0=ALU.add, op1=ALU.mult), r=[vec_r, mod_r], w=[gm2_r])
        p.op("dve", lambda e: e.tensor_scalar_add(out=gm2[:, KC:2 * KC], in0=mod_sb[:, mc + 40:mc + 48], scalar1=1.0), r=[mod_r], w=[gm2_r])
        shift2 = mod_sb[:, mc + 24:mc + 32]
        st = mlp_state(p)
        xt = Rot(p, "po_x", [128, KC, TG], F32, 1)
        z = sb("po_z", [128, KC, TG], BF16)
        z_r = p.res("po_z")
        i1 = Rot(p, "po_i1", [128, TG], F32, 2)
        i3 = Rot(p, "po_i3", [128, TG], F32, 2)
        if variant == "rwkv":
            i2 = Rot(p, "po_i2", [128, TG], F32, 2)
            f1 = Rot(p, "po_f1", [128, TG], F32, 2)
        pw = ProjW(p, "po_w")
        xv = x_in_d.rearrange("(c p) t -> p c t", p=128)
        ov = x_out_d.rearrange("(c p) t -> p c t", p=128)
        for ti in range(NTL):
            cols = slice(ti * TG, (ti + 1) * TG)
            x_t, x_r = xt.get()
            p.dma(x_t[:], xv[:, :, cols], w=[x_r])
            for oc in range(KC):
                rows = slice(oc * 128, (oc + 1) * 128)
                if variant == "rwkv":
                    y, y_r = i1.get()
                    p.dma(y[:], dd["y"][rows, cols], w=[y_r])
                    bn, bn_r = i2.get()
                    p.dma(bn[:], dd["bonus"][rows, cols], w=[bn_r])
                    g, g_r = i3.get()
                    p.dma(g[:], dd["g"][rows, cols], w=[g_r])
                    ps, pr = head_sum(p, cx, cs, cs.bo64, y, y_r)
                    p.op("dve", lambda e, y=y, ps=ps: e.tensor_tensor(out=y[:], in0=y[:], in1=ps[:, 0:TG], op=ALU.subtract), r=[y_r, pr], w=[y_r])
                    sq, sq_r = f1.get()
                    p.op("act", lambda e, y=y, sq=sq: e.activation(out=sq[:], in_=y[:], func=AF.Square), r=[y_r], w=[sq_r])
                    ps2, pr2 = head_sum(p, cx, cs, cs.bo64, sq, sq_r)
                    p.op("act", lambda e, sq=sq, ps2=ps2: e.activation(out=sq[:], in_=ps2[:, 0:TG], func=AF.Sqrt, bias=cx.eps_t[:, 1:2], scale=1.0),
                         r=[pr2, cx.r_const], w=[sq_r])
                    p.op("dve", lambda e, sq=sq: e.reciprocal(out=sq[:], in_=sq[:]), r=[sq_r], w=[sq_r])
                    p.op("dve", lambda e, y=y, sq=sq: e.tensor_tensor(out=y[:], in0=y[:], in1=sq[:], op=ALU.mult), r=[y_r, sq_r], w=[y_r])
                    cw, cb = vc["lnw"] + oc, vc["lnb"] + oc
                    p.op("dve", lambda e, y=y, cw=cw, cb=cb: e.tensor_scalar(out=y[:], in0=y[:], scalar1=vec[:, cw:cw + 1], scalar2=vec[:, cb:cb + 1],
                                                                         op0=ALU.mult, op1=ALU.add), r=[y_r, vec_r], w=[y_r])
                    p.op("pool", lambda e, y=y, bn=bn: e.tensor_tensor(out=y[:], in0=y[:], in1=bn[:], op=ALU.add), r=[y_r, bn_r], w=[y_r])
                    p.op("pool", lambda e, y=y, g=g, oc=oc: e.tensor_tensor(out=z[:, oc, :], in0=y[:], in1=g[:], op=ALU.mult), r=[y_r, g_r], w=[z_r])
                else:
                    o, o_r = i1.get()
                    p.dma(o[:], dd["o"][rows, cols], w=[o_r])
                    g, g_r = i3.get()
                    p.dma(g[:], dd["sig"][rows, cols], w=[g_r])
                    p.op("pool", lambda e, o=o, g=g, oc=oc: e.tensor_tensor(out=z[:, oc, :], in0=o[:], in1=g[:], op=ALU.mult), r=[o_r, g_r], w=[z_r])
            for o2 in range(KC // 2):
                wb, wr = pw.load(wo_d, o2 * 256)
                for oi in range(2):
                    oc = o2 * 2 + oi
                    ps, pr = proj_mm(p, cx, wb, wr, oi * 128, z, z_r)
                    xs = x_t[:, oc, :]
                    p.op("dve", lambda e, ps=ps, xs=xs, oc=oc: e.scalar_tensor_tensor(out=xs, in0=ps[:, 0:TG], scalar=gm1[:, oc:oc + 1], in1=xs,
                                                                                  op0=ALU.mult, op1=ALU.add), r=[pr, gm1_r, x_r], w=[x_r])
            emit_mlp(p, cx, x_t, x_r, gm2, gm2_r, shift2, mod_r, up_d, down_d, st)
            if final_g is not None:
                emit_norm(p, cx, x_t, x_r, final_g, None, vec_r, x_t, x_r, st["scr"])
                out_dma(p, ov[:, :, cols], x_t[:], x_r, "po_out")
            else:
                out_dma(p, ov[:, :, cols], x_t[:], x_r, "po_out")


def stage_scan(p, cx, cs, T, dd):
    nc = p.nc
    NP4 = T // 256
    with p.scope() as es2:
        sb = p.sb
        mr = p.res("sc_masks")
        mU2 = sb("sc_mU2", [128, 256])
        mL = sb("sc_mL", [128, 128])
        p.op("pool", lambda e: e.affine_select(out=mU2[:, 0:128], in_=cs.ones_f[:], pattern=[[1, 128]], compare_op=ALU.is_ge, fill=0.0, base=-1, channel_multiplier=-1), r=[cs.r], w=[mr])
        p.op("pool", lambda e: e.affine_select(out=mU2[:, 128:256], in_=cs.ones_f[:], pattern=[[1, 128]], compare_op=ALU.is_ge, fill=0.0, base=0, channel_multiplier=-1), r=[cs.r], w=[mr])
        p.op("pool", lambda e: e.affine_select(out=mL[:], in_=cs.ones_f[:], pattern=[[-1, 128]], compare_op=ALU.is_ge, fill=0.0, base=-1, channel_multiplier=1), r=[cs.r], w=[mr])
        p.op("pool", lambda e: e.memset(mU2[0:64, 64:128], 0.0), w=[mr])
        p.op("pool", lambda e: e.memset(mU2[0:64, 192:256], 0.0), w=[mr])
        p.op("pool", lambda e: e.memset(mL[64:128, 0:64], 0.0), w=[mr])
        NSTREAM = 3

        def stream(sid, hps):
            names = ["r", "lw", "k", "v", "a", "b"]
            inp = {n: Rot(p, "sc%d_" % sid + "in_" + n, [128, 256], F32, 2) for n in names}
            cum = Rot(p, "sc%d_" % sid + "cum", [128, 128], F32, 2)
            cpv = Rot(p, "sc%d_" % sid + "cpv", [128, 128], F32, 2)
            eP = Rot(p, "sc%d_" % sid + "eP", [128, 128], F32, 3)
            eN = Rot(p, "sc%d_" % sid + "eN", [128, 128], F32, 2)
            eV = Rot(p, "sc%d_" % sid + "eV", [128, 128], F32, 2)
            AR = Rot(p, "sc%d_" % sid + "AR", [128, 256], F32, 2)
            bT = Rot(p, "sc%d_" % sid + "bT", [128, 128], F32, 2)
            kT = Rot(p, "sc%d_" % sid + "kT", [128, 128], F32, 2)
            PadA = Rot(p, "sc%d_" % sid + "PadA", [128, 4, 128], F32, 2)
            PadB = Rot(p, "sc%d_" % sid + "PadB", [128, 4, 128], F32, 2)
            PZA = Rot(p, "sc%d_" % sid + "PZA", [128, 2, 128], F32, 2)
            PZB = Rot(p, "sc%d_" % sid + "PZB", [128, 2, 128], F32, 2)
            for rot in (PadA, PadB, PZA, PZB):
                for t, r in zip(rot.t, rot.r):
                    for j_ in range(t.shape[1]):
                        p.op("pool", lambda e, t=t, j_=j_: e.tensor_scalar_mul(out=R32(t[:, j_, :]), in0=cs.ones_f[:], scalar1=0.0), r=[cs.r], w=[r])
            ZcA = Rot(p, "sc%d_" % sid + "ZcA", [128, 128], F32, 2)
            ZcB = Rot(p, "sc%d_" % sid + "ZcB", [128, 128], F32, 2)
            XP = Rot(p, "sc%d_" % sid + "XP", [128, 256], F32, 3)
            YP = Rot(p, "sc%d_" % sid + "YP", [128, 256], F32, 3)
            Xp = Rot(p, "sc%d_" % sid + "Xp", [128, 128], F32, 4)
            Lp = Rot(p, "sc%d_" % sid + "Lp", [128, 128], F32, 4)
            RH = Rot(p, "sc%d_" % sid + "RH", [128, 128], F32, 2)
            YH = Rot(p, "sc%d_" % sid + "YH", [128, 128], F32, 2)
            IG = Rot(p, "sc%d_" % sid + "IG", [128, 128], F32, 4)
            NN = Rot(p, "sc%d_" % sid + "NN", [128, 128], F32, 4)
            Zbd = sb("sc%d_" % sid + "Zbd", [128, 128])
            Zbd_r = p.res("sc%d_Zbd" % sid)
            yo = Rot(p, "sc%d_" % sid + "yo", [128, 256], F32, 2)
            ci = [0]

            def evac(out, in_, r, w):
                ci[0] += 1
                if ci[0] % 2 == 0:
                    p.op("act", lambda e: e.copy(out=out, in_=in_), r=r, w=w)
                else:
                    p.op("dve", lambda e: e.tensor_copy(out=out, in_=in_), r=r, w=w)

            for hp in hps:
                rows = slice(hp * 128, (hp + 1) * 128)
                p.op("pool", lambda e: e.memset(Zbd[:], 0.0), w=[Zbd_r])
                for p4 in range(NP4):
                    cols4 = slice(p4 * 256, (p4 + 1) * 256)
                    tin = {}
                    for n in names:
                        t, r = inp[n].get()
                        p.dma(t[:], dd[n][rows, cols4], w=[r])
                        tin[n] = (t, r)
                    yo_t, yo_r = yo.get()
                    for u in range(2):
                        c = slice(u * 128, (u + 1) * 128)
                        lw_t, lw_r = tin["lw"]
                        cum_t, cum_r = cum.get()
                        for ch in range(2):
                            cc = slice(u * 128 + ch * 64, u * 128 + ch * 64 + 64)
                            oc = slice(ch * 64, ch * 64 + 64)
                            p.op("dve", lambda e, cc=cc, oc=oc, cum_t=cum_t, lw_t=lw_t: e.tensor_tensor_scan(
                                out=cum_t[:, oc], data0=cs.ones_f[:, oc], data1=lw_t[:, cc], initial=0.0, op0=ALU.mult, op1=ALU.add),
                                r=[lw_r, cs.r], w=[cum_r])
                        cpv_t, cpv_r = cpv.get()
                        p.op("pool", lambda e, cpv_t=cpv_t, cum_t=cum_t, lw_t=lw_t, c=c: e.tensor_tensor(out=cpv_t[:], in0=cum_t[:], in1=lw_t[:, c], op=ALU.subtract),
                             r=[cum_r, lw_r], w=[cpv_r])
                        eP_t, eP_r = eP.get()
                        eN_t, eN_r = eN.get()
                        eV_t, eV_r = eV.get()
                        p.op("act", lambda e, eP_t=eP_t, cum_t=cum_t: e.activation(out=eP_t[:], in_=cum_t[:], func=AF.Exp), r=[cum_r], w=[eP_r])
                        p.op("act", lambda e, eN_t=eN_t, cum_t=cum_t: e.activation(out=eN_t[:], in_=cum_t[:], func=AF.Exp, scale=-1.0), r=[cum_r], w=[eN_r])
                        p.op("act", lambda e, eV_t=eV_t, cpv_t=cpv_t: e.activation(out=eV_t[:], in_=cpv_t[:], func=AF.Exp), r=[cpv_r], w=[eV_r])
                        AR_t, AR_r = AR.get()
                        bT_t, bT_r = bT.get()
                        kT_t, kT_r = kT.get()
                        a_t, a_r = tin["a"]
                        r_t, r_r = tin["r"]
                        b_t, b_r = tin["b"]
                        k_t, k_r = tin["k"]
                        v_t, v_r = tin["v"]
                        p.op("dve", lambda e, AR_t=AR_t, a_t=a_t, eV_t=eV_t, c=c: e.tensor_tensor(out=R32(AR_t[:, 0:128]), in0=a_t[:, c], in1=eV_t[:], op=ALU.mult), r=[a_r, eV_r], w=[AR_r])
                        p.op("dve", lambda e, AR_t=AR_t, r_t=r_t, eP_t=eP_t, c=c: e.tensor_tensor(out=R32(AR_t[:, 128:256]), in0=r_t[:, c], in1=eP_t[:], op=ALU.mult), r=[r_r, eP_r], w=[AR_r])
                        p.op("dve", lambda e, bT_t=bT_t, b_t=b_t, eN_t=eN_t, c=c: e.tensor_tensor(out=R32(bT_t[:]), in0=b_t[:, c], in1=eN_t[:], op=ALU.mult), r=[b_r, eN_r], w=[bT_r])
                        p.op("pool", lambda e, kT_t=kT_t, k_t=k_t, eN_t=eN_t, c=c: e.tensor_tensor(out=R32(kT_t[:]), in0=k_t[:, c], in1=eN_t[:], op=ALU.mult), r=[k_r, eN_r], w=[kT_r])
                        yield
                        ps, pr = cx.bank()
                        srcs = [(AR_t[:, 0:128], AR_r), (bT_t[:], bT_r), (kT_t[:], kT_r), (v_t[:, c], v_r)]
                        for j, (src, sr) in enumerate(srcs):
                            p.op("pe", lambda e, j=j, src=src, ps=ps: e.transpose(out=ps[:, j * 128:(j + 1) * 128], in_=src, identity=cs.ident[:]), r=[sr, cs.r], w=[pr])
                        PA, PA_r = PadA.get()
                        PB, PB_r = PadB.get()
                        psv = ps[:, 0:512].rearrange("p (j c) -> p j c", j=4)
                        p.op("act", lambda e, PA=PA, psv=psv: e.copy(out=R32(PA[:, :, 0:64]), in_=psv[:, :, 0:64]), r=[pr], w=[PA_r])
                        p.op("dve", lambda e, PB=PB, psv=psv: e.tensor_copy(out=R32(PB[:, :, 64:128]), in_=psv[:, :, 64:128]), r=[pr], w=[PB_r])
                        Zcs = [ZcA.get(), ZcB.get()]
                        p.op("act", lambda e, ps=ps: e.copy(out=R32(Zcs[0][0][:, 0:64]), in_=ps[:, 0:64]), r=[pr], w=[Zcs[0][1]])
                        p.op("dve", lambda e, ps=ps: e.tensor_copy(out=R32(Zcs[1][0][:, 0:64]), in_=ps[:, 64:128]), r=[pr], w=[Zcs[1][1]])
                        yield
                        PZ = [PZA.get(), PZB.get()]
                        Pad = [(PA, PA_r), (PB, PB_r)]
                        XPs, YPs = [], []
                        for h in range(2):
                            hs = slice(h * 64, (h + 1) * 64)
                            Zc_t, Zc_r = Zcs[h]
                            ps1, pr1 = cx.bank()
                            p.op("pe", lambda e, ps1=ps1, hs=hs: e.matmul(ps1[:, 0:256], R32(bT_t[hs, :]), R32(AR_t[hs, :]), start=True, stop=True), r=[bT_r, AR_r], w=[pr1])
                            XP_t, XP_r = XP.get()
                            p.op("dve", lambda e, XP_t=XP_t, ps1=ps1: e.tensor_tensor(out=R32(XP_t[:]), in0=ps1[:, 0:256], in1=mU2[:], op=ALU.mult), r=[pr1, mr], w=[XP_r])
                            ps2, pr2 = cx.bank()
                            p.op("pe", lambda e, ps2=ps2, hs=hs: e.matmul(ps2[:, 0:256], R32(kT_t[hs, :]), R32(AR_t[hs, :]), start=True, stop=True), r=[kT_r, AR_r], w=[pr2])
                            YP_t, YP_r = YP.get()
                            p.op("dve", lambda e, YP_t=YP_t, ps2=ps2: e.tensor_tensor(out=R32(YP_t[:]), in0=ps2[:, 0:256], in1=mU2[:], op=ALU.mult), r=[pr2, mr], w=[YP_r])
                            ps3, pr3 = cx.bank()
                            p.op("pe", lambda e, ps3=ps3, hs=hs: e.matmul(ps3[:, 0:128], R32(AR_t[hs, 0:128]), R32(bT_t[hs, :]), start=True, stop=True), r=[bT_r, AR_r], w=[pr3])
                            L_t, L_r = Lp.get()
                            p.op("dve", lambda e, L_t=L_t, ps3=ps3: e.tensor_tensor(out=R32(L_t[:]), in0=ps3[:, 0:128], in1=mL[:], op=ALU.mult), r=[pr3, mr], w=[L_r])
                            XPs.append((XP_t, XP_r))
                            YPs.append((YP_t, YP_r))
                            yield
                            Pd, Pd_r = Pad[h]
                            ps4, pr4 = cx.bank()
                            p.op("pe", lambda e, ps4=ps4, YP_t=YP_t, Pd=Pd, hs=hs: e.matmul(ps4[:, 0:64], R32(YP_t[:, 0:128]), R32(Pd[:, 3, hs]), start=True, stop=True), r=[YP_r, Pd_r], w=[pr4])
                            evac(R32(Zc_t[:, 64:128]), ps4[:, 0:64], [pr4], [Zc_r])
                            X_t, X_r = XP_t[:, 0:128], XP_r
                            Lc_t, Lc_r = L_t[:], L_r
                            PZ_t, PZ_r = PZ[h]
                            for j in range(6):
                                yield
                                psa, pra = cx.bank()
                                p.op("pe", lambda e, psa=psa, X_t=X_t, Zc_t=Zc_t: e.matmul(psa[:, 0:128], R32(X_t), R32(Zc_t[:]), start=True, stop=True), r=[X_r, Zc_r], w=[pra])
                                if j < 5:
                                    psx, prx = cx.bank()
                                    p.op("pe", lambda e, psx=psx, X_t=X_t, Lc_t=Lc_t: e.matmul(psx[:, 0:128], R32(Lc_t), R32(X_t), start=True, stop=True), r=[X_r, Lc_r], w=[prx])
                                    psl, prl = cx.bank()
                                    p.op("pe", lambda e, psl=psl, X_t=X_t, Lc_t=Lc_t: e.matmul(psl[:, 0:128], R32(X_t), R32(Lc_t), start=True, stop=True), r=[X_r, Lc_r], w=[prl])
                                    p.op("dve", lambda e, psa=psa, Zc_t=Zc_t: e.tensor_tensor(out=R32(Zc_t[:]), in0=psa[:, 0:128], in1=Zc_t[:], op=ALU.add), r=[pra, Zc_r], w=[Zc_r])
                                    Xn, Xn_r = Xp.get()
                                    Ln, Ln_r = Lp.get()
                                    evac(R32(Xn[:]), psx[:, 0:128], [prx], [Xn_r])
                                    evac(R32(Ln[:]), psl[:, 0:128], [prl], [Ln_r])
                                    X_t, X_r, Lc_t, Lc_r = Xn[:], Xn_r, Ln[:], Ln_r
                                else:
                                    p.op("dve", lambda e, psa=psa, Zc_t=Zc_t, PZ_t=PZ_t, hs=hs: e.tensor_tensor(
                                        out=R32(PZ_t[:, :, hs]), in0=psa[:, 0:128].rearrange("p (j c) -> p j c", j=2),
                                        in1=Zc_t[:].rearrange("p (j c) -> p j c", j=2), op=ALU.add), r=[pra, Zc_r], w=[PZ_r])
                        yield
                        psr_, prr = cx.bank()
                        for h in range(2):
                            p.op("pe", lambda e, h=h, psr_=psr_: e.matmul(psr_[:, 0:128], R32(PZ[h][0][:, 0, :]), R32(XPs[h][0][:, 128:256]), start=(h == 0), stop=(h == 1)),
                                 r=[PZ[h][1], XPs[h][1]], w=[prr])
                        RH_t, RH_r = RH.get()
                        p.op("dve", lambda e, RH_t=RH_t, psr_=psr_, AR_t=AR_t: e.tensor_tensor(out=RH_t[:], in0=psr_[:, 0:128], in1=AR_t[:, 128:256], op=ALU.add), r=[prr, AR_r], w=[RH_r])
                        psy, pry = cx.bank()
                        for h in range(2):
                            p.op("pe", lambda e, h=h, psy=psy: e.matmul(psy[:, 0:128], R32(PZ[h][0][:, 1, :]), R32(XPs[h][0][:, 128:256]), start=(h == 0), stop=False),
                                 r=[PZ[h][1], XPs[h][1]], w=[pry])
                            p.op("pe", lambda e, h=h, psy=psy: e.matmul(psy[:, 0:128], R32(Pad[h][0][:, 3, :]), R32(YPs[h][0][:, 128:256]), start=False, stop=(h == 1)),
                                 r=[Pad[h][1], YPs[h][1]], w=[pry])
                        YH_t, YH_r = YH.get()
                        evac(YH_t[:], psy[:, 0:128], [pry], [YH_r])
                        yield
                        IGs, NNs = [], []
                        for ch in range(2):
                            tk = slice(ch * 64, ch * 64 + 64)
                            psg, prg = cx.bank()
                            for h in range(2):
                                p.op("pe", lambda e, h=h, psg=psg, tk=tk: e.matmul(psg[:, 0:128], R32(PZ[h][0][tk, 0, :]), R32(Pad[h][0][tk, 1, :]), start=(h == 0), stop=(h == 1)),
                                     r=[PZ[h][1], Pad[h][1]], w=[prg])
                            IG_t, IG_r = IG.get()
                            p.op("dve", lambda e, IG_t=IG_t, psg=psg: e.tensor_tensor(out=IG_t[:], in0=psg[:, 0:128], in1=cs.ident[:], op=ALU.add), r=[prg, cs.r], w=[IG_r])
                            psn, prn = cx.bank()
                            for h in range(2):
                                p.op("pe", lambda e, h=h, psn=psn, tk=tk: e.matmul(psn[:, 0:128], R32(Pad[h][0][tk, 1, :]), R32(PZ[h][0][tk, 1, :]), start=(h == 0), stop=False),
                                     r=[PZ[h][1], Pad[h][1]], w=[prn])
                                p.op("pe", lambda e, h=h, psn=psn, tk=tk: e.matmul(psn[:, 0:128], R32(Pad[h][0][tk, 2, :]), R32(Pad[h][0][tk, 3, :]), start=False, stop=(h == 1)),
                                     r=[Pad[h][1]], w=[prn])
                            NN_t, NN_r = NN.get()
                            wc = eP_t[:, ch * 64 + 63:ch * 64 + 64]
                            p.op("dve", lambda e, NN_t=NN_t, psn=psn, wc=wc: e.tensor_scalar_mul(out=NN_t[:], in0=psn[:, 0:128], scalar1=wc), r=[prn, eP_r], w=[NN_r])
                            IGs.append((IG_t, IG_r))
                            NNs.append((NN_t, NN_r, wc))
                        for ch in range(2):
                            yield
                            tcol = slice(ch * 64, ch * 64 + 64)
                            ocol = slice(u * 128 + ch * 64, u * 128 + ch * 64 + 64)
                            psq, prq = cx.bank()
                            p.op("pe", lambda e, psq=psq, tcol=tcol, RH_t=RH_t: e.matmul(psq[:, 0:64], Zbd[:], RH_t[:, tcol], start=True, stop=True), r=[Zbd_r, RH_r], w=[prq])
                            p.op("dve", lambda e, psq=psq, tcol=tcol, ocol=ocol, YH_t=YH_t, yo_t=yo_t: e.tensor_tensor(out=yo_t[:, ocol], in0=psq[:, 0:64], in1=YH_t[:, tcol], op=ALU.add),
                                 r=[prq, YH_r], w=[yo_r])
                            psz, prz = cx.bank()
                            IG_t, IG_r = IGs[ch]
                            NN_t, NN_r, wc = NNs[ch]
                            p.op("pe", lambda e, psz=psz, IG_t=IG_t: e.matmul(psz[:, 0:128], IG_t[:], Zbd[:], start=True, stop=True), r=[IG_r, Zbd_r], w=[prz])
                            p.op("dve", lambda e, psz=psz, NN_t=NN_t, wc=wc: e.scalar_tensor_tensor(out=Zbd[:], in0=psz[:, 0:128], scalar=wc, in1=NN_t[:], op0=ALU.mult, op1=ALU.add),
                                 r=[prz, NN_r, eP_r], w=[Zbd_r])
                    out_dma(p, dd["y"][rows, cols4], yo_t[:], yo_r, "sc_y")


        gens = [stream(s, list(range(s, H // 2, NSTREAM))) for s in range(NSTREAM)]
        while gens:
            for g in list(gens):
                try:
                    next(g)
                except StopIteration:
                    gens.remove(g)


def stage_kvq(p, cx, cs, T, kind, layer, x_d, vec, vec_r, g_col, mod_sb, mod_r, W_d, gain_sb, gain_r, dd, fb_sb=None, fb_r=None, Wf_d=None):
    nc = p.nc
    NTL = T // TG
    with p.scope() as es2:
        sb = p.sb
        gm = sb("kq_gm", [128, KC])
        gm_r = p.res("kq_gm")
        if kind == "kv":
            sh_c, sc_c = 192, 200
        else:
            sh_c, sc_c = layer * 48, layer * 48 + 8
        p.op("dve", lambda e: e.scalar_tensor_tensor(out=gm[:], in0=mod_sb[:, sc_c:sc_c + 8], scalar=1.0, in1=vec[:, g_col:g_col + KC],
                                                    op0=ALU.add, op1=ALU.mult), r=[vec_r, mod_r], w=[gm_r])
        shift = mod_sb[:, sh_c:sh_c + 8]
        xt = Rot(p, "kq_x", [128, KC, TG], F32, 2)
        h = sb("kq_h", [128, KC, TG], BF16)
        h_r = p.res("kq_h")
        scr = norm_scratch(p, "kqn")
        pw = ProjW(p, "kq_w")
        f1 = Rot(p, "kq_f1", [128, TG], F32, 3)
        ob = Rot(p, "kq_ob", [128, TG], F32, 4)
        xv = x_d.rearrange("(c p) t -> p c t", p=128)
        if kind == "kv":
            wf_s = sb("kq_wfs", [128, KC, 16])
            wf = sb("kq_wf", [128, KC, 16], BF16)
            wf_r = p.res("kq_wf")
            p.dma(wf_s[:], Wf_d.rearrange("(kc p) m -> p kc m", p=128)[:, :, 2 * D:2 * D + 16], w=[wf_r])
            p.op("pool", lambda e: e.tensor_copy(out=wf[:], in_=wf_s[:]), r=[wf_r], w=[wf_r])
            lf = Rot(p, "kq_lf", [16, TG], F32, 2)
        gscale = 1.0 if kind == "kv" else 0.125
        for ti in range(NTL):
            cols = slice(ti * TG, (ti + 1) * TG)
            x_t, x_r = xt.get()
            p.dma(x_t[:], xv[:, :, cols], w=[x_r])
            emit_norm(p, cx, x_t, x_r, gm, shift, gm_r, h, h_r, scr)
            for o2 in range(2 * KC // 2):
                wb, wr = pw.load(W_d, o2 * 256)
                for oi in range(2):
                    oc = o2 * 2 + oi
                    ps, pr = proj_mm(p, cx, wb, wr, oi * 128, h, h_r)
                    if oc < KC:
                        sq, sq_r = f1.get()
                        p.op("act", lambda e, sq=sq, ps=ps: e.activation(out=sq[:], in_=ps[:, 0:TG], func=AF.Square), r=[pr], w=[sq_r])
                        ps2, pr2 = head_sum(p, cx, cs, cs.bo64, sq, sq_r)
                        p.op("act", lambda e, sq=sq, ps2=ps2: e.activation(out=sq[:], in_=ps2[:, 0:TG], func=AF.Sqrt, bias=cx.eps_t[:, 0:1], scale=1.0),
                             r=[pr2, cx.r_const], w=[sq_r])
                        p.op("dve", lambda e, sq=sq: e.reciprocal(out=sq[:], in_=sq[:]), r=[sq_r], w=[sq_r])
                        o, o_r = ob.get()
                        p.op("dve", lambda e, o=o, ps=ps, sq=sq: e.scalar_tensor_tensor(out=o[:], in0=ps[:, 0:TG], scalar=gain_sb, in1=sq[:], op0=ALU.mult, op1=ALU.mult),
                             r=[pr, sq_r, gain_r], w=[o_r])
                        if gscale != 1.0:
                            p.op("pool", lambda e, o=o: e.tensor_scalar_mul(out=o[:], in0=o[:], scalar1=gscale), r=[o_r], w=[o_r])
                        dst = dd["ksh"] if kind == "kv" else dd["q"]
                        out_dma(p, dst[oc * 128:(oc + 1) * 128, cols], o[:], o_r, "kq_o")
                    else:
                        o, o_r = ob.get()
                        if kind == "kv":
                            p.op("act", lambda e, o=o, ps=ps: e.copy(out=o[:], in_=ps[:, 0:TG]), r=[pr], w=[o_r])
                            out_dma(p, dd["vsh"][(oc - KC) * 128:(oc - KC + 1) * 128, cols], o[:], o_r, "kq_o")
                        else:
                            p.op("act", lambda e, o=o, ps=ps: e.activation(out=o[:], in_=ps[:, 0:TG], func=AF.Sigmoid), r=[pr], w=[o_r])
                            out_dma(p, dd["sig"][(oc - KC) * 128:(oc - KC + 1) * 128, cols], o[:], o_r, "kq_o")
            if kind == "kv":
                ps, pr = cx.bank()
                for kc in range(KC):
                    p.op("pe", lambda e, kc=kc, ps=ps: e.matmul(ps[0:16, 0:TG], wf[:, kc, :], h[:, kc, :], start=(kc == 0), stop=(kc == KC - 1)), r=[wf_r, h_r], w=[pr])
                l, l_r = lf.get()
                p.op("act", lambda e, l=l, ps=ps: e.activation(out=l[:], in_=ps[0:16, 0:TG], func=AF.Exp, bias=fb_sb, scale=-1.0), r=[pr, fb_r], w=[l_r])
                p.op("act", lambda e, l=l: e.activation(out=l[:], in_=l[:], func=AF.Ln, bias=cx.eps_t[0:16, 2:3], scale=1.0), r=[l_r, cx.r_const], w=[l_r])
                p.op("pool", lambda e, l=l: e.tensor_scalar_mul(out=l[:], in0=l[:], scalar1=-1.0), r=[l_r], w=[l_r])
                out_dma(p, dd["logf"][:, cols], l[:], l_r, "kq_lf")


def stage_fprep(p, cx, T, dd):
    nc = p.nc
    FB = min(2048, T)
    with p.scope() as es2:
        sb = p.sb
        ones = sb("fp_ones", [16, FB])
        cr = p.res("fp_c")
        p.op("pool", lambda e: e.memset(ones[:], 1.0), w=[cr])
        lf = Rot(p, "fp_lf", [16, FB], F32, 2)
        F = Rot(p, "fp_F", [16, FB], F32, 2)
        r1 = Rot(p, "fp_r1", [16, FB], F32, 2)
        pp = Rot(p, "fp_pp", [16, 3, FB], BF16, 2)
        pn = Rot(p, "fp_pn", [16, 3, FB], BF16, 2)
        prev = None
        for bi in range(T // FB):
            cols = slice(bi * FB, (bi + 1) * FB)
            l, l_r = lf.get()
            p.dma(l[:], dd["logf"][:, cols], w=[l_r])
            f, f_r = F.get()
            init = 0.0 if prev is None else prev[0][:, FB - 1:FB]
            rr = [l_r, cr] + ([] if prev is None else [prev[1]])
            p.op("dve", lambda e, f=f, l=l, init=init: e.tensor_tensor_scan(out=f[:], data0=ones[:], data1=l[:], initial=init, op0=ALU.mult, op1=ALU.add), r=rr, w=[f_r])
            prev = (f, f_r)
            q, q_r = pp.get()
            n, n_r = pn.get()
            r_, r_r = r1.get()
            p.op("act", lambda e, q=q, f=f: e.copy(out=q[:, 0, :], in_=f[:]), r=[f_r], w=[q_r])
            p.op("dve", lambda e, r_=r_, f=f, q=q: e.tensor_tensor(out=r_[:], in0=f[:], in1=q[:, 0, :], op=ALU.subtract), r=[f_r, q_r], w=[r_r])
            p.op("act", lambda e, q=q, r_=r_: e.copy(out=q[:, 1, :], in_=r_[:]), r=[r_r], w=[q_r])
            p.op("dve", lambda e, r_=r_, q=q: e.tensor_tensor(out=r_[:], in0=r_[:], in1=q[:, 1, :], op=ALU.subtract), r=[r_r, q_r], w=[r_r])
            p.op("act", lambda e, q=q, r_=r_: e.copy(out=q[:, 2, :], in_=r_[:]), r=[r_r], w=[q_r])
            p.op("pool", lambda e, n=n, q=q: e.tensor_scalar_mul(out=n[:], in0=q[:], scalar1=-1.0), r=[q_r], w=[n_r])
            out_dma(p, dd["fpos"][:, :, cols], q[:], q_r, "fp_p")
            out_dma(p, dd["fneg"][:, :, cols], n[:], n_r, "fp_n")


def stage_attn(p, cx, cs, T, dd):
    nc = p.nc
    NQT = T // 512
    NKB = T // 128
    LB = min(2048, T)
    with p.scope() as es2:
        sb = p.sb
        cx.rot = [0, 1, 2, 3, 4, 5]
        obank = [(cx.ps[6], cx.psr[6]), (cx.ps[7], cx.psr[7])]
        mr = p.res("at_masks")
        onesb = sb("at_onesb", [128, 512], BF16)
        p.op("pool", lambda e: e.memset(onesb[:], 1.0), w=[mr])
        masks = []
        for o in range(4):
            m = sb("at_mask%d" % o, [128, 512], BF16)
            p.op("pool", lambda e, m=m, o=o: e.affine_select(out=m[:], in_=onesb[:], pattern=[[1, 512]], compare_op=ALU.is_ge, fill=0.0,
                                                          base=-128 * o, channel_multiplier=-1), r=[mr], w=[mr])
            masks.append(m)
        KA = Rot(p, "at_KA", [70, T], BF16, 2)
        QA = Rot(p, "at_QA", [70, T], BF16, 2)
        VP = Rot(p, "at_VP", [128, NKB, 65], BF16, 2)
        for rot in (KA, QA):
            for t, r in zip(rot.t, rot.r):
                p.op("pool", lambda e, t=t: e.memset(t[64:70, :], 1.0), w=[r])
        for t, r in zip(VP.t, VP.r):
            p.op("pool", lambda e, t=t: e.memset(t[:, :, 64:65], 1.0), w=[r])
        stg = Rot(p, "at_stg", [64, LB], F32, 3)
        pT = Rot(p, "at_pT", [128, 512], BF16, 7)
        clp = Rot(p, "at_clp", [128, 512], F32, 2)
        osb = Rot(p, "at_osb", [65, 512], F32, 2)
        rc = Rot(p, "at_rc", [65, 512], F32, 2)
        oo = Rot(p, "at_oo", [64, 512], F32, 2)
        ci = [0]
        DEPTH = 4
        pend = []
        nqt_done = [0]

        def flush_one():
            ob_, ob_r, VP_t, VP_r, kb, nkb, pt, pt_r, fin = pend.pop(0)
            p.op("pe", lambda e, ob_=ob_, kb=kb, pt=pt, nkb=nkb, VP_t=VP_t: e.matmul(ob_[0:65, 0:512], VP_t[:, kb, :], pt[:], start=(kb == 0), stop=(kb == nkb - 1)),
                 r=[VP_r, pt_r], w=[ob_r])
            if fin is not None:
                rows, qc = fin
                os_, os_r = osb.get()
                p.op("act", lambda e, os_=os_, ob_=ob_: e.copy(out=os_[:], in_=ob_[0:65, 0:512]), r=[ob_r], w=[os_r])
                rc_, rc_r = rc.get()
                p.op("dve", lambda e, rc_=rc_, os_=os_: e.reciprocal(out=rc_[64:65, :], in_=os_[64:65, :]), r=[os_r], w=[rc_r])
                ps, pr = cx.bank()
                p.op("pe", lambda e, ps=ps, rc_=rc_: e.matmul(ps[0:64, 0:512], cs.ones_f[64:65, 0:64], rc_[64:65, :], start=True, stop=True), r=[rc_r, cs.r], w=[pr])
                o_, o_r = oo.get()
                p.op("dve", lambda e, o_=o_, os_=os_, ps=ps: e.tensor_tensor(out=o_[:], in0=os_[0:64, :], in1=ps[0:64, 0:512], op=ALU.mult), r=[os_r, pr], w=[o_r])
                out_dma(p, dd["o"][rows, qc], o_[:], o_r, "at_o")

        for h in range(H):
            rows = slice(h * 64, (h + 1) * 64)
            KA_t, KA_r = KA.get()
            QA_t, QA_r = QA.get()
            VP_t, VP_r = VP.get()
            for bi in range(T // LB):
                cols = slice(bi * LB, (bi + 1) * LB)
                for src, dst, dst_r in ((dd["ksh"], KA_t, KA_r), (dd["q"], QA_t, QA_r)):
                    s, s_r = stg.get()
                    p.dma(s[:], src[rows, cols], w=[s_r])
                    ci[0] += 1
                    if ci[0] % 2 == 0:
                        p.op("act", lambda e, s=s, dst=dst, cols=cols: e.copy(out=dst[0:64, cols], in_=s[:]), r=[s_r], w=[dst_r])
                    else:
                        p.op("pool", lambda e, s=s, dst=dst, cols=cols: e.tensor_copy(out=dst[0:64, cols], in_=s[:]), r=[s_r], w=[dst_r])
                s, s_r = stg.get()
                p.dma(s[:], dd["vsh"][rows, cols], w=[s_r])
                GS = min(8, LB // 128)
                for g8 in range(LB // 128 // GS):
                    ps, pr = cx.bank()
                    for j in range(GS):
                        kb = g8 * GS + j
                        p.op("pe", lambda e, ps=ps, j=j, kb=kb, s=s: e.transpose(out=ps[:, j * 64:(j + 1) * 64], in_=s[:, kb * 128:(kb + 1) * 128], identity=cs.ident[0:64, 0:64]),
                             r=[s_r, cs.r], w=[pr])
                    kb0 = bi * (LB // 128) + g8 * GS
                    p.op("dve", lambda e, ps=ps, kb0=kb0, VP_t=VP_t, GS=GS: e.tensor_copy(out=VP_t[:, kb0:kb0 + GS, 0:64], in_=ps[:, 0:GS * 64].rearrange("p (j c) -> p j c", j=GS)),
                         r=[pr], w=[VP_r])
            p.dma(QA_t[64:67, :], dd["fpos"][h], w=[QA_r])
            p.dma(KA_t[67:70, :], dd["fneg"][h], w=[KA_r])
            for qt in range(NQT):
                qc = slice(qt * 512, (qt + 1) * 512)
                ob_, ob_r = obank[nqt_done[0] % 2]
                nqt_done[0] += 1
                nkb = 4 * (qt + 1)
                for kb in range(nkb):
                    ps, pr = cx.bank()
                    p.op("pe", lambda e, ps=ps, kb=kb, qc=qc, KA_t=KA_t, QA_t=QA_t: e.matmul(ps[:, 0:512], KA_t[:, kb * 128:(kb + 1) * 128], QA_t[:, qc], start=True, stop=True), r=[KA_r, QA_r], w=[pr])
                    pt, pt_r = pT.get()
                    if kb >= 4 * qt:
                        cl, cl_r = clp.get()
                        p.op("dve", lambda e, ps=ps, cl=cl: e.tensor_scalar_min(out=cl[:], in0=ps[:, 0:512], scalar1=20.0), r=[pr], w=[cl_r])
                        p.op("act", lambda e, cl=cl, pt=pt: e.activation(out=pt[:], in_=cl[:], func=AF.Exp), r=[cl_r], w=[pt_r])
                        mk_ = masks[kb - 4 * qt]
                        p.op("pool", lambda e, pt=pt, mk_=mk_: e.tensor_tensor(out=pt[:], in0=pt[:], in1=mk_[:], op=ALU.mult), r=[pt_r, mr], w=[pt_r])
                    else:
                        p.op("act", lambda e, ps=ps, pt=pt: e.activation(out=pt[:], in_=ps[:, 0:512], func=AF.Exp), r=[pr], w=[pt_r])
                    last = (kb == nkb - 1)
                    pend.append((ob_, ob_r, VP_t, VP_r, kb, nkb, pt, pt_r, (rows, qc) if last else None))
                    while len(pend) > DEPTH:
                        flush_one()
        while pend:
            flush_one()
        cx.rot = list(range(8))


def stage_wcast(p, cx, jobs):
    with p.scope() as es2:
        engs = ["act", "dve", "pool"]
        ci = 0
        stgs = {}
        for src, dst, bc in jobs:
            kci = src.shape[0] // 128
            key = (kci, bc)
            if key not in stgs:
                stgs[key] = (Rot(p, "wc_s%d_%d" % key, [128, kci, bc], F32, 2), Rot(p, "wc_b%d_%d" % key, [128, kci, bc], BF16, 2))
            srot, brot = stgs[key]
            v = src.rearrange("(kc p) m -> p kc m", p=128)
            for j in range(src.shape[1] // bc):
                s_, s_r = srot.get()
                b_, b_r = brot.get()
                p.dma(s_[:], v[:, :, j * bc:(j + 1) * bc], w=[s_r])
                copy_op(p, engs[ci % 3], b_[:], s_[:], r=[s_r], w=[b_r])
                ci += 1
                p.dma(dst[j], b_[:], r=[b_r], w=[p.res("wc_o")], key="o_" + b_r.name)


NV = 2 * len(RW_VECS) * KC + 2 * 2 * KC + 2 * KC + 8
W_SHAPES = {
    "mod_w": [4, D, 6 * D], "mlp_up": [4, D, DFF], "mlp_down": [4, DFF, D],
    "rw_wr": [2, D, D], "rw_wk": [2, D, D], "rw_wv": [2, D, D], "rw_wo": [2, D, D],
    "rw_w1": [2, D, 64], "rw_w2": [2, 64, D], "rw_a1": [2, D, 64], "rw_a2": [2, 64, D],
    "rw_g1": [2, D, 160], "rw_g2": [2, 160, D], "rw_v1": [1, D, 32], "rw_v2": [1, 32, D],
    "kv_mod_w": [D, 2 * D], "kv_w": [D, 2 * D + 16], "fx_wqg": [2, D, 2 * D], "fx_wo": [2, D, D],
}


def build_program(T, stages=None, dump=()):
    nc = bass.Bass("TRN2", target_bir_lowering=False)
    ein = lambda n, s, d=F32: nc.dram_tensor(n, list(s), d, kind="ExternalInput").ap()
    xT = ein("xT", [D, T])
    cT = ein("cT", [128, KC])
    modb = ein("modb", [128, NMODC])
    vecs_d = ein("vecs", [128, NV])
    nfb_d = ein("nfb", [16, 1])
    Wd = {n: ein(n, s) for n, s in W_SHAPES.items()}
    outT = nc.dram_tensor("outT", [D, T], F32, kind="ExternalOutput").ap()
    _cnt = [0]

    def idr(n, s, d=F32):
        _cnt[0] += 1
        return nc.dram_tensor("scr%02d_%s" % (_cnt[0], n), list(s), d).ap()
    dd = {n: idr(n, [D, T]) for n in ["xa", "xb", "r", "lw", "k", "v", "a", "b", "g", "bonus", "vfirst", "y", "ksh", "vsh", "q", "sig", "o"]}
    dd["logf"] = idr("logf", [16, T])
    dd["fpos"] = idr("fpos", [16, 3, T], BF16)
    dd["fneg"] = idr("fneg", [16, 3, T], BF16)
    dump_out = {n: nc.dram_tensor("dump_" + n, list(dd[n].shape), dd[n].dtype, kind="ExternalOutput").ap() for n in dump}
    with ExitStack() as es:
        p = Prog(nc, es)
        cx = Ctx(p)
        cs = Consts(p, cx)
        vec = p.sb("vecs_sb", [128, NV])
        vec_r = p.res("vecs")
        p.dma(vec[:], vecs_d, w=[vec_r])
        nfb = p.sb("nfb_sb", [16, 1])
        nfb_r = p.res("nfb")
        p.dma(nfb[:], nfb_d, w=[nfb_r])
        p.op("pool", lambda e: e.tensor_scalar_mul(out=nfb[:], in0=nfb[:], scalar1=-1.0), r=[nfb_r], w=[nfb_r])
        mod_sb = p.sb("mod_sb", [128, NMODC])
        mod_r = p.res("mod")
        stage_mod(p, cx, cT, Wd["mod_w"], Wd["kv_mod_w"], modb, mod_sb, mod_r)
        jobs = []
        Wb = {}

        def mkb(name, src, bc):
            kin, mm_ = src.shape
            t = nc.dram_tensor("wbf_" + name, [mm_ // bc, 128, kin // 128, bc], BF16).ap()
            jobs.append((src, t, bc))
            return t
        for i in range(4):
            Wb["up%d" % i] = mkb("up%d" % i, Wd["mlp_up"][i], 256)
            Wb["dn%d" % i] = mkb("dn%d" % i, Wd["mlp_down"][i], 128)
        for i in range(2):
            for n_ in ("wr", "wk", "wv", "wo"):
                Wb["%s%d" % (n_, i)] = mkb("%s%d" % (n_, i), Wd["rw_" + n_][i], 256)
            Wb["qg%d" % i] = mkb("qg%d" % i, Wd["fx_wqg"][i], 256)
            Wb["fo%d" % i] = mkb("fo%d" % i, Wd["fx_wo"][i], 256)
        Wb["kv"] = mkb("kv", Wd["kv_w"][:, 0:2 * D], 256)
        p.barrier()
        stage_wcast(p, cx, jobs)
        nrw = len(RW_VECS) * KC
        vcs = []
        for i in range(2):
            vcs.append({n: i * nrw + j * KC for j, n in enumerate(RW_VECS)})
        for i in range(2):
            vcs.append({"norm_mix_g": 2 * nrw + i * 2 * KC, "norm_mlp_g": 2 * nrw + i * 2 * KC + KC})
        c_kvg = 2 * nrw + 4 * KC
        c_fin = c_kvg + KC
        c_gain = c_fin + KC
        xcur, xnext = xT, dd["xa"]
        nstage = 0

        def want(name):
            return stages is None or name in stages

        for i in range(2):
            W = {"wr": Wb["wr%d" % i], "wk": Wb["wk%d" % i], "wv": Wb["wv%d" % i], "w1": Wd["rw_w1"][i], "w2": Wd["rw_w2"][i],
                 "a1": Wd["rw_a1"][i], "a2": Wd["rw_a2"][i], "g1": Wd["rw_g1"][i], "g2": Wd["rw_g2"][i]}
            if i > 0:
                W["v1"] = Wd["rw_v1"][0]
                W["v2"] = Wd["rw_v2"][0]
            if want("pre%d" % i):
                p.barrier()
                stage_pre_rwkv(p, cx, cs, T, i, xcur, vec, vec_r, vcs[i], mod_sb, mod_r, W, dd)
            if want("scan%d" % i):
                p.barrier()
                stage_scan(p, cx, cs, T, dd)
            if want("post%d" % i):
                p.barrier()
                stage_post(p, cx, cs, T, i, "rwkv", xcur, xnext, vec, vec_r, vcs[i], mod_sb, mod_r, Wb["wo%d" % i], Wb["up%d" % i], Wb["dn%d" % i], dd)
                xcur, xnext = xnext, (dd["xb"] if xnext is dd["xa"] else dd["xa"])
        if want("kv"):
            p.barrier()
            stage_kvq(p, cx, cs, T, "kv", 0, xcur, vec, vec_r, c_kvg, mod_sb, mod_r, Wb["kv"], vec[:, c_gain:c_gain + 1], vec_r, dd, fb_sb=nfb[:, 0:1], fb_r=nfb_r, Wf_d=Wd["kv_w"])
            p.barrier()
            stage_fprep(p, cx, T, dd)
        for j in range(2):
            i = 2 + j
            if want("preq%d" % i):
                p.barrier()
                stage_kvq(p, cx, cs, T, "q", i, xcur, vec, vec_r, vcs[i]["norm_mix_g"], mod_sb, mod_r, Wb["qg%d" % j], vec[:, c_gain + 1 + j:c_gain + 2 + j], vec_r, dd)
            if want("attn%d" % i):
                p.barrier()
                stage_attn(p, cx, cs, T, dd)
            if want("post%d" % i):
                p.barrier()
                last = (i == 3)
                stage_post(p, cx, cs, T, i, "fox", xcur, outT if last else xnext, vec, vec_r, vcs[i], mod_sb, mod_r, Wb["fo%d" % j], Wb["up%d" % i], Wb["dn%d" % i], dd,
                           final_g=vec[:, c_fin:c_fin + KC] if last else None)
                xcur, xnext = xnext, (dd["xb"] if xnext is dd["xa"] else dd["xa"])
        if dump:
            p.barrier()
            for n in dump:
                p.dma(dump_out[n], dd[n], w=[p.res("dump_" + n)])
        p.emit()
        stats = p.stats
    return nc, stats


def fm(v):
    return np.ascontiguousarray(np.asarray(v, np.float32).reshape(-1, 128).T)


def host_inputs(inp, b, T):
    vec_parts = []
    for i in range(2):
        tab = {"norm_mix_g": inp["norm_mix_g"][i], "norm_mlp_g": inp["norm_mlp_g"][i], "w0": inp["rw_w0"][i], "a0": inp["rw_a0"][i],
               "v0": inp["rw_v0"][0], "kk": inp["rw_kk"][i], "ka": inp["rw_ka"][i], "rk": inp["rw_rk"][i].reshape(-1),
               "lnw": inp["rw_lnw"][i], "lnb": inp["rw_lnb"][i]}
        for j in range(6):
            tab["mu%d" % j] = inp["rw_mu"][i, j]
        for n in RW_VECS:
            vec_parts.append(fm(tab[n]))
    for i in (2, 3):
        vec_parts.append(fm(inp["norm_mix_g"][i]))
        vec_parts.append(fm(inp["norm_mlp_g"][i]))
    vec_parts.append(fm(inp["kv_norm_g"]))
    vec_parts.append(fm(inp["final_g"]))
    gains = np.zeros((128, 8), np.float32)
    gains[:, 0] = np.tile(np.asarray(inp["kv_kg"], np.float32), 2)
    gains[:, 1] = np.tile(np.asarray(inp["fx_qg"][0], np.float32), 2)
    gains[:, 2] = np.tile(np.asarray(inp["fx_qg"][1], np.float32), 2)
    vec_parts.append(gains)
    vecs = np.ascontiguousarray(np.concatenate(vec_parts, axis=1))
    assert vecs.shape == (128, NV), vecs.shape
    modb = np.concatenate([fm(inp["mod_b"][i]) for i in range(4)] + [fm(inp["kv_mod_b"])], axis=1)
    m = {
        "xT": np.ascontiguousarray(np.asarray(inp["x"][b, :T], np.float32).T),
        "cT": fm(inp["c"][b]),
        "modb": np.ascontiguousarray(modb),
        "vecs": vecs,
        "nfb": np.ascontiguousarray(np.asarray(inp["kv_fb"], np.float32).reshape(16, 1)),
    }
    for n in W_SHAPES:
        m[n] = np.ascontiguousarray(np.asarray(inp[n], np.float32))
    return m


def kernel(**inputs):
    T = 8192
    nc, _ = build_program(T)
    inp = {k: np.asarray(v) for k, v in inputs.items()}
    in_maps = [host_inputs(inp, b, T) for b in range(2)]
    res = run_bass_kernel_spmd(nc, in_maps, core_ids=[0, 1])
    out = np.stack([np.asarray(res.results[b]["outT"]).T for b in range(2)])
    return np.ascontiguousarray(out.astype(np.float32))
```

```python
import os
import numpy as np
from contextlib import ExitStack
import concourse.bass as bass
import concourse.mybir as mybir
from concourse.bass_utils import run_bass_kernel_spmd

F32 = mybir.dt.float32
BF16 = mybir.dt.bfloat16
ALU = mybir.AluOpType
AF = mybir.ActivationFunctionType
AX = mybir.AxisListType

SAME_ENGINE_SYNC = os.environ.get("SES", "0") == "1"


import types as _types


def _snapshot(fn):
    cl = fn.__closure__
    if not cl:
        return fn
    cells = []
    for c in cl:
        try:
            cells.append(_types.CellType(c.cell_contents))
        except ValueError:
            cells.append(c)
    g = _types.FunctionType(fn.__code__, fn.__globals__, fn.__name__, fn.__defaults__, tuple(cells))
    g.__kwdefaults__ = fn.__kwdefaults__
    return g


class Res:
    __slots__ = ("name", "w", "r", "excl")

    def __init__(self, name):
        self.name = name
        self.w = None
        self.r = []
        self.excl = False


class Op:
    __slots__ = ("eng", "fn", "deps", "is_dma", "sem", "count", "signal", "idx", "line", "alldeps")


class Prog:
    ENGS = ("pe", "act", "dve", "pool", "sp")

    def __init__(self, nc, es):
        self.nc = nc
        self.es = es
        self.ops = []
        self.dma_keys = {}
        self.nres = 0
        self._uid = 0
        self.fence = {}
        self.scopes = []
        self.barriers = []

    def sb(self, name, shape, dt=F32):
        es = self.scopes[-1] if self.scopes else self.es
        self._uid += 1
        return es.enter_context(self.nc.sbuf_tensor("%s_u%d" % (name, self._uid), list(shape), dt))

    def scope(self):
        import contextlib

        @contextlib.contextmanager
        def cm():
            with ExitStack() as es2:
                self.scopes.append(es2)
                try:
                    yield es2
                finally:
                    self.scopes.pop()
        return cm()

    def ps(self, name, shape, dt=F32):
        return self.es.enter_context(self.nc.psum_tensor(name, list(shape), dt))

    def res(self, name=None):
        self.nres += 1
        return Res(name or ("r%d" % self.nres))

    def resl(self, n, name="r"):
        return [self.res("%s%d" % (name, i)) for i in range(n)]

    def op(self, eng, fn, r=(), w=(), dma_key=None):
        o = Op()
        o.eng = eng
        o.fn = _snapshot(fn)
        o.is_dma = dma_key is not None
        o.signal = False
        o.sem = dma_key
        o.count = 0
        o.idx = len(self.ops)
        import sys as _sys
        o.line = _sys._getframe(1).f_lineno
        ex = [x for x in r if x.excl]
        if ex:
            w = list(w) + [x for x in ex if x not in w]
            r = [x for x in r if not x.excl]
        deps = {}
        for x in r:
            if x.w is not None:
                deps[x.w.idx] = (x.w, "raw")
        for x in w:
            if x.w is not None:
                deps[x.w.idx] = (x.w, "waw")
            for rd in x.r:
                if rd.idx not in deps:
                    deps[rd.idx] = (rd, "war")
        fd = []
        if self.fence.get(eng):
            for d in self.fence[eng]:
                if d.is_dma or d.eng != eng or eng == "pool":
                    fd.append(d)
            self.fence[eng] = None
        for d, kind in deps.values():
            if d is o:
                continue
            if d.is_dma:
                fd.append(d)
            elif d.eng != eng:
                fd.append(d)
            else:
                if o.is_dma or eng == "pool" or (SAME_ENGINE_SYNC and kind != "war" and eng != "pe"):
                    fd.append(d)
        o.deps = fd
        o.alldeps = [(d.idx, d.eng, d.line, k) for d, k in deps.values()]
        for d in fd:
            d.signal = True
        for x in r:
            x.r.append(o)
        for x in w:
            x.w = o
            x.r = []
        self.ops.append(o)
        return o

    def barrier(self):
        self.barriers.append(len(self.ops))
        last = {}
        for o in self.ops:
            if o.is_dma:
                last[("dma", o.sem)] = o
            else:
                last[o.eng] = o
        ops = list(last.values())
        for e in self.ENGS:
            self.fence[e] = list(ops)

    def dma(self, out, in_, r=(), w=(), key=None, q="sp"):
        if key is None:
            key = w[0].name
        return self.op(q, lambda e, out=out, in_=in_: e.dma_start(out=out, in_=in_), r=r, w=w, dma_key=key)

    def emit(self):
        nc = self.nc
        es = self.es
        ROT = 60000
        eng_sems = {e: [] for e in self.ENGS}
        dma_sem = {}
        dma_cnt = {}
        dma_uses = {}
        cnt = {e: 0 for e in self.ENGS}
        nsem = 0
        all_dma_sems = []
        final_cnt = {}
        free_sems = []
        bset = set(self.barriers)
        for o in self.ops:
            if o.idx in bset:
                for k in list(dma_sem.keys()):
                    free_sems.append((dma_sem.pop(k), dma_cnt.pop(k)))
            if o.is_dma:
                k = o.sem
                if k in dma_sem and dma_cnt[k] > 60000:
                    dma_sem.pop(k)
                    dma_cnt.pop(k)
                if k not in dma_sem:
                    while free_sems and free_sems[0][1] > 50000:
                        free_sems.pop(0)
                    if free_sems:
                        dma_sem[k], dma_cnt[k] = free_sems.pop(0)
                    else:
                        dma_sem[k] = es.enter_context(nc.semaphore("dsem_%d" % nsem))
                        dma_cnt[k] = 0
                        nsem += 1
                        all_dma_sems.append(dma_sem[k])
                dma_cnt[k] += 16
                final_cnt[dma_sem[k].num] = (dma_sem[k], dma_cnt[k])
                o.sem = dma_sem[k]
                o.count = dma_cnt[k]
            elif o.signal:
                n = cnt[o.eng]
                cnt[o.eng] += 1
                si = n // ROT
                if si >= len(eng_sems[o.eng]):
                    eng_sems[o.eng].append(es.enter_context(nc.semaphore("sem_%s%d" % (o.eng, si))))
                    nsem += 1
                o.sem = eng_sems[o.eng][si]
                o.count = n % ROT + 1
        self.stats = dict(cnt)
        self.stats["n_ops"] = len(self.ops)
        self.stats["n_sems"] = nsem
        per = {e: [o for o in self.ops if o.eng == e] for e in self.ENGS}
        final_dma = list(final_cnt.values())

        def run(e, ename):
            seen = {}
            nw = 0
            for o in per[ename]:
                need = {}
                for d in o.deps:
                    s = d.sem
                    if need.get(s.num, (None, 0))[1] < d.count:
                        need[s.num] = (s, d.count)
                for s, c in need.values():
                    if seen.get(s.num, 0) < c:
                        e.wait_ge(s, c)
                        seen[s.num] = c
                        nw += 1
                ins = o.fn(e)
                if o.is_dma:
                    ins.then_inc(o.sem, 16)
                elif o.signal:
                    ins.then_inc(o.sem, 1)
            if ename == "sp":
                for s, c in final_dma:
                    e.wait_ge(s, c)
            self.stats["waits_" + ename] = nw

        with nc.Block() as block:
            block.sync(lambda e: run(e, "sp"))
            block.tensor(lambda e: run(e, "pe"))
            block.scalar(lambda e: run(e, "act"))
            block.vector(lambda e: run(e, "dve"))
            block.gpsimd(lambda e: run(e, "pool"))


D = 1024
KC = 8
NT = 2048
TG = 512
NTG = NT // TG
DFF = 4096
FC = DFF // 128
NORM_EPS = 1e-6
GN_EPS = 64e-5


class Ctx:
    def __init__(self, p):
        self.p = p
        nc = p.nc
        self.ps = [p.ps("psb%d" % i, [128, 512], F32) for i in range(8)]
        self.psr = [p.res("psb%d" % i) for i in range(8)]
        for r_ in self.psr:
            r_.excl = True
        self.psi = 0
        self.rot = list(range(8))
        self.ones_bf = p.sb("ones_bf", [128, 128], BF16)
        self.r_const = p.res("consts")
        self.eps_t = p.sb("eps_t", [128, 4], F32)
        self.ones_f32 = p.sb("ones_f32c", [128, 128], F32)
        p.op("pool", lambda e: e.memset(self.ones_f32[:], 1.0), w=[self.r_const])
        p.op("pool", lambda e: e.memset(self.ones_bf[:], 1.0), w=[self.r_const])
        p.op("pool", lambda e: e.memset(self.eps_t[:, 0:1], NORM_EPS), w=[self.r_const])
        p.op("pool", lambda e: e.memset(self.eps_t[:, 1:2], GN_EPS), w=[self.r_const])
        p.op("pool", lambda e: e.memset(self.eps_t[:, 2:3], 1.0), w=[self.r_const])
        p.op("pool", lambda e: e.memset(self.eps_t[:, 3:4], 0.0), w=[self.r_const])

    def bank(self):
        self.psi = (self.psi + 1) % len(self.rot)
        i = self.rot[self.psi]
        return self.ps[i], self.psr[i]


class WStream:
    def __init__(self, p, name, shape, nbuf=2, cast_engs=("pool",), direct=False):
        self.p = p
        self.shape = shape
        self.nbuf = nbuf
        if not direct:
            self.stg = [p.sb("%s_stg%d" % (name, i), shape, F32) for i in range(nbuf)]
            self.stg_r = [p.res("%s_stg%d" % (name, i)) for i in range(nbuf)]
        self.bf = [p.sb("%s_bf%d" % (name, i), shape, BF16) for i in range(nbuf)]
        self.bf_r = [p.res("%s_bf%d" % (name, i)) for i in range(nbuf)]
        self.i = 0
        self.cast_engs = cast_engs
        self.ci = 0

    def load(self, src_ap, sl=None):
        p = self.p
        i = self.i
        self.i = (i + 1) % self.nbuf
        bf, br = self.bf[i], self.bf_r[i]
        idx = sl if sl is not None else tuple(slice(None) for _ in self.shape)
        if src_ap.dtype == BF16:
            p.dma(bf[idx], src_ap, w=[br])
            return bf, br
        stg, sr = self.stg[i], self.stg_r[i]
        p.dma(stg[idx], src_ap, w=[sr])
        ce = self.cast_engs[self.ci % len(self.cast_engs)]
        self.ci += 1
        copy_op(p, ce, bf[idx], stg[idx], r=[sr], w=[br])
        return bf, br


def copy_op(p, eng, out, in_, r, w):
    if eng == "act":
        return p.op("act", lambda e: e.copy(out=out, in_=in_), r=r, w=w)
    return p.op(eng, lambda e: e.tensor_copy(out=out, in_=in_), r=r, w=w)


def emit_norm(p, cx, x_sb, x_r, gmul, shift, vec_r, h_out, h_r, scr):
    ts = slice(0, TG)
    sq, sq_r = scr["sq"]
    rstd, rstd_r = scr["rstd"]
    tmp, tmp_r = scr["tmp"]
    p.op("act", lambda e: e.activation(out=sq[:], in_=x_sb[:, :, ts], func=AF.Square), r=[x_r], w=[sq_r])
    ps, pr = cx.bank()
    for kc in range(KC):
        p.op("pe", lambda e, kc=kc: e.matmul(ps[:, 0:TG], cx.ones_bf[:], sq[:, kc, :], start=(kc == 0), stop=(kc == KC - 1)),
             r=[sq_r, cx.r_const], w=[pr])
    p.op("act", lambda e: e.activation(out=rstd[:], in_=ps[:, 0:TG], func=AF.Sqrt, bias=cx.eps_t[:, 0:1], scale=1.0 / D),
         r=[pr, cx.r_const], w=[rstd_r])
    p.op("dve", lambda e: e.reciprocal(out=rstd[:], in_=rstd[:]), r=[rstd_r], w=[rstd_r])
    for kc in range(KC):
        if shift is None:
            p.op("dve", lambda e, kc=kc: e.scalar_tensor_tensor(out=h_out[:, kc, :], in0=x_sb[:, kc, ts], scalar=gmul[:, kc:kc + 1],
                                                               in1=rstd[:], op0=ALU.mult, op1=ALU.mult),
                 r=[x_r, rstd_r, vec_r], w=[h_r])
        else:
            t2, t2r = tmp[kc % 2], tmp_r[kc % 2]
            p.op("dve", lambda e, kc=kc, t2=t2: e.scalar_tensor_tensor(out=t2[:], in0=x_sb[:, kc, ts], scalar=gmul[:, kc:kc + 1],
                                                                      in1=rstd[:], op0=ALU.mult, op1=ALU.mult),
                 r=[x_r, rstd_r, vec_r], w=[t2r])
            p.op("act", lambda e, kc=kc, t2=t2: e.activation(out=h_out[:, kc, :], in_=t2[:], func=AF.Identity,
                                                            bias=shift[:, kc:kc + 1], scale=1.0),
                 r=[t2r, vec_r], w=[h_r])


def norm_scratch(p, name):
    return {
        "sq": (p.sb(name + "_sq", [128, KC, TG], BF16), p.res(name + "_sq")),
        "rstd": (p.sb(name + "_rstd", [128, TG], F32), p.res(name + "_rstd")),
        "tmp": ([p.sb(name + "_tmp%d" % i, [128, TG], F32) for i in range(2)], [p.res(name + "_tmp%d" % i) for i in range(2)]),
    }


def emit_mlp(p, cx, xt, x_r, gm, gm_r, shift, vec_r, up_d, down_d, st):
    h, h_r, a, a_r, relu, relu_r, wu, wd, scr = st["h"], st["h_r"], st["a"], st["a_r"], st["relu"], st["relu_r"], st["wu"], st["wd"], st["scr"]
    emit_norm(p, cx, xt, x_r, gm[:, 0:KC], shift, vec_r, h, h_r, scr)
    for f2 in range(FC // 2):
        wb, wr = wu.load(up_d[f2])
        for fi in range(2):
            fc = f2 * 2 + fi
            ps, pr = cx.bank()
            for kc in range(KC):
                p.op("pe", lambda e, kc=kc, ps=ps, wb=wb, fi=fi: e.matmul(ps[:, 0:TG], wb[:, kc, fi * 128:(fi + 1) * 128], h[:, kc, :],
                                                                      start=(kc == 0), stop=(kc == KC - 1)),
                     r=[wr, h_r], w=[pr])
            ri = st["ri"]
            st["ri"] += 1
            rl, rr = relu[ri % 2], relu_r[ri % 2]
            p.op("act", lambda e, ps=ps, rl=rl: e.activation(out=rl[:], in_=ps[:, 0:TG], func=AF.Relu), r=[pr], w=[rr])
            p.op("dve", lambda e, ps=ps, rl=rl, fc=fc: e.scalar_tensor_tensor(
                out=a[:, fc, :], in0=ps[:, 0:TG], scalar=0.0, in1=rl[:], op0=ALU.max, op1=ALU.mult),
                r=[pr, rr], w=[a_r])
    for oc in range(KC):
        ps, pr = cx.bank()
        wb, wr = wd.load(down_d[oc])
        for fc in range(FC):
            p.op("pe", lambda e, fc=fc, ps=ps, wb=wb: e.matmul(ps[:, 0:TG], wb[:, fc, :], a[:, fc, :],
                                                             start=(fc == 0), stop=(fc == FC - 1)),
                 r=[wr, a_r], w=[pr])
        xs = xt[:, oc, :]
        p.op("dve", lambda e, ps=ps, xs=xs, oc=oc: e.scalar_tensor_tensor(out=xs, in0=ps[:, 0:TG], scalar=gm[:, KC + oc:KC + oc + 1],
                                                                      in1=xs, op0=ALU.mult, op1=ALU.add),
             r=[pr, gm_r, x_r], w=[x_r])


def mlp_state(p):
    return {
        "h": p.sb("mlp_h", [128, KC, TG], BF16), "h_r": p.res("mlp_h"),
        "a": p.sb("mlp_a", [128, FC, TG], BF16), "a_r": p.res("mlp_a"),
        "relu": [p.sb("mlp_relu%d" % i, [128, TG], F32) for i in range(2)], "relu_r": p.resl(2, "mlp_relu"),
        "wu": WStream(p, "wup", [128, KC, 256], nbuf=3, direct=True),
        "wd": WStream(p, "wdn", [128, FC, 128], nbuf=2, direct=True),
        "scr": norm_scratch(p, "mlpn"), "ri": 0,
    }


def mod_gm(p, name, vec, vec_r, col_g, col_sc, col_gt):
    gm = p.sb(name, [128, 2 * KC], F32)
    gm_r = p.res(name)
    p.op("dve", lambda e: e.scalar_tensor_tensor(out=gm[:, 0:KC], in0=vec[:, col_sc:col_sc + KC], scalar=1.0,
                                                in1=vec[:, col_g:col_g + KC], op0=ALU.add, op1=ALU.mult), r=[vec_r], w=[gm_r])
    if col_gt is not None:
        p.op("dve", lambda e: e.tensor_scalar_add(out=gm[:, KC:2 * KC], in0=vec[:, col_gt:col_gt + KC], scalar1=1.0), r=[vec_r], w=[gm_r])
    return gm, gm_r


import os
PRE_STOP = int(os.environ.get('PRE_STOP', '99'))
USE_F32R = os.environ.get("USE_F32R", "1") == "1"


def R32(ap):
    return ap.bitcast(mybir.dt.float32r) if USE_F32R else ap


H = 16
HD = 64
NMODC = 208
LW_SCALE = -0.6065306597126334


def vec_cols(names):
    return {n: i * KC for i, n in enumerate(names)}


RW_VECS = ["norm_mix_g", "norm_mlp_g", "mu0", "mu1", "mu2", "mu3", "mu4", "mu5", "w0", "a0", "v0", "kk", "ka", "rk", "lnw", "lnb"]
FX_VECS = ["norm_mix_g", "norm_mlp_g"]


class Consts:
    def __init__(self, p, cx):
        self.r = p.res("consts2")
        self.bo1 = p.sb("bo1", [128, 128], F32)
        self.bo64 = p.sb("bo64", [128, 128], F32)
        self.ident = p.sb("ident", [128, 128], F32)
        self.ones_f = p.sb("ones_f", [128, 128], F32)
        for t, v in ((self.bo1, 1.0), (self.bo64, 1.0 / 64)):
            p.op("pool", lambda e, t=t: e.memset(t[:], 0.0), w=[self.r])
            p.op("pool", lambda e, t=t, v=v: e.memset(t[0:64, 0:64], v), w=[self.r])
            p.op("pool", lambda e, t=t, v=v: e.memset(t[64:128, 64:128], v), w=[self.r])
        p.op("pool", lambda e: e.memset(self.ones_f[:], 1.0), w=[self.r])
        p.op("pool", lambda e: e.affine_select(out=self.ident[:], in_=self.ones_f[:], pattern=[[1, 128]], compare_op=ALU.is_equal,
                                               fill=0.0, base=0, channel_multiplier=-1), r=[self.r], w=[self.r])


def stage_mod(p, cx, cT_d, modw_d, kvmodw_d, modb_d, mod_sb, mod_r):
    with p.scope() as es2:
        nc = p.nc
        sb = p.sb
        cT = sb("mod_cT", [128, KC])
        ca = sb("mod_ca", [128, KC, 128])
        bias = sb("mod_bias", [128, NMODC])
        r_c = p.res("mod_c")
        r_b = p.res("mod_b")
        p.dma(cT[:], cT_d, w=[r_c])
        p.dma(bias[:], modb_d, w=[r_b])
        sig = sb("mod_sig", [128, KC])
        p.op("act", lambda e: e.activation(out=sig[:], in_=cT[:], func=AF.Sigmoid), r=[r_c], w=[r_c])
        p.op("dve", lambda e: e.tensor_tensor(out=sig[:], in0=cT[:], in1=sig[:], op=ALU.mult), r=[r_c], w=[r_c])
        for kc in range(KC):
            p.op("dve", lambda e, kc=kc: e.tensor_scalar_mul(out=ca[:, kc, :], in0=cx_ones(cx), scalar1=sig[:, kc:kc + 1]), r=[r_c, cx.r_const], w=[r_c])
        wst = [sb("mod_w%d" % i, [128, KC, 512]) for i in range(2)]
        wr = p.resl(2, "mod_w")
        blocks = []
        for i in range(4):
            v = modw_d[i].rearrange("(kc p) m -> p kc m", p=128)
            for j in range(12):
                blocks.append((v[:, :, j * 512:(j + 1) * 512], i * 48 + j * 4))
        v = kvmodw_d.rearrange("(kc p) m -> p kc m", p=128)
        for j in range(4):
            blocks.append((v[:, :, j * 512:(j + 1) * 512], 192 + j * 4))
        for bi, (src, c0) in enumerate(blocks):
            w, r = wst[bi % 2], wr[bi % 2]
            p.dma(w[:], src, w=[r])
            ps, pr = cx.bank()
            for j in range(4):
                for kc in range(KC):
                    p.op("pe", lambda e, w=w, j=j, kc=kc, ps=ps: e.matmul(ps[:, j * 128:(j + 1) * 128], w[:, kc, j * 128:(j + 1) * 128], ca[:, kc, :],
                                                                       start=(kc == 0), stop=(kc == KC - 1)),
                         r=[r, r_c], w=[pr])
            psv = ps[:, 0:512].rearrange("p (j c) -> p j c", j=4)
            p.op("dve", lambda e, psv=psv, c0=c0: e.tensor_tensor(out=mod_sb[:, c0:c0 + 4], in0=psv[:, :, 0], in1=bias[:, c0:c0 + 4], op=ALU.add), r=[pr, r_b], w=[mod_r])


def cx_ones(cx):
    return cx.ones_f32[:]


class ProjW:
    def __init__(self, p, name):
        self.ws = WStream(p, name, [128, KC, 256], nbuf=3, direct=True)
        self.p = p

    def load(self, W_d, c0, n=256):
        return self.ws.load(W_d[c0 // 256])


def proj_mm(p, cx, wb, wr, off, h, h_r, ncols=128):
    ps, pr = cx.bank()
    for kc in range(KC):
        p.op("pe", lambda e, kc=kc: e.matmul(ps[0:ncols, 0:TG], wb[:, kc, off:off + ncols], h[:, kc, :], start=(kc == 0), stop=(kc == KC - 1)),
             r=[wr, h_r], w=[pr])
    return ps, pr


def head_sum(p, cx, cs, mat, src, src_r):
    ps, pr = cx.bank()
    p.op("pe", lambda e: e.matmul(ps[:, 0:TG], mat[:], src[:], start=True, stop=True), r=[src_r, cs.r], w=[pr])
    return ps, pr


class Rot:
    def __init__(self, p, name, shape, dt, n):
        self.t = [p.sb("%s%d" % (name, i), shape, dt) for i in range(n)]
        self.r = [p.res("%s%d" % (name, i)) for i in range(n)]
        self.i = 0

    def get(self):
        i = self.i
        self.i = (i + 1) % len(self.t)
        return self.t[i], self.r[i]


OUT_Q = os.environ.get("OUT_Q", "sp")


def out_dma(p, dst, src, src_r, key, dst_r=None):
    p.dma(dst, src, r=[src_r], w=[dst_r if dst_r is not None else p.res("o_" + key)], key="o_" + src_r.name, q=OUT_Q)


def stage_pre_rwkv(p, cx, cs, T, layer, x_d, vec, vec_r, vc, mod_sb, mod_r, W, dd):
    nc = p.nc
    NTL = T // TG
    mc = layer * 48
    with p.scope() as es2:
        sb = p.sb
        gm = sb("pr_gm", [128, 2 * KC])
        gm_r = p.res("pr_gm")
        g0 = vc["norm_mix_g"]
        p.op("dve", lambda e: e.scalar_tensor_tensor(out=gm[:, 0:KC], in0=mod_sb[:, mc + 8:mc + 16], scalar=1.0, in1=vec[:, g0:g0 + KC],
                                                    op0=ALU.add, op1=ALU.mult), r=[vec_r, mod_r], w=[gm_r])
        ka0 = vc["ka"]
        p.op("dve", lambda e: e.tensor_scalar(out=gm[:, KC:2 * KC], in0=vec[:, ka0:ka0 + KC], scalar1=-1.0, scalar2=1.0, op0=ALU.mult, op1=ALU.add),
             r=[vec_r], w=[gm_r])
        shift = mod_sb[:, mc:mc + 8]
        lw_r = p.res("pr_lora")
        specs = [("a1", W["a1"].rearrange("(kc p) m -> p kc m", p=128), [128, KC, 64]),
                 ("w1", W["w1"].rearrange("(kc p) m -> p kc m", p=128), [128, KC, 64]),
                 ("g1", W["g1"].rearrange("(kc p) m -> p kc m", p=128), [128, KC, 160]),
                 ("a2", W["a2"], [64, D]), ("w2", W["w2"], [64, D]),
                 ("g2a", W["g2"][0:128, :], [128, D]), ("g2b", W["g2"][128:160, :], [32, D])]
        if layer > 0:
            specs += [("v1", W["v1"].rearrange("(kc p) m -> p kc m", p=128), [128, KC, 32]), ("v2", W["v2"], [32, D])]
        lt_ = {n: sb("prb_" + n, shp, BF16) for n, _, shp in specs}
        with p.scope() as stg_es:
            for n, src, shp in specs:
                st = p.sb("prs_" + n, shp)
                rr = p.res("prs_" + n)
                p.dma(st[:], src, w=[rr])
                p.op("pool", lambda e, n=n, st=st: e.tensor_copy(out=lt_[n][:], in_=st[:]), r=[rr], w=[lw_r])
        a1, w1, g1, a2, w2, g2a, g2b = (lt_[n] for n in ("a1", "w1", "g1", "a2", "w2", "g2a", "g2b"))
        if layer > 0:
            v1, v2 = lt_["v1"], lt_["v2"]
        p.barrier()
        xt = Rot(p, "pr_x", [128, KC, TG], F32, 1)
        hh = sb("pr_h", [128, KC, TG + 1])
        hh_r = p.res("pr_h")
        xx = sb("pr_xx", [128, KC, TG], BF16)
        xx_r = p.res("pr_xx")
        xj = Rot(p, "pr_xj", [128, KC, TG], BF16, 1)
        a_t = sb("pr_a", [128, KC, TG])
        a_r = p.res("pr_a")
        k_t = sb("pr_k", [128, KC, TG])
        k_r = p.res("pr_k")
        v_t = sb("pr_v", [128, KC, TG])
        v_r = p.res("pr_v")
        scr = norm_scratch(p, "prn")
        pw = ProjW(p, "pr_w")
        f1 = Rot(p, "pr_f1", [128, TG], F32, 2)
        f2 = Rot(p, "pr_f2", [128, TG], F32, 2)
        ob = Rot(p, "pr_ob", [128, TG], F32, 3)
        obn = Rot(p, "pr_obn", [128, TG], F32, int(os.environ.get("OBN", "2")))
        if os.environ.get("PRE_INIT"):
            for t_, r_ in zip(obn.t, obn.r):
                p.op("pool", lambda e, t_=t_: e.memset(t_[:], 0.0), w=[r_])
        lt = Rot(p, "pr_lt", [128, TG], BF16, 2)
        xv = x_d.rearrange("(c p) t -> p c t", p=128)
        p.op("pool", lambda e: e.memset(hh[:, :, 0:1], 0.0), w=[hh_r])
        if os.environ.get("PRE_PAD"):
            dr = p.res("padr")
            for i_ in range(int(os.environ["PRE_PAD"])):
                p.op("dve", lambda e: e.tensor_scalar_mul(out=xx[:, 0, 0:8], in0=xx[:, 0, 0:8], scalar1=1.0), r=[dr], w=[dr])

        def mix_dve(j):
            t, r = xj.get()
            m0 = vc["mu%d" % j]
            for kc in range(KC):
                p.op("dve", lambda e, kc=kc: e.scalar_tensor_tensor(
                    out=t[:, kc, :], in0=xx[:, kc, :], scalar=vec[:, m0 + kc:m0 + kc + 1], in1=hh[:, kc, 1:TG + 1], op0=ALU.mult, op1=ALU.add),
                    r=[xx_r, hh_r, vec_r], w=[r])
            return t, r

        def lora1(xm, xm_r, w1t, R, func):
            ps, pr = cx.bank()
            for kc in range(KC):
                p.op("pe", lambda e, kc=kc: e.matmul(ps[0:R, 0:TG], w1t[:, kc, 0:R], xm[:, kc, :], start=(kc == 0), stop=(kc == KC - 1)),
                     r=[lw_r, xm_r], w=[pr])
            t, r = lt.get()
            p.op("act", lambda e: e.activation(out=t[0:R, :], in_=ps[0:R, 0:TG], func=func), r=[pr], w=[r])
            return t, r

        for ti in range(NTL):
            cols = slice(ti * TG, (ti + 1) * TG)
            x_t, x_r = xt.get()
            p.dma(x_t[:], xv[:, :, cols], w=[x_r])
            if ti > 0:
                p.op("pool", lambda e: e.tensor_copy(out=hh[:, :, 0:1], in_=hh[:, :, TG:TG + 1]), r=[hh_r], w=[hh_r])
            emit_norm(p, cx, x_t, x_r, gm[:, 0:KC], shift, gm_r, hh[:, :, 1:TG + 1], hh_r, scr)
            p.op("dve", lambda e: e.tensor_tensor(out=xx[:], in0=hh[:, :, 0:TG], in1=hh[:, :, 1:TG + 1], op=ALU.subtract), r=[hh_r], w=[xx_r])
            if PRE_STOP <= 0:
                continue
            xm, xm_r = mix_dve(4)
            ta, ta_r = lora1(xm, xm_r, a1, 64, AF.Identity)
            for oc in range(KC):
                ps, pr = cx.bank()
                p.op("pe", lambda e, oc=oc, ps=ps: e.matmul(ps[:, 0:TG], a2[:, oc * 128:(oc + 1) * 128], ta[0:64, :], start=True, stop=True),
                     r=[lw_r, ta_r], w=[pr])
                c0 = vc["a0"] + oc
                p.op("act", lambda e, oc=oc, ps=ps, c0=c0: e.activation(out=a_t[:, oc, :], in_=ps[:, 0:TG], func=AF.Sigmoid, bias=vec[:, c0:c0 + 1], scale=1.0),
                     r=[pr, vec_r], w=[a_r])
            if PRE_STOP <= 1:
                continue
            xm, xm_r = mix_dve(2)
            for o2 in range(KC // 2):
                wb, wr = pw.load(W["wk"], o2 * 256)
                for oi in range(2):
                    oc = o2 * 2 + oi
                    ps, pr = proj_mm(p, cx, wb, wr, oi * 128, xm, xm_r)
                    kkr, kkr_r = f1.get()
                    ck = vc["kk"] + oc
                    p.op("dve", lambda e, ps=ps, kkr=kkr, ck=ck: e.tensor_scalar_mul(out=kkr[:], in0=ps[:, 0:TG], scalar1=vec[:, ck:ck + 1]), r=[pr, vec_r], w=[kkr_r])
                    sq, sq_r = f2.get()
                    p.op("act", lambda e, kkr=kkr, sq=sq: e.activation(out=sq[:], in_=kkr[:], func=AF.Square), r=[kkr_r], w=[sq_r])
                    ps2, pr2 = head_sum(p, cx, cs, cs.bo1, sq, sq_r)
                    rn, rn_r = f2.get()
                    p.op("act", lambda e, ps2=ps2, rn=rn: e.activation(out=rn[:], in_=ps2[:, 0:TG], func=AF.Sqrt), r=[pr2], w=[rn_r])
                    p.op("dve", lambda e, rn=rn: e.tensor_scalar_max(out=rn[:], in0=rn[:], scalar1=1e-12), r=[rn_r], w=[rn_r])
                    p.op("dve", lambda e, rn=rn: e.reciprocal(out=rn[:], in_=rn[:]), r=[rn_r], w=[rn_r])
                    nk, nk_r = ob.get()
                    p.op("dve", lambda e, nk=nk, kkr=kkr, rn=rn: e.scalar_tensor_tensor(out=nk[:], in0=kkr[:], scalar=-1.0, in1=rn[:], op0=ALU.mult, op1=ALU.mult),
                         r=[kkr_r, rn_r], w=[nk_r])
                    out_dma(p, dd["a"][oc * 128:(oc + 1) * 128, cols], nk[:], nk_r, "pr_oa")
                    bt, bt_r = ob.get()
                    p.op("dve", lambda e, bt=bt, nk=nk, oc=oc: e.scalar_tensor_tensor(out=bt[:], in0=nk[:], scalar=-1.0, in1=a_t[:, oc, :], op0=ALU.mult, op1=ALU.mult),
                         r=[nk_r, a_r], w=[bt_r])
                    out_dma(p, dd["b"][oc * 128:(oc + 1) * 128, cols], bt[:], bt_r, "pr_ob")
                    tm, tm_r = f1.get()
                    cka = vc["ka"] + oc
                    p.op("dve", lambda e, tm=tm, oc=oc, cka=cka: e.tensor_scalar(out=tm[:], in0=a_t[:, oc, :], scalar1=vec[:, cka:cka + 1],
                                                                             scalar2=gm[:, KC + oc:KC + oc + 1], op0=ALU.mult, op1=ALU.add),
                         r=[a_r, vec_r, gm_r], w=[tm_r])
                    p.op("dve", lambda e, tm=tm, ps=ps, oc=oc: e.tensor_tensor(out=k_t[:, oc, :], in0=ps[:, 0:TG], in1=tm[:], op=ALU.mult),
                         r=[pr, tm_r], w=[k_r])
            out_dma(p, dd["k"].rearrange("(c p) t -> p c t", p=128)[:, :, cols], k_t[:], k_r, "pr_ok")
            if PRE_STOP <= 2:
                continue
            xm, xm_r = mix_dve(3)
            if layer > 0:
                tv, tv_r = lora1(xm, xm_r, v1, 32, AF.Identity)
            for o2 in range(KC // 2):
                wb, wr = pw.load(W["wv"], o2 * 256)
                for oi in range(2):
                    oc = o2 * 2 + oi
                    ps, pr = proj_mm(p, cx, wb, wr, oi * 128, xm, xm_r)
                    if layer == 0:
                        p.op("act", lambda e, ps=ps, oc=oc: e.copy(out=v_t[:, oc, :], in_=ps[:, 0:TG]), r=[pr], w=[v_r])
                    else:
                        ps2, pr2 = cx.bank()
                        p.op("pe", lambda e, oc=oc, ps2=ps2: e.matmul(ps2[:, 0:TG], v2[:, oc * 128:(oc + 1) * 128], tv[0:32, :], start=True, stop=True),
                             r=[lw_r, tv_r], w=[pr2])
                        gt, gt_r = f1.get()
                        c0 = vc["v0"] + oc
                        p.op("act", lambda e, gt=gt, ps2=ps2, c0=c0: e.activation(out=gt[:], in_=ps2[:, 0:TG], func=AF.Sigmoid, bias=vec[:, c0:c0 + 1], scale=1.0),
                             r=[pr2, vec_r], w=[gt_r])
                        vf, vf_r = f2.get()
                        p.dma(vf[:], dd["vfirst"][oc * 128:(oc + 1) * 128, cols], w=[vf_r])
                        p.op("dve", lambda e, vf=vf, ps=ps: e.tensor_tensor(out=vf[:], in0=vf[:], in1=ps[:, 0:TG], op=ALU.subtract), r=[vf_r, pr], w=[vf_r])
                        p.op("dve", lambda e, vf=vf, gt=gt: e.tensor_tensor(out=vf[:], in0=vf[:], in1=gt[:], op=ALU.mult), r=[vf_r, gt_r], w=[vf_r])
                        p.op("dve", lambda e, vf=vf, ps=ps, oc=oc: e.tensor_tensor(out=v_t[:, oc, :], in0=vf[:], in1=ps[:, 0:TG], op=ALU.add), r=[vf_r, pr], w=[v_r])
            out_dma(p, dd["v"].rearrange("(c p) t -> p c t", p=128)[:, :, cols], v_t[:], v_r, "pr_ov")
            if layer == 0:
                out_dma(p, dd["vfirst"].rearrange("(c p) t -> p c t", p=128)[:, :, cols], v_t[:], v_r, "pr_ovf")
            if PRE_STOP <= 3:
                continue
            xm, xm_r = mix_dve(0)
            for o2 in range(KC // 2):
                wb, wr = pw.load(W["wr"], o2 * 256)
                for oi in range(2):
                    oc = o2 * 2 + oi
                    ps, pr = proj_mm(p, cx, wb, wr, oi * 128, xm, xm_r)
                    rt, rt_r = ob.get()
                    p.op("act", lambda e, rt=rt, ps=ps: e.copy(out=rt[:], in_=ps[:, 0:TG]), r=[pr], w=[rt_r])
                    out_dma(p, dd["r"][oc * 128:(oc + 1) * 128, cols], rt[:], rt_r, "pr_or")
                    if os.environ.get("PRE_NOBONUS"):
                        continue
                    rk, rk_r = f1.get()
                    crk = vc["rk"] + oc
                    p.op("dve", lambda e, rk=rk, ps=ps, oc=oc, crk=crk: e.scalar_tensor_tensor(out=rk[:], in0=ps[:, 0:TG], scalar=vec[:, crk:crk + 1],
                                                                                          in1=k_t[:, oc, :], op0=ALU.mult, op1=ALU.mult),
                         r=[pr, vec_r, k_r], w=[rk_r])
                    if os.environ.get("PRE_NOBONUS") == "2":
                        continue
                    ps2, pr2 = head_sum(p, cx, cs, cs.bo1, rk, rk_r)
                    if os.environ.get("PRE_NOBONUS") == "3":
                        continue
                    bn, bn_r = obn.get()
                    p.op("dve", lambda e, bn=bn, ps2=ps2, oc=oc: e.tensor_tensor(out=bn[:], in0=ps2[:, 0:TG], in1=v_t[:, oc, :], op=ALU.mult),
                         r=[pr2, v_r], w=[bn_r])
                    if os.environ.get("PRE_NOBONUS") == "5":
                        p.op("pool", lambda e, bn=bn: e.tensor_scalar_mul(out=bn[:], in0=bn[:], scalar1=1.0), r=[bn_r], w=[bn_r])
                    elif os.environ.get("PRE_NOBONUS") == "8":
                        b2, b2_r = ob.get()
                        p.op("pool", lambda e, bn=bn, b2=b2: e.tensor_copy(out=b2[:], in_=bn[:]), r=[bn_r], w=[b2_r])
                        out_dma(p, dd["bonus"][oc * 128:(oc + 1) * 128, cols], b2[:], b2_r, "pr_obn")
                    elif os.environ.get("PRE_NOBONUS") == "6":
                        out_dma(p, dd["bonus"][oc * 128:(oc + 1) * 128, cols], rt[:], rt_r, "pr_obn")
                    elif os.environ.get("PRE_NOBONUS") != "4":
                        out_dma(p, dd["bonus"][oc * 128:(oc + 1) * 128, cols], bn[:], bn_r, "pr_obn")
            if PRE_STOP <= 4:
                continue
            xm, xm_r = mix_dve(1)
            tw, tw_r = lora1(xm, xm_r, w1, 64, AF.Tanh)
            for oc in range(KC):
                ps, pr = cx.bank()
                p.op("pe", lambda e, oc=oc, ps=ps: e.matmul(ps[:, 0:TG], w2[:, oc * 128:(oc + 1) * 128], tw[0:64, :], start=True, stop=True),
                     r=[lw_r, tw_r], w=[pr])
                wt, wt_r = ob.get()
                c0 = vc["w0"] + oc
                p.op("act", lambda e, wt=wt, ps=ps, c0=c0: e.activation(out=wt[:], in_=ps[:, 0:TG], func=AF.Sigmoid, bias=vec[:, c0:c0 + 1], scale=1.0),
                     r=[pr, vec_r], w=[wt_r])
                p.op("pool", lambda e, wt=wt: e.tensor_scalar_mul(out=wt[:], in0=wt[:], scalar1=LW_SCALE), r=[wt_r], w=[wt_r])
                out_dma(p, dd["lw"][oc * 128:(oc + 1) * 128, cols], wt[:], wt_r, "pr_ow")
            if PRE_STOP <= 5:
                continue
            xm, xm_r = mix_dve(5)
            tg0, tg0_r = lora1(xm, xm_r, g1, 128, AF.Sigmoid)
            ps, pr = cx.bank()
            for kc in range(KC):
                p.op("pe", lambda e, kc=kc, ps=ps: e.matmul(ps[0:32, 0:TG], g1[:, kc, 128:160], xm[:, kc, :], start=(kc == 0), stop=(kc == KC - 1)),
                     r=[lw_r, xm_r], w=[pr])
            tg1, tg1_r = lt.get()
            p.op("act", lambda e, ps=ps: e.activation(out=tg1[0:32, :], in_=ps[0:32, 0:TG], func=AF.Sigmoid), r=[pr], w=[tg1_r])
            for oc in range(KC):
                ps, pr = cx.bank()
                p.op("pe", lambda e, oc=oc, ps=ps: e.matmul(ps[:, 0:TG], g2a[:, oc * 128:(oc + 1) * 128], tg0[:, :], start=True, stop=False),
                     r=[lw_r, tg0_r], w=[pr])
                p.op("pe", lambda e, oc=oc, ps=ps: e.matmul(ps[:, 0:TG], g2b[:, oc * 128:(oc + 1) * 128], tg1[0:32, :], start=False, stop=True),
                     r=[lw_r, tg1_r], w=[pr])
                gt, gt_r = ob.get()
                p.op("act", lambda e, gt=gt, ps=ps: e.copy(out=gt[:], in_=ps[:, 0:TG]), r=[pr], w=[gt_r])
                out_dma(p, dd["g"][oc * 128:(oc + 1) * 128, cols], gt[:], gt_r, "pr_og")


def stage_post(p, cx, cs, T, layer, variant, x_in_d, x_out_d, vec, vec_r, vc, mod_sb, mod_r, wo_d, up_d, down_d, dd, final_g=None):
    nc = p.nc
    NTL = T // TG
    mc = layer * 48
    with p.scope() as es2:
        sb = p.sb
        gm1 = sb("po_gm1", [128, KC])
        gm1_r = p.res("po_gm1")
        p.op("dve", lambda e: e.tensor_scalar_add(out=gm1[:], in0=mod_sb[:, mc + 16:mc + 24], scalar1=1.0), r=[mod_r], w=[gm1_r])
        gm2 = sb("po_gm2", [128, 2 * KC])
        gm2_r = p.res("po_gm2")
        g0 = vc["norm_mlp_g"]
        p.op("dve", lambda e: e.scalar_tensor_tensor(out=gm2[:, 0:KC], in0=mod_sb[:, mc + 32:mc + 40], scalar=1.0, in1=vec[:, g0:g0 + KC],
                                                    op0=ALU.add, op1=ALU.mult), r=[vec_r, mod_r], w=[gm2_r])
        p.op("dve", lambda e: e.tensor_scalar_add(out=gm2[:, KC:2 * KC], in0=mod_sb[:, mc + 40:mc + 48], scalar1=1.0), r=[mod_r], w=[gm2_r])
        shift2 = mod_sb[:, mc + 24:mc + 32]
        st = mlp_state(p)
        xt = Rot(p, "po_x", [128, KC, TG], F32, 1)
        z = sb("po_z", [128, KC, TG], BF16)
        z_r = p.res("po_z")
        i1 = Rot(p, "po_i1", [128, TG], F32, 2)
        i3 = Rot(p, "po_i3", [128, TG], F32, 2)
        if variant == "rwkv":
            i2 = Rot(p, "po_i2", [128, TG], F32, 2)
            f1 = Rot(p, "po_f1", [128, TG], F32, 2)
        pw = ProjW(p, "po_w")
        xv = x_in_d.rearrange("(c p) t -> p c t", p=128)
        ov = x_out_d.rearrange("(c p) t -> p c t", p=128)
        for ti in range(NTL):
            cols = slice(ti * TG, (ti + 1) * TG)
            x_t, x_r = xt.get()
            p.dma(x_t[:], xv[:, :, cols], w=[x_r])
            for oc in range(KC):
                rows = slice(oc * 128, (oc + 1) * 128)
                if variant == "rwkv":
                    y, y_r = i1.get()
                    p.dma(y[:], dd["y"][rows, cols], w=[y_r])
                    bn, bn_r = i2.get()
                    p.dma(bn[:], dd["bonus"][rows, cols], w=[bn_r])
                    g, g_r = i3.get()
                    p.dma(g[:], dd["g"][rows, cols], w=[g_r])
                    ps, pr = head_sum(p, cx, cs, cs.bo64, y, y_r)
                    p.op("dve", lambda e, y=y, ps=ps: e.tensor_tensor(out=y[:], in0=y[:], in1=ps[:, 0:TG], op=ALU.subtract), r=[y_r, pr], w=[y_r])
                    sq, sq_r = f1.get()
                    p.op("act", lambda e, y=y, sq=sq: e.activation(out=sq[:], in_=y[:], func=AF.Square), r=[y_r], w=[sq_r])
                    ps2, pr2 = head_sum(p, cx, cs, cs.bo64, sq, sq_r)
                    p.op("act", lambda e, sq=sq, ps2=ps2: e.activation(out=sq[:], in_=ps2[:, 0:TG], func=AF.Sqrt, bias=cx.eps_t[:, 1:2], scale=1.0),
                         r=[pr2, cx.r_const], w=[sq_r])
                    p.op("dve", lambda e, sq=sq: e.reciprocal(out=sq[:], in_=sq[:]), r=[sq_r], w=[sq_r])
                    p.op("dve", lambda e, y=y, sq=sq: e.tensor_tensor(out=y[:], in0=y[:], in1=sq[:], op=ALU.mult), r=[y_r, sq_r], w=[y_r])
                    cw, cb = vc["lnw"] + oc, vc["lnb"] + oc
                    p.op("dve", lambda e, y=y, cw=cw, cb=cb: e.tensor_scalar(out=y[:], in0=y[:], scalar1=vec[:, cw:cw + 1], scalar2=vec[:, cb:cb + 1],
                                                                         op0=ALU.mult, op1=ALU.add), r=[y_r, vec_r], w=[y_r])
                    p.op("pool", lambda e, y=y, bn=bn: e.tensor_tensor(out=y[:], in0=y[:], in1=bn[:], op=ALU.add), r=[y_r, bn_r], w=[y_r])
                    p.op("pool", lambda e, y=y, g=g, oc=oc: e.tensor_tensor(out=z[:, oc, :], in0=y[:], in1=g[:], op=ALU.mult), r=[y_r, g_r], w=[z_r])
                else:
                    o, o_r = i1.get()
                    p.dma(o[:], dd["o"][rows, cols], w=[o_r])
                    g, g_r = i3.get()
                    p.dma(g[:], dd["sig"][rows, cols], w=[g_r])
                    p.op("pool", lambda e, o=o, g=g, oc=oc: e.tensor_tensor(out=z[:, oc, :], in0=o[:], in1=g[:], op=ALU.mult), r=[o_r, g_r], w=[z_r])
            for o2 in range(KC // 2):
                wb, wr = pw.load(wo_d, o2 * 256)
                for oi in range(2):
                    oc = o2 * 2 + oi
                    ps, pr = proj_mm(p, cx, wb, wr, oi * 128, z, z_r)
                    xs = x_t[:, oc, :]
                    p.op("dve", lambda e, ps=ps, xs=xs, oc=oc: e.scalar_tensor_tensor(out=xs, in0=ps[:, 0:TG], scalar=gm1[:, oc:oc + 1], in1=xs,
                                                                                  op0=ALU.mult, op1=ALU.add), r=[pr, gm1_r, x_r], w=[x_r])
            emit_mlp(p, cx, x_t, x_r, gm2, gm2_r, shift2, mod_r, up_d, down_d, st)
            if final_g is not None:
                emit_norm(p, cx, x_t, x_r, final_g, None, vec_r, x_t, x_r, st["scr"])
                out_dma(p, ov[:, :, cols], x_t[:], x_r, "po_out")
            else:
                out_dma(p, ov[:, :, cols], x_t[:], x_r, "po_out")


def stage_scan(p, cx, cs, T, dd):
    nc = p.nc
    NP4 = T // 256
    with p.scope() as es2:
        sb = p.sb
        mr = p.res("sc_masks")
        mU2 = sb("sc_mU2", [128, 256])
        mL = sb("sc_mL", [128, 128])
        p.op("pool", lambda e: e.affine_select(out=mU2[:, 0:128], in_=cs.ones_f[:], pattern=[[1, 128]], compare_op=ALU.is_ge, fill=0.0, base=-1, channel_multiplier=-1), r=[cs.r], w=[mr])
        p.op("pool", lambda e: e.affine_select(out=mU2[:, 128:256], in_=cs.ones_f[:], pattern=[[1, 128]], compare_op=ALU.is_ge, fill=0.0, base=0, channel_multiplier=-1), r=[cs.r], w=[mr])
        p.op("pool", lambda e: e.affine_select(out=mL[:], in_=cs.ones_f[:], pattern=[[-1, 128]], compare_op=ALU.is_ge, fill=0.0, base=-1, channel_multiplier=1), r=[cs.r], w=[mr])
        p.op("pool", lambda e: e.memset(mU2[0:64, 64:128], 0.0), w=[mr])
        p.op("pool", lambda e: e.memset(mU2[0:64, 192:256], 0.0), w=[mr])
        p.op("pool", lambda e: e.memset(mL[64:128, 0:64], 0.0), w=[mr])
        NSTREAM = 3

        def stream(sid, hps):
            names = ["r", "lw", "k", "v", "a", "b"]
            inp = {n: Rot(p, "sc%d_" % sid + "in_" + n, [128, 256], F32, 2) for n in names}
            cum = Rot(p, "sc%d_" % sid + "cum", [128, 128], F32, 2)
            cpv = Rot(p, "sc%d_" % sid + "cpv", [128, 128], F32, 2)
            eP = Rot(p, "sc%d_" % sid + "eP", [128, 128], F32, 3)
            eN = Rot(p, "sc%d_" % sid + "eN", [128, 128], F32, 2)
            eV = Rot(p, "sc%d_" % sid + "eV", [128, 128], F32, 2)
            AR = Rot(p, "sc%d_" % sid + "AR", [128, 256], F32, 2)
            bT = Rot(p, "sc%d_" % sid + "bT", [128, 128], F32, 2)
            kT = Rot(p, "sc%d_" % sid + "kT", [128, 128], F32, 2)
            PadA = Rot(p, "sc%d_" % sid + "PadA", [128, 4, 128], F32, 2)
            PadB = Rot(p, "sc%d_" % sid + "PadB", [128, 4, 128], F32, 2)
            PZA = Rot(p, "sc%d_" % sid + "PZA", [128, 2, 128], F32, 2)
            PZB = Rot(p, "sc%d_" % sid + "PZB", [128, 2, 128], F32, 2)
            for rot in (PadA, PadB, PZA, PZB):
                for t, r in zip(rot.t, rot.r):
                    for j_ in range(t.shape[1]):
                        p.op("pool", lambda e, t=t, j_=j_: e.tensor_scalar_mul(out=R32(t[:, j_, :]), in0=cs.ones_f[:], scalar1=0.0), r=[cs.r], w=[r])
            ZcA = Rot(p, "sc%d_" % sid + "ZcA", [128, 128], F32, 2)
            ZcB = Rot(p, "sc%d_" % sid + "ZcB", [128, 128], F32, 2)
            XP = Rot(p, "sc%d_" % sid + "XP", [128, 256], F32, 3)
            YP = Rot(p, "sc%d_" % sid + "YP", [128, 256], F32, 3)
            Xp = Rot(p, "sc%d_" % sid + "Xp", [128, 128], F32, 4)
            Lp = Rot(p, "sc%d_" % sid + "Lp", [128, 128], F32, 4)
            RH = Rot(p, "sc%d_" % sid + "RH", [128, 128], F32, 2)
            YH = Rot(p, "sc%d_" % sid + "YH", [128, 128], F32, 2)
            IG = Rot(p, "sc%d_" % sid + "IG", [128, 128], F32, 4)
            NN = Rot(p, "sc%d_" % sid + "NN", [128, 128], F32, 4)
            Zbd = sb("sc%d_" % sid + "Zbd", [128, 128])
            Zbd_r = p.res("sc%d_Zbd" % sid)
            yo = Rot(p, "sc%d_" % sid + "yo", [128, 256], F32, 2)
            ci = [0]

            def evac(out, in_, r, w):
                ci[0] += 1
                if ci[0] % 2 == 0:
                    p.op("act", lambda e: e.copy(out=out, in_=in_), r=r, w=w)
                else:
                    p.op("dve", lambda e: e.tensor_copy(out=out, in_=in_), r=r, w=w)

            for hp in hps:
                rows = slice(hp * 128, (hp + 1) * 128)
                p.op("pool", lambda e: e.memset(Zbd[:], 0.0), w=[Zbd_r])
                for p4 in range(NP4):
                    cols4 = slice(p4 * 256, (p4 + 1) * 256)
                    tin = {}
                    for n in names:
                        t, r = inp[n].get()
                        p.dma(t[:], dd[n][rows, cols4], w=[r])
                        tin[n] = (t, r)
                    yo_t, yo_r = yo.get()
                    for u in range(2):
                        c = slice(u * 128, (u + 1) * 128)
                        lw_t, lw_r = tin["lw"]
                        cum_t, cum_r = cum.get()
                        for ch in range(2):
                            cc = slice(u * 128 + ch * 64, u * 128 + ch * 64 + 64)
                            oc = slice(ch * 64, ch * 64 + 64)
                            p.op("dve", lambda e, cc=cc, oc=oc, cum_t=cum_t, lw_t=lw_t: e.tensor_tensor_scan(
                                out=cum_t[:, oc], data0=cs.ones_f[:, oc], data1=lw_t[:, cc], initial=0.0, op0=ALU.mult, op1=ALU.add),
                                r=[lw_r, cs.r], w=[cum_r])
                        cpv_t, cpv_r = cpv.get()
                        p.op("pool", lambda e, cpv_t=cpv_t, cum_t=cum_t, lw_t=lw_t, c=c: e.tensor_tensor(out=cpv_t[:], in0=cum_t[:], in1=lw_t[:, c], op=ALU.subtract),
                             r=[cum_r, lw_r], w=[cpv_r])
                        eP_t, eP_r = eP.get()
                        eN_t, eN_r = eN.get()
                        eV_t, eV_r = eV.get()
                        p.op("act", lambda e, eP_t=eP_t, cum_t=cum_t: e.activation(out=eP_t[:], in_=cum_t[:], func=AF.Exp), r=[cum_r], w=[eP_r])
                        p.op("act", lambda e, eN_t=eN_t, cum_t=cum_t: e.activation(out=eN_t[:], in_=cum_t[:], func=AF.Exp, scale=-1.0), r=[cum_r], w=[eN_r])
                        p.op("act", lambda e, eV_t=eV_t, cpv_t=cpv_t: e.activation(out=eV_t[:], in_=cpv_t[:], func=AF.Exp), r=[cpv_r], w=[eV_r])
                        AR_t, AR_r = AR.get()
                        bT_t, bT_r = bT.get()
                        kT_t, kT_r = kT.get()
                        a_t, a_r = tin["a"]
                        r_t, r_r = tin["r"]
                        b_t, b_r = tin["b"]
                        k_t, k_r = tin["k"]
                        v_t, v_r = tin["v"]
                        p.op("dve", lambda e, AR_t=AR_t, a_t=a_t, eV_t=eV_t, c=c: e.tensor_tensor(out=R32(AR_t[:, 0:128]), in0=a_t[:, c], in1=eV_t[:], op=ALU.mult), r=[a_r, eV_r], w=[AR_r])
                        p.op("dve", lambda e, AR_t=AR_t, r_t=r_t, eP_t=eP_t, c=c: e.tensor_tensor(out=R32(AR_t[:, 128:256]), in0=r_t[:, c], in1=eP_t[:], op=ALU.mult), r=[r_r, eP_r], w=[AR_r])
                        p.op("dve", lambda e, bT_t=bT_t, b_t=b_t, eN_t=eN_t, c=c: e.tensor_tensor(out=R32(bT_t[:]), in0=b_t[:, c], in1=eN_t[:], op=ALU.mult), r=[b_r, eN_r], w=[bT_r])
                        p.op("pool", lambda e, kT_t=kT_t, k_t=k_t, eN_t=eN_t, c=c: e.tensor_tensor(out=R32(kT_t[:]), in0=k_t[:, c], in1=eN_t[:], op=ALU.mult), r=[k_r, eN_r], w=[kT_r])
                        yield
                        ps, pr = cx.bank()
                        srcs = [(AR_t[:, 0:128], AR_r), (bT_t[:], bT_r), (kT_t[:], kT_r), (v_t[:, c], v_r)]
                        for j, (src, sr) in enumerate(srcs):
                            p.op("pe", lambda e, j=j, src=src, ps=ps: e.transpose(out=ps[:, j * 128:(j + 1) * 128], in_=src, identity=cs.ident[:]), r=[sr, cs.r], w=[pr])
                        PA, PA_r = PadA.get()
                        PB, PB_r = PadB.get()
                        psv = ps[:, 0:512].rearrange("p (j c) -> p j c", j=4)
                        p.op("act", lambda e, PA=PA, psv=psv: e.copy(out=R32(PA[:, :, 0:64]), in_=psv[:, :, 0:64]), r=[pr], w=[PA_r])
                        p.op("dve", lambda e, PB=PB, psv=psv: e.tensor_copy(out=R32(PB[:, :, 64:128]), in_=psv[:, :, 64:128]), r=[pr], w=[PB_r])
                        Zcs = [ZcA.get(), ZcB.get()]
                        p.op("act", lambda e, ps=ps: e.copy(out=R32(Zcs[0][0][:, 0:64]), in_=ps[:, 0:64]), r=[pr], w=[Zcs[0][1]])
                        p.op("dve", lambda e, ps=ps: e.tensor_copy(out=R32(Zcs[1][0][:, 0:64]), in_=ps[:, 64:128]), r=[pr], w=[Zcs[1][1]])
                        yield
                        PZ = [PZA.get(), PZB.get()]
                        Pad = [(PA, PA_r), (PB, PB_r)]
                        XPs, YPs = [], []
                        for h in range(2):
                            hs = slice(h * 64, (h + 1) * 64)
                            Zc_t, Zc_r = Zcs[h]
                            ps1, pr1 = cx.bank()
                            p.op("pe", lambda e, ps1=ps1, hs=hs: e.matmul(ps1[:, 0:256], R32(bT_t[hs, :]), R32(AR_t[hs, :]), start=True, stop=True), r=[bT_r, AR_r], w=[pr1])
                            XP_t, XP_r = XP.get()
                            p.op("dve", lambda e, XP_t=XP_t, ps1=ps1: e.tensor_tensor(out=R32(XP_t[:]), in0=ps1[:, 0:256], in1=mU2[:], op=ALU.mult), r=[pr1, mr], w=[XP_r])
                            ps2, pr2 = cx.bank()
                            p.op("pe", lambda e, ps2=ps2, hs=hs: e.matmul(ps2[:, 0:256], R32(kT_t[hs, :]), R32(AR_t[hs, :]), start=True, stop=True), r=[kT_r, AR_r], w=[pr2])
                            YP_t, YP_r = YP.get()
                            p.op("dve", lambda e, YP_t=YP_t, ps2=ps2: e.tensor_tensor(out=R32(YP_t[:]), in0=ps2[:, 0:256], in1=mU2[:], op=ALU.mult), r=[pr2, mr], w=[YP_r])
                            ps3, pr3 = cx.bank()
                            p.op("pe", lambda e, ps3=ps3, hs=hs: e.matmul(ps3[:, 0:128], R32(AR_t[hs, 0:128]), R32(bT_t[hs, :]), start=True, stop=True), r=[bT_r, AR_r], w=[pr3])
                            L_t, L_r = Lp.get()
                            p.op("dve", lambda e, L_t=L_t, ps3=ps3: e.tensor_tensor(out=R32(L_t[:]), in0=ps3[:, 0:128], in1=mL[:], op=ALU.mult), r=[pr3, mr], w=[L_r])
                            XPs.append((XP_t, XP_r))
                            YPs.append((YP_t, YP_r))
                            yield
                            Pd, Pd_r = Pad[h]
                            ps4, pr4 = cx.bank()
                            p.op("pe", lambda e, ps4=ps4, YP_t=YP_t, Pd=Pd, hs=hs: e.matmul(ps4[:, 0:64], R32(YP_t[:, 0:128]), R32(Pd[:, 3, hs]), start=True, stop=True), r=[YP_r, Pd_r], w=[pr4])
                            evac(R32(Zc_t[:, 64:128]), ps4[:, 0:64], [pr4], [Zc_r])
                            X_t, X_r = XP_t[:, 0:128], XP_r
                            Lc_t, Lc_r = L_t[:], L_r
                            PZ_t, PZ_r = PZ[h]
                            for j in range(6):
                                yield
                                psa, pra = cx.bank()
                                p.op("pe", lambda e, psa=psa, X_t=X_t, Zc_t=Zc_t: e.matmul(psa[:, 0:128], R32(X_t), R32(Zc_t[:]), start=True, stop=True), r=[X_r, Zc_r], w=[pra])
                                if j < 5:
                                    psx, prx = cx.bank()
                                    p.op("pe", lambda e, psx=psx, X_t=X_t, Lc_t=Lc_t: e.matmul(psx[:, 0:128], R32(Lc_t), R32(X_t), start=True, stop=True), r=[X_r, Lc_r], w=[prx])
                                    psl, prl = cx.bank()
                                    p.op("pe", lambda e, psl=psl, X_t=X_t, Lc_t=Lc_t: e.matmul(psl[:, 0:128], R32(X_t), R32(Lc_t), start=True, stop=True), r=[X_r, Lc_r], w=[prl])
                                    p.op("dve", lambda e, psa=psa, Zc_t=Zc_t: e.tensor_tensor(out=R32(Zc_t[:]), in0=psa[:, 0:128], in1=Zc_t[:], op=ALU.add), r=[pra, Zc_r], w=[Zc_r])
                                    Xn, Xn_r = Xp.get()
                                    Ln, Ln_r = Lp.get()
                                    evac(R32(Xn[:]), psx[:, 0:128], [prx], [Xn_r])
                                    evac(R32(Ln[:]), psl[:, 0:128], [prl], [Ln_r])
                                    X_t, X_r, Lc_t, Lc_r = Xn[:], Xn_r, Ln[:], Ln_r
                                else:
                                    p.op("dve", lambda e, psa=psa, Zc_t=Zc_t, PZ_t=PZ_t, hs=hs: e.tensor_tensor(
                                        out=R32(PZ_t[:, :, hs]), in0=psa[:, 0:128].rearrange("p (j c) -> p j c", j=2),
                                        in1=Zc_t[:].rearrange("p (j c) -> p j c", j=2), op=ALU.add), r=[pra, Zc_r], w=[PZ_r])
                        yield
                        psr_, prr = cx.bank()
                        for h in range(2):
                            p.op("pe", lambda e, h=h, psr_=psr_: e.matmul(psr_[:, 0:128], R32(PZ[h][0][:, 0, :]), R32(XPs[h][0][:, 128:256]), start=(h == 0), stop=(h == 1)),
                                 r=[PZ[h][1], XPs[h][1]], w=[prr])
                        RH_t, RH_r = RH.get()
                        p.op("dve", lambda e, RH_t=RH_t, psr_=psr_, AR_t=AR_t: e.tensor_tensor(out=RH_t[:], in0=psr_[:, 0:128], in1=AR_t[:, 128:256], op=ALU.add), r=[prr, AR_r], w=[RH_r])
                        psy, pry = cx.bank()
                        for h in range(2):
                            p.op("pe", lambda e, h=h, psy=psy: e.matmul(psy[:, 0:128], R32(PZ[h][0][:, 1, :]), R32(XPs[h][0][:, 128:256]), start=(h == 0), stop=False),
                                 r=[PZ[h][1], XPs[h][1]], w=[pry])
                            p.op("pe", lambda e, h=h, psy=psy: e.matmul(psy[:, 0:128], R32(Pad[h][0][:, 3, :]), R32(YPs[h][0][:, 128:256]), start=False, stop=(h == 1)),
                                 r=[Pad[h][1], YPs[h][1]], w=[pry])
                        YH_t, YH_r = YH.get()
                        evac(YH_t[:], psy[:, 0:128], [pry], [YH_r])
                        yield
                        IGs, NNs = [], []
                        for ch in range(2):
                            tk = slice(ch * 64, ch * 64 + 64)
                            psg, prg = cx.bank()
                            for h in range(2):
                                p.op("pe", lambda e, h=h, psg=psg, tk=tk: e.matmul(psg[:, 0:128], R32(PZ[h][0][tk, 0, :]), R32(Pad[h][0][tk, 1, :]), start=(h == 0), stop=(h == 1)),
                                     r=[PZ[h][1], Pad[h][1]], w=[prg])
                            IG_t, IG_r = IG.get()
                            p.op("dve", lambda e, IG_t=IG_t, psg=psg: e.tensor_tensor(out=IG_t[:], in0=psg[:, 0:128], in1=cs.ident[:], op=ALU.add), r=[prg, cs.r], w=[IG_r])
                            psn, prn = cx.bank()
                            for h in range(2):
                                p.op("pe", lambda e, h=h, psn=psn, tk=tk: e.matmul(psn[:, 0:128], R32(Pad[h][0][tk, 1, :]), R32(PZ[h][0][tk, 1, :]), start=(h == 0), stop=False),
                                     r=[PZ[h][1], Pad[h][1]], w=[prn])
                                p.op("pe", lambda e, h=h, psn=psn, tk=tk: e.matmul(psn[:, 0:128], R32(Pad[h][0][tk, 2, :]), R32(Pad[h][0][tk, 3, :]), start=False, stop=(h == 1)),
                                     r=[Pad[h][1]], w=[prn])
                            NN_t, NN_r = NN.get()
                            wc = eP_t[:, ch * 64 + 63:ch * 64 + 64]
                            p.op("dve", lambda e, NN_t=NN_t, psn=psn, wc=wc: e.tensor_scalar_mul(out=NN_t[:], in0=psn[:, 0:128], scalar1=wc), r=[prn, eP_r], w=[NN_r])
                            IGs.append((IG_t, IG_r))
                            NNs.append((NN_t, NN_r, wc))
                        for ch in range(2):
                            yield
                            tcol = slice(ch * 64, ch * 64 + 64)
                            ocol = slice(u * 128 + ch * 64, u * 128 + ch * 64 + 64)
                            psq, prq = cx.bank()
                            p.op("pe", lambda e, psq=psq, tcol=tcol, RH_t=RH_t: e.matmul(psq[:, 0:64], Zbd[:], RH_t[:, tcol], start=True, stop=True), r=[Zbd_r, RH_r], w=[prq])
                            p.op("dve", lambda e, psq=psq, tcol=tcol, ocol=ocol, YH_t=YH_t, yo_t=yo_t: e.tensor_tensor(out=yo_t[:, ocol], in0=psq[:, 0:64], in1=YH_t[:, tcol], op=ALU.add),
                                 r=[prq, YH_r], w=[yo_r])
                            psz, prz = cx.bank()
                            IG_t, IG_r = IGs[ch]
                            NN_t, NN_r, wc = NNs[ch]
                            p.op("pe", lambda e, psz=psz, IG_t=IG_t: e.matmul(psz[:, 0:128], IG_t[:], Zbd[:], start=True, stop=True), r=[IG_r, Zbd_r], w=[prz])
                            p.op("dve", lambda e, psz=psz, NN_t=NN_t, wc=wc: e.scalar_tensor_tensor(out=Zbd[:], in0=psz[:, 0:128], scalar=wc, in1=NN_t[:], op0=ALU.mult, op1=ALU.add),
                                 r=[prz, NN_r, eP_r], w=[Zbd_r])
                    out_dma(p, dd["y"][rows, cols4], yo_t[:], yo_r, "sc_y")


        gens = [stream(s, list(range(s, H // 2, NSTREAM))) for s in range(NSTREAM)]
        while gens:
            for g in list(gens):
                try:
                    next(g)
                except StopIteration:
                    gens.remove(g)


def stage_kvq(p, cx, cs, T, kind, layer, x_d, vec, vec_r, g_col, mod_sb, mod_r, W_d, gain_sb, gain_r, dd, fb_sb=None, fb_r=None, Wf_d=None):
    nc = p.nc
    NTL = T // TG
    with p.scope() as es2:
        sb = p.sb
        gm = sb("kq_gm", [128, KC])
        gm_r = p.res("kq_gm")
        if kind == "kv":
            sh_c, sc_c = 192, 200
        else:
            sh_c, sc_c = layer * 48, layer * 48 + 8
        p.op("dve", lambda e: e.scalar_tensor_tensor(out=gm[:], in0=mod_sb[:, sc_c:sc_c + 8], scalar=1.0, in1=vec[:, g_col:g_col + KC],
                                                    op0=ALU.add, op1=ALU.mult), r=[vec_r, mod_r], w=[gm_r])
        shift = mod_sb[:, sh_c:sh_c + 8]
        xt = Rot(p, "kq_x", [128, KC, TG], F32, 2)
        h = sb("kq_h", [128, KC, TG], BF16)
        h_r = p.res("kq_h")
        scr = norm_scratch(p, "kqn")
        pw = ProjW(p, "kq_w")
        f1 = Rot(p, "kq_f1", [128, TG], F32, 3)
        ob = Rot(p, "kq_ob", [128, TG], F32, 4)
        xv = x_d.rearrange("(c p) t -> p c t", p=128)
        if kind == "kv":
            wf_s = sb("kq_wfs", [128, KC, 16])
            wf = sb("kq_wf", [128, KC, 16], BF16)
            wf_r = p.res("kq_wf")
            p.dma(wf_s[:], Wf_d.rearrange("(kc p) m -> p kc m", p=128)[:, :, 2 * D:2 * D + 16], w=[wf_r])
            p.op("pool", lambda e: e.tensor_copy(out=wf[:], in_=wf_s[:]), r=[wf_r], w=[wf_r])
            lf = Rot(p, "kq_lf", [16, TG], F32, 2)
        gscale = 1.0 if kind == "kv" else 0.125
        for ti in range(NTL):
            cols = slice(ti * TG, (ti + 1) * TG)
            x_t, x_r = xt.get()
            p.dma(x_t[:], xv[:, :, cols], w=[x_r])
            emit_norm(p, cx, x_t, x_r, gm, shift, gm_r, h, h_r, scr)
            for o2 in range(2 * KC // 2):
                wb, wr = pw.load(W_d, o2 * 256)
                for oi in range(2):
                    oc = o2 * 2 + oi
                    ps, pr = proj_mm(p, cx, wb, wr, oi * 128, h, h_r)
                    if oc < KC:
                        sq, sq_r = f1.get()
                        p.op("act", lambda e, sq=sq, ps=ps: e.activation(out=sq[:], in_=ps[:, 0:TG], func=AF.Square), r=[pr], w=[sq_r])
                        ps2, pr2 = head_sum(p, cx, cs, cs.bo64, sq, sq_r)
                        p.op("act", lambda e, sq=sq, ps2=ps2: e.activation(out=sq[:], in_=ps2[:, 0:TG], func=AF.Sqrt, bias=cx.eps_t[:, 0:1], scale=1.0),
                             r=[pr2, cx.r_const], w=[sq_r])
                        p.op("dve", lambda e, sq=sq: e.reciprocal(out=sq[:], in_=sq[:]), r=[sq_r], w=[sq_r])
                        o, o_r = ob.get()
                        p.op("dve", lambda e, o=o, ps=ps, sq=sq: e.scalar_tensor_tensor(out=o[:], in0=ps[:, 0:TG], scalar=gain_sb, in1=sq[:], op0=ALU.mult, op1=ALU.mult),
                             r=[pr, sq_r, gain_r], w=[o_r])
                        if gscale != 1.0:
                            p.op("pool", lambda e, o=o: e.tensor_scalar_mul(out=o[:], in0=o[:], scalar1=gscale), r=[o_r], w=[o_r])
                        dst = dd["ksh"] if kind == "kv" else dd["q"]
                        out_dma(p, dst[oc * 128:(oc + 1) * 128, cols], o[:], o_r, "kq_o")
                    else:
                        o, o_r = ob.get()
                        if kind == "kv":
                            p.op("act", lambda e, o=o, ps=ps: e.copy(out=o[:], in_=ps[:, 0:TG]), r=[pr], w=[o_r])
                            out_dma(p, dd["vsh"][(oc - KC) * 128:(oc - KC + 1) * 128, cols], o[:], o_r, "kq_o")
                        else:
                            p.op("act", lambda e, o=o, ps=ps: e.activation(out=o[:], in_=ps[:, 0:TG], func=AF.Sigmoid), r=[pr], w=[o_r])
                            out_dma(p, dd["sig"][(oc - KC) * 128:(oc - KC + 1) * 128, cols], o[:], o_r, "kq_o")
            if kind == "kv":
                ps, pr = cx.bank()
                for kc in range(KC):
                    p.op("pe", lambda e, kc=kc, ps=ps: e.matmul(ps[0:16, 0:TG], wf[:, kc, :], h[:, kc, :], start=(kc == 0), stop=(kc == KC - 1)), r=[wf_r, h_r], w=[pr])
                l, l_r = lf.get()
                p.op("act", lambda e, l=l, ps=ps: e.activation(out=l[:], in_=ps[0:16, 0:TG], func=AF.Exp, bias=fb_sb, scale=-1.0), r=[pr, fb_r], w=[l_r])
                p.op("act", lambda e, l=l: e.activation(out=l[:], in_=l[:], func=AF.Ln, bias=cx.eps_t[0:16, 2:3], scale=1.0), r=[l_r, cx.r_const], w=[l_r])
                p.op("pool", lambda e, l=l: e.tensor_scalar_mul(out=l[:], in0=l[:], scalar1=-1.0), r=[l_r], w=[l_r])
                out_dma(p, dd["logf"][:, cols], l[:], l_r, "kq_lf")


def stage_fprep(p, cx, T, dd):
    nc = p.nc
    FB = min(2048, T)
    with p.scope() as es2:
        sb = p.sb
        ones = sb("fp_ones", [16, FB])
        cr = p.res("fp_c")
        p.op("pool", lambda e: e.memset(ones[:], 1.0), w=[cr])
        lf = Rot(p, "fp_lf", [16, FB], F32, 2)
        F = Rot(p, "fp_F", [16, FB], F32, 2)
        r1 = Rot(p, "fp_r1", [16, FB], F32, 2)
        pp = Rot(p, "fp_pp", [16, 3, FB], BF16, 2)
        pn = Rot(p, "fp_pn", [16, 3, FB], BF16, 2)
        prev = None
        for bi in range(T // FB):
            cols = slice(bi * FB, (bi + 1) * FB)
            l, l_r = lf.get()
            p.dma(l[:], dd["logf"][:, cols], w=[l_r])
            f, f_r = F.get()
            init = 0.0 if prev is None else prev[0][:, FB - 1:FB]
            rr = [l_r, cr] + ([] if prev is None else [prev[1]])
            p.op("dve", lambda e, f=f, l=l, init=init: e.tensor_tensor_scan(out=f[:], data0=ones[:], data1=l[:], initial=init, op0=ALU.mult, op1=ALU.add), r=rr, w=[f_r])
            prev = (f, f_r)
            q, q_r = pp.get()
            n, n_r = pn.get()
            r_, r_r = r1.get()
            p.op("act", lambda e, q=q, f=f: e.copy(out=q[:, 0, :], in_=f[:]), r=[f_r], w=[q_r])
            p.op("dve", lambda e, r_=r_, f=f, q=q: e.tensor_tensor(out=r_[:], in0=f[:], in1=q[:, 0, :], op=ALU.subtract), r=[f_r, q_r], w=[r_r])
            p.op("act", lambda e, q=q, r_=r_: e.copy(out=q[:, 1, :], in_=r_[:]), r=[r_r], w=[q_r])
            p.op("dve", lambda e, r_=r_, q=q: e.tensor_tensor(out=r_[:], in0=r_[:], in1=q[:, 1, :], op=ALU.subtract), r=[r_r, q_r], w=[r_r])
            p.op("act", lambda e, q=q, r_=r_: e.copy(out=q[:, 2, :], in_=r_[:]), r=[r_r], w=[q_r])
            p.op("pool", lambda e, n=n, q=q: e.tensor_scalar_mul(out=n[:], in0=q[:], scalar1=-1.0), r=[q_r], w=[n_r])
            out_dma(p, dd["fpos"][:, :, cols], q[:], q_r, "fp_p")
            out_dma(p, dd["fneg"][:, :, cols], n[:], n_r, "fp_n")


def stage_attn(p, cx, cs, T, dd):
    nc = p.nc
    NQT = T // 512
    NKB = T // 128
    LB = min(2048, T)
    with p.scope() as es2:
        sb = p.sb
        cx.rot = [0, 1, 2, 3, 4, 5]
        obank = [(cx.ps[6], cx.psr[6]), (cx.ps[7], cx.psr[7])]
        mr = p.res("at_masks")
        onesb = sb("at_onesb", [128, 512], BF16)
        p.op("pool", lambda e: e.memset(onesb[:], 1.0), w=[mr])
        masks = []
        for o in range(4):
            m = sb("at_mask%d" % o, [128, 512], BF16)
            p.op("pool", lambda e, m=m, o=o: e.affine_select(out=m[:], in_=onesb[:], pattern=[[1, 512]], compare_op=ALU.is_ge, fill=0.0,
                                                          base=-128 * o, channel_multiplier=-1), r=[mr], w=[mr])
            masks.append(m)
        KA = Rot(p, "at_KA", [70, T], BF16, 2)
        QA = Rot(p, "at_QA", [70, T], BF16, 2)
        VP = Rot(p, "at_VP", [128, NKB, 65], BF16, 2)
        for rot in (KA, QA):
            for t, r in zip(rot.t, rot.r):
                p.op("pool", lambda e, t=t: e.memset(t[64:70, :], 1.0), w=[r])
        for t, r in zip(VP.t, VP.r):
            p.op("pool", lambda e, t=t: e.memset(t[:, :, 64:65], 1.0), w=[r])
        stg = Rot(p, "at_stg", [64, LB], F32, 3)
        pT = Rot(p, "at_pT", [128, 512], BF16, 7)
        clp = Rot(p, "at_clp", [128, 512], F32, 2)
        osb = Rot(p, "at_osb", [65, 512], F32, 2)
        rc = Rot(p, "at_rc", [65, 512], F32, 2)
        oo = Rot(p, "at_oo", [64, 512], F32, 2)
        ci = [0]
        DEPTH = 4
        pend = []
        nqt_done = [0]

        def flush_one():
            ob_, ob_r, VP_t, VP_r, kb, nkb, pt, pt_r, fin = pend.pop(0)
            p.op("pe", lambda e, ob_=ob_, kb=kb, pt=pt, nkb=nkb, VP_t=VP_t: e.matmul(ob_[0:65, 0:512], VP_t[:, kb, :], pt[:], start=(kb == 0), stop=(kb == nkb - 1)),
                 r=[VP_r, pt_r], w=[ob_r])
            if fin is not None:
                rows, qc = fin
                os_, os_r = osb.get()
                p.op("act", lambda e, os_=os_, ob_=ob_: e.copy(out=os_[:], in_=ob_[0:65, 0:512]), r=[ob_r], w=[os_r])
                rc_, rc_r = rc.get()
                p.op("dve", lambda e, rc_=rc_, os_=os_: e.reciprocal(out=rc_[64:65, :], in_=os_[64:65, :]), r=[os_r], w=[rc_r])
                ps, pr = cx.bank()
                p.op("pe", lambda e, ps=ps, rc_=rc_: e.matmul(ps[0:64, 0:512], cs.ones_f[64:65, 0:64], rc_[64:65, :], start=True, stop=True), r=[rc_r, cs.r], w=[pr])
                o_, o_r = oo.get()
                p.op("dve", lambda e, o_=o_, os_=os_, ps=ps: e.tensor_tensor(out=o_[:], in0=os_[0:64, :], in1=ps[0:64, 0:512], op=ALU.mult), r=[os_r, pr], w=[o_r])
                out_dma(p, dd["o"][rows, qc], o_[:], o_r, "at_o")

        for h in range(H):
            rows = slice(h * 64, (h + 1) * 64)
            KA_t, KA_r = KA.get()
            QA_t, QA_r = QA.get()
            VP_t, VP_r = VP.get()
            for bi in range(T // LB):
                cols = slice(bi * LB, (bi + 1) * LB)
                for src, dst, dst_r in ((dd["ksh"], KA_t, KA_r), (dd["q"], QA_t, QA_r)):
                    s, s_r = stg.get()
                    p.dma(s[:], src[rows, cols], w=[s_r])
                    ci[0] += 1
                    if ci[0] % 2 == 0:
                        p.op("act", lambda e, s=s, dst=dst, cols=cols: e.copy(out=dst[0:64, cols], in_=s[:]), r=[s_r], w=[dst_r])
                    else:
                        p.op("pool", lambda e, s=s, dst=dst, cols=cols: e.tensor_copy(out=dst[0:64, cols], in_=s[:]), r=[s_r], w=[dst_r])
                s, s_r = stg.get()
                p.dma(s[:], dd["vsh"][rows, cols], w=[s_r])
                GS = min(8, LB // 128)
                for g8 in range(LB // 128 // GS):
                    ps, pr = cx.bank()
                    for j in range(GS):
                        kb = g8 * GS + j
                        p.op("pe", lambda e, ps=ps, j=j, kb=kb, s=s: e.transpose(out=ps[:, j * 64:(j + 1) * 64], in_=s[:, kb * 128:(kb + 1) * 128], identity=cs.ident[0:64, 0:64]),
                             r=[s_r, cs.r], w=[pr])
                    kb0 = bi * (LB // 128) + g8 * GS
                    p.op("dve", lambda e, ps=ps, kb0=kb0, VP_t=VP_t, GS=GS: e.tensor_copy(out=VP_t[:, kb0:kb0 + GS, 0:64], in_=ps[:, 0:GS * 64].rearrange("p (j c) -> p j c", j=GS)),
                         r=[pr], w=[VP_r])
            p.dma(QA_t[64:67, :], dd["fpos"][h], w=[QA_r])
            p.dma(KA_t[67:70, :], dd["fneg"][h], w=[KA_r])
            for qt in range(NQT):
                qc = slice(qt * 512, (qt + 1) * 512)
                ob_, ob_r = obank[nqt_done[0] % 2]
                nqt_done[0] += 1
                nkb = 4 * (qt + 1)
                for kb in range(nkb):
                    ps, pr = cx.bank()
                    p.op("pe", lambda e, ps=ps, kb=kb, qc=qc, KA_t=KA_t, QA_t=QA_t: e.matmul(ps[:, 0:512], KA_t[:, kb * 128:(kb + 1) * 128], QA_t[:, qc], start=True, stop=True), r=[KA_r, QA_r], w=[pr])
                    pt, pt_r = pT.get()
                    if kb >= 4 * qt:
                        cl, cl_r = clp.get()
                        p.op("dve", lambda e, ps=ps, cl=cl: e.tensor_scalar_min(out=cl[:], in0=ps[:, 0:512], scalar1=20.0), r=[pr], w=[cl_r])
                        p.op("act", lambda e, cl=cl, pt=pt: e.activation(out=pt[:], in_=cl[:], func=AF.Exp), r=[cl_r], w=[pt_r])
                        mk_ = masks[kb - 4 * qt]
                        p.op("pool", lambda e, pt=pt, mk_=mk_: e.tensor_tensor(out=pt[:], in0=pt[:], in1=mk_[:], op=ALU.mult), r=[pt_r, mr], w=[pt_r])
                    else:
                        p.op("act", lambda e, ps=ps, pt=pt: e.activation(out=pt[:], in_=ps[:, 0:512], func=AF.Exp), r=[pr], w=[pt_r])
                    last = (kb == nkb - 1)
                    pend.append((ob_, ob_r, VP_t, VP_r, kb, nkb, pt, pt_r, (rows, qc) if last else None))
                    while len(pend) > DEPTH:
                        flush_one()
        while pend:
            flush_one()
        cx.rot = list(range(8))


def stage_wcast(p, cx, jobs):
    with p.scope() as es2:
        engs = ["act", "dve", "pool"]
        ci = 0
        stgs = {}
        for src, dst, bc in jobs:
            kci = src.shape[0] // 128
            key = (kci, bc)
            if key not in stgs:
                stgs[key] = (Rot(p, "wc_s%d_%d" % key, [128, kci, bc], F32, 2), Rot(p, "wc_b%d_%d" % key, [128, kci, bc], BF16, 2))
            srot, brot = stgs[key]
            v = src.rearrange("(kc p) m -> p kc m", p=128)
            for j in range(src.shape[1] // bc):
                s_, s_r = srot.get()
                b_, b_r = brot.get()
                p.dma(s_[:], v[:, :, j * bc:(j + 1) * bc], w=[s_r])
                copy_op(p, engs[ci % 3], b_[:], s_[:], r=[s_r], w=[b_r])
                ci += 1
                p.dma(dst[j], b_[:], r=[b_r], w=[p.res("wc_o")], key="o_" + b_r.name)


NV = 2 * len(RW_VECS) * KC + 2 * 2 * KC + 2 * KC + 8
W_SHAPES = {
    "mod_w": [4, D, 6 * D], "mlp_up": [4, D, DFF], "mlp_down": [4, DFF, D],
    "rw_wr": [2, D, D], "rw_wk": [2, D, D], "rw_wv": [2, D, D], "rw_wo": [2, D, D],
    "rw_w1": [2, D, 64], "rw_w2": [2, 64, D], "rw_a1": [2, D, 64], "rw_a2": [2, 64, D],
    "rw_g1": [2, D, 160], "rw_g2": [2, 160, D], "rw_v1": [1, D, 32], "rw_v2": [1, 32, D],
    "kv_mod_w": [D, 2 * D], "kv_w": [D, 2 * D + 16], "fx_wqg": [2, D, 2 * D], "fx_wo": [2, D, D],
}


def build_program(T, stages=None, dump=()):
    nc = bass.Bass("TRN2", target_bir_lowering=False)
    ein = lambda n, s, d=F32: nc.dram_tensor(n, list(s), d, kind="ExternalInput").ap()
    xT = ein("xT", [D, T])
    cT = ein("cT", [128, KC])
    modb = ein("modb", [128, NMODC])
    vecs_d = ein("vecs", [128, NV])
    nfb_d = ein("nfb", [16, 1])
    Wd = {n: ein(n, s) for n, s in W_SHAPES.items()}
    outT = nc.dram_tensor("outT", [D, T], F32, kind="ExternalOutput").ap()
    _cnt = [0]

    def idr(n, s, d=F32):
        _cnt[0] += 1
        return nc.dram_tensor("scr%02d_%s" % (_cnt[0], n), list(s), d).ap()
    dd = {n: idr(n, [D, T]) for n in ["xa", "xb", "r", "lw", "k", "v", "a", "b", "g", "bonus", "vfirst", "y", "ksh", "vsh", "q", "sig", "o"]}
    dd["logf"] = idr("logf", [16, T])
    dd["fpos"] = idr("fpos", [16, 3, T], BF16)
    dd["fneg"] = idr("fneg", [16, 3, T], BF16)
    dump_out = {n: nc.dram_tensor("dump_" + n, list(dd[n].shape), dd[n].dtype, kind="ExternalOutput").ap() for n in dump}
    with ExitStack() as es:
        p = Prog(nc, es)
        cx = Ctx(p)
        cs = Consts(p, cx)
        vec = p.sb("vecs_sb", [128, NV])
        vec_r = p.res("vecs")
        p.dma(vec[:], vecs_d, w=[vec_r])
        nfb = p.sb("nfb_sb", [16, 1])
        nfb_r = p.res("nfb")
        p.dma(nfb[:], nfb_d, w=[nfb_r])
        p.op("pool", lambda e: e.tensor_scalar_mul(out=nfb[:], in0=nfb[:], scalar1=-1.0), r=[nfb_r], w=[nfb_r])
        mod_sb = p.sb("mod_sb", [128, NMODC])
        mod_r = p.res("mod")
        stage_mod(p, cx, cT, Wd["mod_w"], Wd["kv_mod_w"], modb, mod_sb, mod_r)
        jobs = []
        Wb = {}

        def mkb(name, src, bc):
            kin, mm_ = src.shape
            t = nc.dram_tensor("wbf_" + name, [mm_ // bc, 128, kin // 128, bc], BF16).ap()
            jobs.append((src, t, bc))
            return t
        for i in range(4):
            Wb["up%d" % i] = mkb("up%d" % i, Wd["mlp_up"][i], 256)
            Wb["dn%d" % i] = mkb("dn%d" % i, Wd["mlp_down"][i], 128)
        for i in range(2):
            for n_ in ("wr", "wk", "wv", "wo"):
                Wb["%s%d" % (n_, i)] = mkb("%s%d" % (n_, i), Wd["rw_" + n_][i], 256)
            Wb["qg%d" % i] = mkb("qg%d" % i, Wd["fx_wqg"][i], 256)
            Wb["fo%d" % i] = mkb("fo%d" % i, Wd["fx_wo"][i], 256)
        Wb["kv"] = mkb("kv", Wd["kv_w"][:, 0:2 * D], 256)
        p.barrier()
        stage_wcast(p, cx, jobs)
        nrw = len(RW_VECS) * KC
        vcs = []
        for i in range(2):
            vcs.append({n: i * nrw + j * KC for j, n in enumerate(RW_VECS)})
        for i in range(2):
            vcs.append({"norm_mix_g": 2 * nrw + i * 2 * KC, "norm_mlp_g": 2 * nrw + i * 2 * KC + KC})
        c_kvg = 2 * nrw + 4 * KC
        c_fin = c_kvg + KC
        c_gain = c_fin + KC
        xcur, xnext = xT, dd["xa"]
        nstage = 0

        def want(name):
            return stages is None or name in stages

        for i in range(2):
            W = {"wr": Wb["wr%d" % i], "wk": Wb["wk%d" % i], "wv": Wb["wv%d" % i], "w1": Wd["rw_w1"][i], "w2": Wd["rw_w2"][i],
                 "a1": Wd["rw_a1"][i], "a2": Wd["rw_a2"][i], "g1": Wd["rw_g1"][i], "g2": Wd["rw_g2"][i]}
            if i > 0:
                W["v1"] = Wd["rw_v1"][0]
                W["v2"] = Wd["rw_v2"][0]
            if want("pre%d" % i):
                p.barrier()
                stage_pre_rwkv(p, cx, cs, T, i, xcur, vec, vec_r, vcs[i], mod_sb, mod_r, W, dd)
            if want("scan%d" % i):
                p.barrier()
                stage_scan(p, cx, cs, T, dd)
            if want("post%d" % i):
                p.barrier()
                stage_post(p, cx, cs, T, i, "rwkv", xcur, xnext, vec, vec_r, vcs[i], mod_sb, mod_r, Wb["wo%d" % i], Wb["up%d" % i], Wb["dn%d" % i], dd)
                xcur, xnext = xnext, (dd["xb"] if xnext is dd["xa"] else dd["xa"])
        if want("kv"):
            p.barrier()
            stage_kvq(p, cx, cs, T, "kv", 0, xcur, vec, vec_r, c_kvg, mod_sb, mod_r, Wb["kv"], vec[:, c_gain:c_gain + 1], vec_r, dd, fb_sb=nfb[:, 0:1], fb_r=nfb_r, Wf_d=Wd["kv_w"])
            p.barrier()
            stage_fprep(p, cx, T, dd)
        for j in range(2):
            i = 2 + j
            if want("preq%d" % i):
                p.barrier()
                stage_kvq(p, cx, cs, T, "q", i, xcur, vec, vec_r, vcs[i]["norm_mix_g"], mod_sb, mod_r, Wb["qg%d" % j], vec[:, c_gain + 1 + j:c_gain + 2 + j], vec_r, dd)
            if want("attn%d" % i):
                p.barrier()
                stage_attn(p, cx, cs, T, dd)
            if want("post%d" % i):
                p.barrier()
                last = (i == 3)
                stage_post(p, cx, cs, T, i, "fox", xcur, outT if last else xnext, vec, vec_r, vcs[i], mod_sb, mod_r, Wb["fo%d" % j], Wb["up%d" % i], Wb["dn%d" % i], dd,
                           final_g=vec[:, c_fin:c_fin + KC] if last else None)
                xcur, xnext = xnext, (dd["xb"] if xnext is dd["xa"] else dd["xa"])
        if dump:
            p.barrier()
            for n in dump:
                p.dma(dump_out[n], dd[n], w=[p.res("dump_" + n)])
        p.emit()
        stats = p.stats
    return nc, stats


def fm(v):
    return np.ascontiguousarray(np.asarray(v, np.float32).reshape(-1, 128).T)


def host_inputs(inp, b, T):
    vec_parts = []
    for i in range(2):
        tab = {"norm_mix_g": inp["norm_mix_g"][i], "norm_mlp_g": inp["norm_mlp_g"][i], "w0": inp["rw_w0"][i], "a0": inp["rw_a0"][i],
               "v0": inp["rw_v0"][0], "kk": inp["rw_kk"][i], "ka": inp["rw_ka"][i], "rk": inp["rw_rk"][i].reshape(-1),
               "lnw": inp["rw_lnw"][i], "lnb": inp["rw_lnb"][i]}
        for j in range(6):
            tab["mu%d" % j] = inp["rw_mu"][i, j]
        for n in RW_VECS:
            vec_parts.append(fm(tab[n]))
    for i in (2, 3):
        vec_parts.append(fm(inp["norm_mix_g"][i]))
        vec_parts.append(fm(inp["norm_mlp_g"][i]))
    vec_parts.append(fm(inp["kv_norm_g"]))
    vec_parts.append(fm(inp["final_g"]))
    gains = np.zeros((128, 8), np.float32)
    gains[:, 0] = np.tile(np.asarray(inp["kv_kg"], np.float32), 2)
    gains[:, 1] = np.tile(np.asarray(inp["fx_qg"][0], np.float32), 2)
    gains[:, 2] = np.tile(np.asarray(inp["fx_qg"][1], np.float32), 2)
    vec_parts.append(gains)
    vecs = np.ascontiguousarray(np.concatenate(vec_parts, axis=1))
    assert vecs.shape == (128, NV), vecs.shape
    modb = np.concatenate([fm(inp["mod_b"][i]) for i in range(4)] + [fm(inp["kv_mod_b"])], axis=1)
    m = {
        "xT": np.ascontiguousarray(np.asarray(inp["x"][b, :T], np.float32).T),
        "cT": fm(inp["c"][b]),
        "modb": np.ascontiguousarray(modb),
        "vecs": vecs,
        "nfb": np.ascontiguousarray(np.asarray(inp["kv_fb"], np.float32).reshape(16, 1)),
    }
    for n in W_SHAPES:
        m[n] = np.ascontiguousarray(np.asarray(inp[n], np.float32))
    return m


def kernel(**inputs):
    T = 8192
    nc, _ = build_program(T)
    inp = {k: np.asarray(v) for k, v in inputs.items()}
    in_maps = [host_inputs(inp, b, T) for b in range(2)]
    res = run_bass_kernel_spmd(nc, in_maps, core_ids=[0, 1])
    out = np.stack([np.asarray(res.results[b]["outT"]).T for b in range(2)])
    return np.ascontiguousarray(out.astype(np.float32))
```
